# Optimizing a Trainium2 kernel written in Bass

```python
import jax, jax.numpy as jnp
from jax import lax
import numpy as np

D_MODEL = 1024
BATCH = 1
SEQ = 16384
DEPTH = 1

GRID_W = 64
CTX_LEN = 256
N_HEADS = 8
N_KV_HEADS = 2
HEAD_DIM = 128
ROPE_THETA = 10000.0
Q_BLOCK = 128
CONV_DIM = 512
CONV_WIDTH = 31
FFN_DIM = 2816
FFN_CONV_WIDTH = 3
EPS = 1e-6
Q_DIM = N_HEADS * HEAD_DIM
KV_DIM = N_KV_HEADS * HEAD_DIM
IN_DIM = Q_DIM + 2 * KV_DIM + 2 * CONV_DIM + 2 * D_MODEL
SPLITS = (Q_DIM, Q_DIM + KV_DIM, Q_DIM + 2 * KV_DIM, Q_DIM + 2 * KV_DIM + 2 * CONV_DIM)

kernel_name = 'hybrid_gqa_conformer_convglu_dit_block'


def rms_norm(x, g):
    xf = x.astype(jnp.float32)
    y = xf * lax.rsqrt(jnp.mean(xf * xf, axis=-1, keepdims=True) + EPS)
    return (y * g.astype(jnp.float32)).astype(x.dtype)


def layer_norm(x, g, b):
    xf = x.astype(jnp.float32)
    mu = jnp.mean(xf, axis=-1, keepdims=True)
    var = jnp.mean(jnp.square(xf - mu), axis=-1, keepdims=True)
    y = (xf - mu) * lax.rsqrt(var + EPS)
    return (y * g.astype(jnp.float32) + b.astype(jnp.float32)).astype(x.dtype)


def depthwise_conv(x, w, b):
    C = x.shape[-1]
    y = lax.conv_general_dilated(x, w[:, None, :], window_strides=(1,), padding='SAME',
                                 dimension_numbers=('NWC', 'WIO', 'NWC'), feature_group_count=C)
    return y + b


def modulation(cvec, w_mod, b_mod):
    m = jax.nn.silu(cvec) @ w_mod + b_mod
    return jnp.split(m[:, None, :], 6, axis=-1)


def modulate(h, shift, scale):
    return h * (1.0 + scale) + shift


def axial_rope(rows):
    half = HEAD_DIM // 2
    inv_freq = ROPE_THETA ** (-jnp.arange(0, half, 2, dtype=jnp.float32) / half)
    r = jnp.repeat(jnp.arange(rows, dtype=jnp.float32), GRID_W)
    col = jnp.tile(jnp.arange(GRID_W, dtype=jnp.float32), rows)
    ang = jnp.concatenate([r[:, None] * inv_freq, col[:, None] * inv_freq], axis=-1)
    return jnp.cos(ang), jnp.sin(ang)


def apply_rope(x, cos, sin):
    xf = x.astype(jnp.float32).reshape(x.shape[:-1] + (HEAD_DIM // 2, 2))
    x0, x1 = xf[..., 0], xf[..., 1]
    c = cos[None, :, None, :]
    s = sin[None, :, None, :]
    out = jnp.stack([x0 * c - x1 * s, x0 * s + x1 * c], axis=-1)
    return out.reshape(x.shape).astype(x.dtype)


def head_rms(t, n_heads, g):
    B, L, _ = t.shape
    return rms_norm(t.reshape(B, L, n_heads, HEAD_DIM), g)


def project(h, w_in, q_g, k_g):
    p = h @ w_in
    q, k, v, u, g = jnp.split(p, SPLITS, axis=-1)
    B, L, _ = h.shape
    q = head_rms(q, N_HEADS, q_g)
    k = head_rms(k, N_KV_HEADS, k_g)
    v = v.reshape(B, L, N_KV_HEADS, HEAD_DIM)
    return q, k, v, u, g


def context_kv(hc, w_in, k_g):
    p = hc @ w_in[:, Q_DIM:Q_DIM + 2 * KV_DIM]
    k, v = jnp.split(p, 2, axis=-1)
    B, L, _ = hc.shape
    return head_rms(k, N_KV_HEADS, k_g), v.reshape(B, L, N_KV_HEADS, HEAD_DIM)


def block_attention(q, k, v):
    B, Lq, H, d = q.shape
    G = H // N_KV_HEADS
    nblk = Lq // Q_BLOCK
    qb = q.reshape(B, nblk, Q_BLOCK, N_KV_HEADS, G, d).transpose(1, 0, 2, 3, 4, 5)
    scale = HEAD_DIM ** -0.5

    def one_block(qblk):
        s = jnp.einsum('bqkgd,bskd->bkgqs', qblk, k, preferred_element_type=jnp.float32) * scale
        p = jax.nn.softmax(s, axis=-1)
        return jnp.einsum('bkgqs,bskd->bqkgd', p.astype(v.dtype), v)

    o = lax.map(one_block, qb)
    return o.transpose(1, 0, 2, 3, 4, 5).reshape(B, Lq, H * d)


def conformer_conv(u, dw_w, dw_b, ln_g, ln_b, w_conv_out):
    a, b = jnp.split(u, 2, axis=-1)
    h = a * jax.nn.sigmoid(b)
    h = depthwise_conv(h, dw_w, dw_b)
    h = jax.nn.silu(layer_norm(h, ln_g, ln_b))
    return h @ w_conv_out


def merge_branches(attn, u, g, w_attn_out, dw_w, dw_b, ln_g, ln_b, w_conv_out, w_out):
    y_a = attn @ w_attn_out
    y_b = conformer_conv(u, dw_w, dw_b, ln_g, ln_b, w_conv_out)
    g_a, g_b = jnp.split(g, 2, axis=-1)
    return (jax.nn.sigmoid(g_a) * y_a + jax.nn.sigmoid(g_b) * y_b) @ w_out


def conv_glu_ffn(h, w_up, dw_w, dw_b, w_down):
    up = h @ w_up
    a, b = jnp.split(up, 2, axis=-1)
    a = depthwise_conv(a, dw_w, dw_b)
    return (jax.nn.gelu(a, approximate=True) * b) @ w_down


def setup_inputs(seed: int = 0) -> dict:
    key = jax.random.key(seed)
    ks = jax.random.split(key, 24)

    def nrm(k, shape, scale):
        return jax.random.normal(k, shape, jnp.float32) * scale

    return {
        'x': nrm(ks[0], (BATCH, SEQ, D_MODEL), 1.0),
        'c': nrm(ks[1], (BATCH, D_MODEL), 1.0),
        'ctx': nrm(ks[2], (BATCH, CTX_LEN, D_MODEL), 1.0),
        'c_ctx': nrm(ks[3], (D_MODEL,), 1.0),
        'w_mod': nrm(ks[4], (DEPTH, D_MODEL, 6 * D_MODEL), 0.5 * D_MODEL ** -0.5),
        'b_mod': nrm(ks[5], (DEPTH, 6 * D_MODEL), 0.02),
        'norm1_g': 1.0 + nrm(ks[6], (DEPTH, D_MODEL), 0.02),
        'w_in': nrm(ks[7], (DEPTH, D_MODEL, IN_DIM), D_MODEL ** -0.5),
        'q_norm_g': 1.0 + nrm(ks[8], (DEPTH, HEAD_DIM), 0.02),
        'k_norm_g': 1.0 + nrm(ks[9], (DEPTH, HEAD_DIM), 0.02),
        'w_attn_out': nrm(ks[10], (DEPTH, Q_DIM, D_MODEL), Q_DIM ** -0.5),
        'conv_dw_w': nrm(ks[11], (DEPTH, CONV_WIDTH, CONV_DIM), CONV_WIDTH ** -0.5),
        'conv_dw_b': nrm(ks[12], (DEPTH, CONV_DIM), 0.02),
        'conv_ln_g': 1.0 + nrm(ks[13], (DEPTH, CONV_DIM), 0.02),
        'conv_ln_b': nrm(ks[14], (DEPTH, CONV_DIM), 0.02),
        'w_conv_out': nrm(ks[15], (DEPTH, CONV_DIM, D_MODEL), CONV_DIM ** -0.5),
        'w_out': nrm(ks[16], (DEPTH, D_MODEL, D_MODEL), D_MODEL ** -0.5),
        'norm2_g': 1.0 + nrm(ks[17], (DEPTH, D_MODEL), 0.02),
        'w_up': nrm(ks[18], (DEPTH, D_MODEL, 2 * FFN_DIM), D_MODEL ** -0.5),
        'ffn_dw_w': nrm(ks[19], (DEPTH, FFN_CONV_WIDTH, FFN_DIM), FFN_CONV_WIDTH ** -0.5),
        'ffn_dw_b': nrm(ks[20], (DEPTH, FFN_DIM), 0.02),
        'w_down': nrm(ks[21], (DEPTH, FFN_DIM, D_MODEL), FFN_DIM ** -0.5),
        'final_g': 1.0 + nrm(ks[22], (D_MODEL,), 0.02),
    }


def reference(x, c, ctx, c_ctx, w_mod, b_mod, norm1_g, w_in, q_norm_g, k_norm_g, w_attn_out,
              conv_dw_w, conv_dw_b, conv_ln_g, conv_ln_b, w_conv_out, w_out, norm2_g,
              w_up, ffn_dw_w, ffn_dw_b, w_down, final_g):
    B, L, _ = x.shape
    rows = L // GRID_W
    cos, sin = axial_rope(rows)
    for i in range(DEPTH):
        sh1, sc1, g1, sh2, sc2, g2 = modulation(c, w_mod[i], b_mod[i])
        csh1, csc1, cg1, csh2, csc2, cg2 = modulation(c_ctx[None], w_mod[i], b_mod[i])

        hx = modulate(rms_norm(x, norm1_g[i]), sh1, sc1)
        hc = modulate(rms_norm(ctx, norm1_g[i]), csh1, csc1)
        qx, kx, vx, ux, gx = project(hx, w_in[i], q_norm_g[i], k_norm_g[i])
        qx = apply_rope(qx, cos, sin)
        kx = apply_rope(kx, cos, sin)
        last = i == DEPTH - 1
        if last:
            kc, vc = context_kv(hc, w_in[i], k_norm_g[i])
        else:
            qc, kc, vc, uc, gc = project(hc, w_in[i], q_norm_g[i], k_norm_g[i])
        attn_x = block_attention(qx, jnp.concatenate([kc, kx], axis=1), jnp.concatenate([vc, vx], axis=1))
        mix_x = merge_branches(attn_x, ux, gx, w_attn_out[i], conv_dw_w[i], conv_dw_b[i],
                               conv_ln_g[i], conv_ln_b[i], w_conv_out[i], w_out[i])
        x = x + g1 * mix_x
        hx2 = modulate(rms_norm(x, norm2_g[i]), sh2, sc2)
        x = x + g2 * conv_glu_ffn(hx2, w_up[i], ffn_dw_w[i], ffn_dw_b[i], w_down[i])

        if not last:
            attn_c = block_attention(qc, kc, vc)
            mix_c = merge_branches(attn_c, uc, gc, w_attn_out[i], conv_dw_w[i], conv_dw_b[i],
                                   conv_ln_g[i], conv_ln_b[i], w_conv_out[i], w_out[i])
            ctx = ctx + cg1 * mix_c
            hc2 = modulate(rms_norm(ctx, norm2_g[i]), csh2, csc2)
            ctx = ctx + cg2 * conv_glu_ffn(hc2, w_up[i], ffn_dw_w[i], ffn_dw_b[i], w_down[i])
    return rms_norm(x, final_g)
```

```python
import numpy as np
import concourse.bass as bass
import concourse.mybir as mybir
from concourse.bass_utils import run_bass_kernel_spmd

F32 = mybir.dt.float32
BF16 = mybir.dt.bfloat16
ALU = mybir.AluOpType
AF = mybir.ActivationFunctionType
AX = mybir.AxisListType

NCORES = 8
L = 16384
D = 1024
KC = 8
OWN = 2048
HALO = 16
E = OWN + 2 * HALO
CH = 416
NCH = E // CH
NKEY = L + 256
NKT = NKEY // 128
FF = 2816
NFC = FF // 128
EPS = 1e-6

ENGS = ("pe", "act", "dve", "pool", "sp")


class Op:
    __slots__ = ("eng", "fn", "deps", "sig", "sigval", "dsem", "dval", "idx")


class Sched:
    def __init__(self):
        self.ops = {e: [] for e in ENGS}
        self.lastw = {}
        self.readers = {}
        self.dma_count = {}
        self.n = 0
        self.pending_barrier = {}
        self.live = True
        self.stop_after = None

    def phase_end(self, name):
        self.barrier()
        if self.stop_after == name:
            self.live = False

    def add(self, eng, fn, reads=(), writes=(), dsem=None):
        if not self.live:
            return None
        op = Op()
        op.eng = eng
        op.fn = fn
        op.dsem = dsem
        op.sig = False
        op.sigval = 0
        op.dval = 0
        op.idx = self.n
        self.n += 1
        deps = {}
        is_dma = dsem is not None

        def need(d, kind):
            if d is op:
                return
            d_dma = d.dsem is not None
            if not is_dma and not d_dma and d.eng == eng:
                if eng == "pe":
                    return
                if kind == "war":
                    return
            deps[d.idx] = d

        for k in reads:
            w = self.lastw.get(k)
            if w is not None:
                need(w, "raw")
            if isinstance(k, tuple) and k[0] == "ps":
                for r in self.readers.get(k, ()):
                    if r.eng != eng:
                        need(r, "rar")
        for k in writes:
            w = self.lastw.get(k)
            if w is not None:
                need(w, "waw")
            for r in self.readers.get(k, ()):
                need(r, "war")
        bar = self.pending_barrier.pop(eng, None)
        if bar is not None:
            for d in bar:
                if d is not op and not (d.eng == eng and d.dsem is None and not is_dma):
                    deps[d.idx] = d
        op.deps = list(deps.values())
        for k in writes:
            self.lastw[k] = op
            self.readers[k] = []
        for k in reads:
            self.readers.setdefault(k, []).append(op)
        if is_dma:
            c = self.dma_count.get(dsem, 0) + 1
            self.dma_count[dsem] = c
            op.dval = 16 * c
        self.ops[eng].append(op)
        return op

    def barrier(self):
        if not self.live:
            return
        lst = []
        for e in ENGS:
            last_c = None
            last_d = {}
            for op in self.ops[e]:
                if op.dsem is None:
                    last_c = op
                else:
                    last_d[id(op.dsem)] = op
            if last_c is not None:
                lst.append(last_c)
            lst.extend(last_d.values())
        for e in ENGS:
            prev = self.pending_barrier.get(e, [])
            self.pending_barrier[e] = prev + lst
        self.lastw = {}
        self.readers = {}

    def finalize(self, nc, engsem, final_sems=()):
        for e in ENGS:
            for op in self.ops[e]:
                for d in op.deps:
                    if d.dsem is None:
                        d.sig = True
        for e in ENGS:
            c = 0
            for op in self.ops[e]:
                if op.dsem is None and op.sig:
                    c += 1
                    op.sigval = c
        stats = {}

        def run(e, engine):
            waited = {}
            nw = 0
            for op in self.ops[e]:
                want = {}
                for d in op.deps:
                    if d.dsem is not None:
                        s, v = d.dsem, d.dval
                    else:
                        s, v = engsem[d.eng], d.sigval
                    key = id(s)
                    if key not in want or want[key][1] < v:
                        want[key] = (s, v)
                for key, (s, v) in want.items():
                    if waited.get(key, 0) >= v:
                        continue
                    engine.wait_ge(s, v)
                    waited[key] = v
                    nw += 1
                ins = op.fn(engine)
                if op.dsem is not None:
                    ins.then_inc(op.dsem, 16)
                elif op.sig:
                    ins.then_inc(engsem[e], 1)
            stats[e] = (len(self.ops[e]), nw)

        with nc.Block() as block:

            @block.tensor
            def _(eng):
                run("pe", eng)

            @block.scalar
            def _(eng):
                run("act", eng)

            @block.vector
            def _(eng):
                run("dve", eng)

            @block.gpsimd
            def _(eng):
                run("pool", eng)

            @block.sync
            def _(eng):
                run("sp", eng)
                for s in final_sems:
                    if s in self.dma_count:
                        eng.wait_ge(s, 16 * self.dma_count[s])

        return stats


import os
SKIP = os.environ.get('KSKIP', '').split(',')
SCRKIND = os.environ.get('KSCR', 'Internal')


def build_program(debug=False, stop_after=None):
    from contextlib import ExitStack

    nc = bass.Bass("TRN2", target_bir_lowering=False)
    S = Sched()
    S.stop_after = stop_after

    def din(name, shape):
        return nc.dram_tensor(name, list(shape), F32, kind="ExternalInput").ap()

    xT_all = din("xT_all", [L // 512, 128, KC * 512])
    ctxT = din("ctxT", [128, KC, 256])
    xT_own = din("xT_own", [128, KC, E])
    ropeK = din("ropeK", [L // 512 + 1, 128, 2 * 512])
    ropeQ = din("ropeQ", [128, 2, E])
    maskE = din("maskE", [128, 2 * HALO])
    ccin = din("ccin", [128, KC, 2])
    wmod = din("wmod", [128, KC, 6 * D])
    bmod = din("bmod", [128, 48])
    vec8 = din("vec8", [128, 3, KC])
    gqk = din("gqk", [128, 4])
    gqk_row = din("gqk_row", [128, 256])
    wq2 = din("wq2", [128, KC, 2, D])
    wk2 = din("wk2", [128, KC, 2, 256])
    wv = din("wv", [128, KC, 256])
    wu = din("wu", [128, KC, D])
    wg = din("wg", [128, KC, 2 * D])
    wao = din("wao", [128, KC, D])
    wco = din("wco", [128, 4, D])
    wout = din("wout", [128, KC, D])
    wup = din("wup", [128, KC, 2 * FF])
    wdn = din("wdn", [128, NFC, D])
    cdw = din("cdw", [128, 4, 31])
    cvec = din("cvec", [128, 3, 4])
    fdw = din("fdw", [128, NFC, 3])
    fdb = din("fdb", [128, NFC])
    ident_in = din("ident", [128, 128])

    yT = nc.dram_tensor("yT", [128, KC, OWN], F32, kind="ExternalOutput").ap()
    kscr = [nc.dram_tensor("kscr%d" % g, [128, NKEY], BF16, kind=SCRKIND).ap() for g in range(2)]
    vscr = nc.dram_tensor("vscr", [128, NKT, 256], BF16, kind=SCRKIND).ap()
    x1scr = nc.dram_tensor("x1scr", [128, KC, E], F32, kind=SCRKIND).ap()
    dbg = {}
    if debug:
        dbg["qT"] = nc.dram_tensor("dbg_qT", [128, KC, E], F32, kind="ExternalOutput").ap()
        dbg["k0"] = nc.dram_tensor("dbg_k0", [128, NKEY], F32, kind="ExternalOutput").ap()
        dbg["v0"] = nc.dram_tensor("dbg_v0", [128, NKT, 128], F32, kind="ExternalOutput").ap()
        dbg["at"] = nc.dram_tensor("dbg_at", [128, KC, E], F32, kind="ExternalOutput").ap()
        dbg["mod"] = nc.dram_tensor("dbg_mod", [128, 48, 2], F32, kind="ExternalOutput").ap()
        dbg["hT"] = nc.dram_tensor("dbg_hT", [128, 4, E], F32, kind="ExternalOutput").ap()
        dbg["yc"] = nc.dram_tensor("dbg_yc", [128, 4, E], F32, kind="ExternalOutput").ap()

    es = ExitStack()
    sem_n = [0]

    def newsem(st=None):
        sem_n[0] += 1
        return es.enter_context(nc.semaphore("sm%d" % sem_n[0]))

    def sb(st, name, shape, dt):
        return st.enter_context(nc.sbuf_tensor(name, list(shape), dt))

    with es:
        engsem = {e: newsem() for e in ("pe", "act", "dve", "pool")}
        sem_out = newsem()
        sem_scr = newsem()
        sem_const = newsem()

        PS = [es.enter_context(nc.psum_tensor("ps%d" % i, [128, 2, 512], F32)) for i in range(4)]

        def bank(b):
            return PS[b // 2][:, b % 2, :]

        def bk(b):
            return ("ps", b)

        ones_bf = sb(es, "ones_bf", [128, 128], BF16)
        ident_bf = sb(es, "ident_bf", [128, 128], BF16)
        eps_t = sb(es, "eps_t", [128, 1], F32)
        modsb = sb(es, "modsb", [128, 48, 2], F32)
        bmod_t = sb(es, "bmod_t", [128, 48], F32)
        vec8_t = sb(es, "vec8_t", [128, 3, KC], F32)
        a1 = sb(es, "a1", [128, KC], F32)
        a1c = sb(es, "a1c", [128, KC], F32)
        a2 = sb(es, "a2", [128, KC], F32)
        gqk_t = sb(es, "gqk_t", [128, 4], F32)
        nbias = sb(es, "nbias", [128, 1], F32)
        cvec_t = sb(es, "cvec_t", [128, 3, 4], F32)
        fdb_t = sb(es, "fdb_t", [128, NFC], F32)
        mask_t = sb(es, "mask_t", [128, 2 * HALO], F32)

        def b1(kc):
            return modsb[:, 0 + kc, 0:1]

        def b1c(kc):
            return modsb[:, 0 + kc, 1:2]

        def g1(kc):
            return modsb[:, 16 + kc, 0:1]

        def b2(kc):
            return modsb[:, 24 + kc, 0:1]

        def g2(kc):
            return modsb[:, 40 + kc, 0:1]

        S.add("pool", lambda e: e.memset(ones_bf[:], 1.0), writes=["ones"])
        S.add("pool", lambda e: e.memset(eps_t[:], EPS), writes=["eps"])
        S.add("pool", lambda e: e.dma_start(out=ident_bf[:], in_=ident_in), writes=["ident"], dsem=sem_const)
        S.add("sp", lambda e: e.dma_start(out=bmod_t[:], in_=bmod), writes=["bmod"], dsem=sem_const)
        S.add("sp", lambda e: e.dma_start(out=vec8_t[:], in_=vec8), writes=["vec8"], dsem=sem_const)
        S.add("sp", lambda e: e.dma_start(out=gqk_t[:], in_=gqk), writes=["gqk"], dsem=sem_const)
        S.add("sp", lambda e: e.dma_start(out=cvec_t[:], in_=cvec), writes=["cvec"], dsem=sem_const)
        S.add("sp", lambda e: e.dma_start(out=fdb_t[:], in_=fdb), writes=["fdb"], dsem=sem_const)
        S.add("sp", lambda e: e.dma_start(out=mask_t[:], in_=maskE), writes=["mask"], dsem=sem_const)
        CONSTK = ["ident", "bmod", "vec8", "gqk", "cvec", "fdb", "mask"]

        with ExitStack() as p0:
            cc_t = sb(p0, "cc_t", [128, KC, 2], F32)
            sc_t = sb(p0, "sc_t", [128, KC, 2], F32)
            wm = [sb(p0, "wm%d" % i, [128, KC, D], F32) for i in range(2)]
            wm_sem = [newsem() for _ in range(2)]
            grow = sb(p0, "grow", [128, 256], F32)
            gmax = sb(p0, "gmax", [128, 2], F32)
            gprod = sb(p0, "gprod", [1, 1], F32)
            ones_f = sb(p0, "ones_f", [1, 128], F32)
            S.add("sp", lambda e: e.dma_start(out=cc_t[:], in_=ccin), writes=["cc"], dsem=sem_const)
            S.add("sp", lambda e: e.dma_start(out=grow[:], in_=gqk_row), writes=["grow"], dsem=sem_const)
            ALLC = CONSTK + ["cc", "grow"]
            S.add("act", lambda e: e.activation(out=sc_t[:], in_=cc_t[:], func=AF.Silu), reads=ALLC, writes=["sc"])
            mps = bank(0)
            mps3 = PS[0][:, 0, 0:96].rearrange("p (a b) -> p a b", b=2)
            for v in range(6):
                bi = v % 2
                S.add("sp", lambda e, v=v, bi=bi: e.dma_start(out=wm[bi][:], in_=wmod[:, :, v * D:(v + 1) * D]),
                      writes=[("wm", bi)], dsem=wm_sem[bi])
                for fcol in range(8):
                    for kc in range(KC):
                        S.add("pe", lambda e, v=v, bi=bi, fcol=fcol, kc=kc: e.matmul(
                            mps3[:, v * 8 + fcol, :], lhsT=wm[bi][:, kc, fcol * 128:(fcol + 1) * 128],
                            rhs=sc_t[:, kc, :], start=(kc == 0), stop=(kc == KC - 1)),
                            reads=[("wm", bi), "sc"], writes=[bk(0)])
            for i in range(2):
                S.add("dve", lambda e, i=i: e.tensor_tensor(out=modsb[:, :, i], in0=mps3[:, :, i], in1=bmod_t[:], op=ALU.add),
                      reads=[bk(0)] + ALLC, writes=["modsb"])
            S.add("dve", lambda e: e.scalar_tensor_tensor(out=a1[:], in0=modsb[:, 8:16, 0], scalar=1.0, in1=vec8_t[:, 0, :],
                                                          op0=ALU.add, op1=ALU.mult), reads=["modsb"] + ALLC, writes=["a1"])
            S.add("dve", lambda e: e.scalar_tensor_tensor(out=a1c[:], in0=modsb[:, 8:16, 1], scalar=1.0, in1=vec8_t[:, 0, :],
                                                          op0=ALU.add, op1=ALU.mult), reads=["modsb"] + ALLC, writes=["a1c"])
            S.add("dve", lambda e: e.scalar_tensor_tensor(out=a2[:], in0=modsb[:, 32:40, 0], scalar=1.0, in1=vec8_t[:, 1, :],
                                                          op0=ALU.add, op1=ALU.mult), reads=["modsb"] + ALLC, writes=["a2"])
            import os
            if os.environ.get('KSKIP') == 'shift':
                S.add('pool', lambda e: e.memset(nbias[:], -12.0), writes=['nbias'])
            else:
                S.add("act", lambda e: e.activation(out=grow[:], in_=grow[:], func=AF.Abs),
                      reads=ALLC, writes=["grow"])
                S.add("dve", lambda e: e.tensor_reduce(out=gmax[:], in_=grow[:].rearrange("p (a b) -> p a b", a=2), axis=AX.X,
                                                       op=ALU.max), reads=["grow"], writes=["gmax"])
                S.add("dve", lambda e: e.scalar_tensor_tensor(out=nbias[:], in0=gmax[:, 0:1], scalar=-(128.0 ** 0.5), in1=gmax[:, 1:2],
                                                              op0=ALU.mult, op1=ALU.mult), reads=["gmax"], writes=["nbias"])
            if debug:
                S.add("sp", lambda e: e.dma_start(out=dbg["mod"], in_=modsb[:]), reads=["modsb"], dsem=sem_out)
            S.phase_end("P0")

        MODK = []

        def rsqrt_act(out_ap, in_ap, scl, rd, key):
            S.add("act", lambda e: e.activation(out=out_ap, in_=in_ap, func=AF.Ln, bias=eps_t[:], scale=scl),
                  reads=rd, writes=[key])
            S.add("act", lambda e: e.activation(out=out_ap, in_=out_ap, func=AF.Exp, scale=-0.5),
                  reads=[key], writes=[key])

        def front(tag, xs, n, sq, rstd, tmp, hx, avec, bfun, psb, xs_key, hx_key, part="ab", stag=None):
            stag = stag or tag
            if "a" in part:
                S.add("act", lambda e: e.activation(out=sq[:, :, 0:n], in_=xs[:, :, 0:n], func=AF.Square),
                      reads=[xs_key], writes=[stag + "sq"])
                for kc in range(KC):
                    S.add("pe", lambda e, kc=kc: e.matmul(bank(psb)[:, 0:n], lhsT=ones_bf[:], rhs=sq[:, kc, 0:n],
                                                         start=(kc == 0), stop=(kc == KC - 1)),
                          reads=[stag + "sq"], writes=[bk(psb)])
                rsqrt_act(rstd[:, 0:n], bank(psb)[:, 0:n], 1.0 / D, [bk(psb)], stag + "rstd")
            if "b" not in part:
                return
            for kc in range(KC):
                tb = kc % 4
                S.add("dve", lambda e, kc=kc, tb=tb: e.scalar_tensor_tensor(out=tmp[:, tb, 0:n], in0=xs[:, kc, 0:n],
                                                                              scalar=avec[:, kc:kc + 1], in1=rstd[:, 0:n],
                                                                              op0=ALU.mult, op1=ALU.mult),
                      reads=[xs_key, stag + "rstd"], writes=[(tag + "tmp", tb)])
                if kc % 2 == 0:
                    S.add("dve", lambda e, kc=kc, tb=tb: e.tensor_scalar(out=hx[:, kc, 0:n], in0=tmp[:, tb, 0:n], scalar1=bfun(kc),
                                                                          scalar2=None, op0=ALU.add),
                          reads=[(tag + "tmp", tb)], writes=[(hx_key, kc)])
                else:
                    S.add("act", lambda e, kc=kc, tb=tb: e.activation(out=hx[:, kc, 0:n], in_=tmp[:, tb, 0:n], func=AF.Identity,
                                                                       bias=bfun(kc), scale=1.0),
                          reads=[(tag + "tmp", tb)], writes=[(hx_key, kc)])

        def head_norm_rope(tag, pA, pB, n, gcol, rope_t, rope_key, sqh, rsth, t1, t2, out_ap, out_key, pss):
            S.add("act", lambda e: e.activation(out=sqh[:, 0:n], in_=bank(pA)[:, 0:n], func=AF.Square),
                  reads=[bk(pA)], writes=[tag + "sqh"])
            S.add("pe", lambda e: e.matmul(bank(pss)[:, 0:n], lhsT=ones_bf[:], rhs=sqh[:, 0:n], start=True, stop=True),
                  reads=[tag + "sqh"], writes=[bk(pss)])
            rsqrt_act(rsth[:, 0:n], bank(pss)[:, 0:n], 1.0 / 128, [bk(pss)], tag + "rsth")
            S.add("dve", lambda e: e.scalar_tensor_tensor(out=t1[:, 0:n], in0=bank(pA)[:, 0:n], scalar=gqk_t[:, gcol:gcol + 1],
                                                          in1=rope_t[:, 0, 0:n], op0=ALU.mult, op1=ALU.mult),
                  reads=[bk(pA), rope_key], writes=[tag + "t1"])
            S.add("dve", lambda e: e.scalar_tensor_tensor(out=t2[:, 0:n], in0=bank(pB)[:, 0:n], scalar=gqk_t[:, gcol + 1:gcol + 2],
                                                          in1=rope_t[:, 1, 0:n], op0=ALU.mult, op1=ALU.mult),
                  reads=[bk(pB), rope_key], writes=[tag + "t2"])
            S.add("dve", lambda e: e.tensor_tensor(out=t1[:, 0:n], in0=t1[:, 0:n], in1=t2[:, 0:n], op=ALU.add),
                  reads=[tag + "t1", tag + "t2"], writes=[tag + "t1"])
            S.add("dve", lambda e: e.tensor_tensor(out=out_ap, in0=t1[:, 0:n], in1=rsth[:, 0:n], op=ALU.mult),
                  reads=[tag + "t1", tag + "rsth"], writes=[out_key])

        with ExitStack() as sQA:
            QA = sb(sQA, "QA", [128, KC, E], BF16)

            def alloc_front(st, pfx, W):
                d = {}
                d["xs"] = [sb(st, pfx + "xs%d" % i, [128, KC, W], F32) for i in range(2)]
                d["xs_sem"] = [newsem() for _ in range(2)]
                d["rp"] = [sb(st, pfx + "rp%d" % i, [128, 2, W], F32) for i in range(2)]
                d["rp_sem"] = [newsem() for _ in range(2)]
                d["sq"] = sb(st, pfx + "sq", [128, KC, W], BF16)
                d["tmp"] = sb(st, pfx + "tmp", [128, 4, W], F32)
                d["rstd"] = sb(st, pfx + "rstd", [128, W], F32)
                d["hx"] = [sb(st, pfx + "hx%d" % i, [128, KC, W], BF16) for i in range(2)]
                d["sqh"] = [sb(st, pfx + "sqh%d" % i, [128, W], BF16) for i in range(2)]
                d["rsth"] = [sb(st, pfx + "rsth%d" % i, [128, W], F32) for i in range(2)]
                d["t1"] = [sb(st, pfx + "t1%d" % i, [128, W], F32) for i in range(2)]
                d["t2"] = [sb(st, pfx + "t2%d" % i, [128, W], F32) for i in range(2)]
                return d

            with ExitStack() as sA:
                wq_t = sb(sA, "wq_t", [128, KC, 2, D], BF16)
                sem_w = newsem()
                for kc in range(KC):
                    S.add("pool", lambda e, kc=kc: e.dma_start(out=wq_t[:, kc], in_=wq2[:, kc]), writes=["wq"], dsem=sem_w)
                WA = ["wq"]
                fa = alloc_front(sA, "a", CH)
                def a1_front(c):
                    bi = c % 2
                    xs_, rp_, hx_ = fa["xs"][bi], fa["rp"][bi], fa["hx"][bi]
                    S.add("sp", lambda e, c=c, xs_=xs_: e.dma_start(out=xs_[:], in_=xT_own[:, :, c * CH:(c + 1) * CH]),
                          writes=[("xs", bi)], dsem=fa["xs_sem"][bi])
                    S.add("sp", lambda e, c=c, rp_=rp_: e.dma_start(out=rp_[:], in_=ropeQ[:, :, c * CH:(c + 1) * CH]),
                          writes=[("rp", bi)], dsem=fa["rp_sem"][bi])
                    front("A", xs_, CH, fa["sq"], fa["rstd"], fa["tmp"], hx_, a1, b1, 0, ("xs", bi), ("hx", bi))

                def a1_heads(c):
                    bi = c % 2
                    n = CH
                    rp_, hx_ = fa["rp"][bi], fa["hx"][bi]
                    hxk = [(("hx", bi), kc) for kc in range(KC)]
                    for h in range(8):
                        hb = h % 2
                        pA, pB, pss = 2 + 2 * hb, 3 + 2 * hb, 1
                        for ver, pb_ in ((0, pA), (1, pB)):
                            for kc in range(KC):
                                S.add("pe", lambda e, kc=kc, ver=ver, pb_=pb_, h=h, hx_=hx_, n=n: e.matmul(
                                    bank(pb_)[:, 0:n], lhsT=wq_t[:, kc, ver, h * 128:(h + 1) * 128], rhs=hx_[:, kc, 0:n],
                                    start=(kc == 0), stop=(kc == KC - 1)), reads=WA + hxk, writes=[bk(pb_)])
                        head_norm_rope("q%d" % hb, pA, pB, n, 0, rp_, ("rp", bi), fa["sqh"][hb], fa["rsth"][hb],
                                       fa["t1"][hb], fa["t2"][hb], QA[:, h, c * CH:(c + 1) * CH], ("QA", h, c), pss)

                a1_front(0)
                for c in range(NCH):
                    if c + 1 < NCH:
                        a1_front(c + 1)
                    a1_heads(c)
                S.phase_end("A1")
            with ExitStack() as sA:
                wk_t = sb(sA, "wk_t", [128, KC, 2, 256], BF16)
                wv_t = sb(sA, "wv_t", [128, KC, 256], BF16)
                sem_w = newsem()
                S.add("pool", lambda e: e.dma_start(out=wk_t[:], in_=wk2), writes=["wk"], dsem=sem_w)
                S.add("pool", lambda e: e.dma_start(out=wv_t[:], in_=wv), writes=["wv"], dsem=sem_w)
                WA = ["wk", "wv"]
                fa = alloc_front(sA, "b", 512)
                fa["xs"].append(sb(sA, "bxs2", [128, KC, 512], F32))
                fa["xs_sem"].append(newsem())
                fa["sq2"] = [fa["sq"], sb(sA, "bsq2", [128, KC, 512], BF16)]
                fa["rstd2"] = [fa["rstd"], sb(sA, "brstd2", [128, 512], F32)]
                kst = [[sb(sA, "kst%d_%d" % (g, i), [128, 512], BF16) for i in range(2)] for g in range(2)]
                vst = [sb(sA, "vst%d" % i, [128, 4, 256], BF16) for i in range(2)]
                kst_sem = [[newsem() for i in range(2)] for g in range(2)]
                vst_sem = [[newsem() for i in range(2)] for g in range(2)]
                NJ = L // 512 + 1
                JL = [int(v) for v in os.environ['KJ'].split(',')] if os.environ.get('KJ') else list(range(NJ))
                def a2_front(j):
                    xi = j % 3
                    bi = j % 2
                    isctx = j == NJ - 1
                    n = 256 if isctx else 512
                    xs_ = fa["xs"][xi]
                    if isctx:
                        S.add("sp", lambda e, xs_=xs_: e.dma_start(out=xs_[:, :, 0:256], in_=ctxT), writes=[("xs", xi)], dsem=fa["xs_sem"][xi])
                    else:
                        S.add("sp", lambda e, xs_=xs_, j=j: e.dma_start(out=xs_[:].rearrange("p a b -> p (a b)"), in_=xT_all[j]),
                              writes=[("xs", xi)], dsem=fa["xs_sem"][xi])
                    front("A", xs_, n, fa["sq2"][bi], fa["rstd2"][bi], fa["tmp"], fa["hx"][bi], a1c if isctx else a1, b1c if isctx else b1, 0,
                          ("xs", xi), ("hx", bi), part="a", stag="A%d" % bi)

                def a2_front_b(j):
                    xi = j % 3
                    bi = j % 2
                    isctx = j == NJ - 1
                    n = 256 if isctx else 512
                    xs_, rp_, hx_ = fa["xs"][xi], fa["rp"][bi], fa["hx"][bi]
                    S.add("sp", lambda e, rp_=rp_, j=j: e.dma_start(out=rp_[:].rearrange("p a b -> p (a b)"), in_=ropeK[j]),
                          writes=[("rp", bi)], dsem=fa["rp_sem"][bi])
                    front("A", xs_, n, fa["sq2"][bi], fa["rstd2"][bi], fa["tmp"], hx_, a1c if isctx else a1, b1c if isctx else b1, 0,
                          ("xs", xi), ("hx", bi), part="b", stag="A%d" % bi)

                def a2_kv(j, which):
                    bi = j % 2
                    isctx = j == NJ - 1
                    n = 256 if isctx else 512
                    k0 = j * 512
                    xs_, rp_, hx_ = fa["xs"][bi], fa["rp"][bi], fa["hx"][bi]
                    hxk = [(("hx", bi), kc) for kc in range(KC)]
                    for kvh in range(2 if which == "k" else 0):
                        pA, pB, pss = 2 + 2 * kvh, 3 + 2 * kvh, 1
                        for ver, pb_ in ((0, pA), (1, pB)):
                            for kc in range(KC):
                                S.add("pe", lambda e, kc=kc, ver=ver, pb_=pb_, kvh=kvh, hx_=hx_, n=n: e.matmul(
                                    bank(pb_)[:, 0:n], lhsT=wk_t[:, kc, ver, kvh * 128:(kvh + 1) * 128], rhs=hx_[:, kc, 0:n],
                                    start=(kc == 0), stop=(kc == KC - 1)), reads=WA + hxk, writes=[bk(pb_)])
                        ks_ = kst[kvh][bi]
                        head_norm_rope("k%d" % kvh, pA, pB, n, 2, rp_, ("rp", bi), fa["sqh"][kvh], fa["rsth"][kvh],
                                       fa["t1"][kvh], fa["t2"][kvh], ks_[:, 0:n], ("kst", kvh, bi), pss)
                        if 'kscr' not in SKIP:
                            S.add("pool", lambda e, ks_=ks_, kvh=kvh, k0=k0, n=n: e.dma_start(out=kscr[kvh][:, k0:k0 + n], in_=ks_[:, 0:n]),
                                  reads=[("kst", kvh, bi)], dsem=kst_sem[kvh][bi])
                    if which != "v":
                        return
                    nt = n // 128
                    for tt in range(0 if 'v' in SKIP else nt):
                        pv_ = int(os.environ.get("KPV", "6")) + (tt % 2)
                        for kc in range(KC):
                            S.add("pe", lambda e, kc=kc, tt=tt, pv_=pv_, hx_=hx_: e.matmul(
                                bank(pv_)[:, 0:256], lhsT=hx_[:, kc, tt * 128:(tt + 1) * 128], rhs=wv_t[:, kc, :],
                                start=(kc == 0), stop=(kc == KC - 1)), reads=WA + hxk, writes=[bk(pv_)])
                        if tt % 2 == 0:
                            S.add("act", lambda e, pv_=pv_, tt=tt, bi=bi: e.activation(out=vst[bi][:, tt, :], in_=bank(pv_)[:, 0:256], func=AF.Identity),
                                  reads=[bk(pv_)], writes=[("vst", bi)])
                        else:
                            S.add("dve", lambda e, pv_=pv_, tt=tt, bi=bi: e.tensor_copy(out=vst[bi][:, tt, :], in_=bank(pv_)[:, 0:256]),
                                  reads=[bk(pv_)], writes=[("vst", bi)])
                    S.add("pool", lambda e, bi=bi, j=j, nt=nt: e.dma_start(out=vscr[:, j * 4:j * 4 + nt, :], in_=vst[bi][:, 0:nt, :]),
                          reads=[("vst", bi)], dsem=vst_sem[0][bi])

                a2_front(JL[0])
                if len(JL) > 1:
                    a2_front(JL[1])
                a2_front_b(JL[0])
                for jj, j in enumerate(JL):
                    if jj + 2 < len(JL):
                        a2_front(JL[jj + 2])
                    a2_kv(j, "k")
                    if jj + 1 < len(JL):
                        a2_front_b(JL[jj + 1])
                    a2_kv(j, "v")
                if debug and 'dump' not in SKIP:
                    with ExitStack() as sd:
                        dtmp = sb(sd, "dtmp", [128, 2080], F32)
                        for h in range(8):
                            S.add("dve", lambda e, dtmp=dtmp, h=h: e.tensor_copy(out=dtmp[:, 0:E], in_=QA[:, h, :]), writes=["dtmp"])
                            S.add("sp", lambda e, dtmp=dtmp, h=h: e.dma_start(out=dbg["qT"][:, h, :], in_=dtmp[:, 0:E]), reads=["dtmp"], dsem=sem_out)
                S.phase_end("A2")

            with ExitStack() as sB:
                KK = [sb(sB, "K%d" % g, [128, NKEY], BF16) for g in range(2)]
                VA = sb(sB, "VA", [128, NKT, 256], BF16)
                NPC = 5
                TPP = NKT // NPC
                kv_sem = [[newsem() for p in range(NPC)] for g in range(2)]
                vv_sem = [[newsem() for p in range(NPC)] for g in range(2)]
                for g in range(2):
                    for p in range(NPC):
                        S.add("sp", lambda e, g=g, p=p: e.dma_start(out=KK[g][:, p * TPP * 128:(p + 1) * TPP * 128],
                                                                    in_=kscr[g][:, p * TPP * 128:(p + 1) * TPP * 128]),
                              writes=[("K", g, p)], dsem=kv_sem[g][p])
                        if g == 0:
                            S.add("sp", lambda e, p=p: e.dma_start(out=VA[:, p * TPP:(p + 1) * TPP, :],
                                                                   in_=vscr[:, p * TPP:(p + 1) * TPP, :]),
                                  writes=[("V", 0, p), ("V", 1, p)], dsem=vv_sem[g][p])
                if debug:
                    if True:
                        dtmp = sb(sB, "dtmp1", [128, 2080], F32)
                        for q in range(8):
                            S.add("dve", lambda e, dtmp=dtmp, q=q: e.tensor_copy(out=dtmp[:], in_=KK[0][:, q * 2080:(q + 1) * 2080]),
                                  reads=[("K", 0, p) for p in range(NPC)], writes=["dtmp"])
                            S.add("sp", lambda e, dtmp=dtmp, q=q: e.dma_start(out=dbg["k0"][:, q * 2080:(q + 1) * 2080], in_=dtmp[:]), reads=["dtmp"], dsem=sem_out)
                        for q in range(10):
                            S.add("dve", lambda e, dtmp=dtmp, q=q: e.tensor_copy(out=dtmp[:, 0:13 * 128].rearrange("p (a b) -> p a b", b=128),
                                                                        in_=VA[:, q * 13:(q + 1) * 13, 0:128]),
                                  reads=[("V", 0, p) for p in range(NPC)], writes=["dtmp"])
                            S.add("sp", lambda e, q=q, dtmp=dtmp: e.dma_start(out=dbg["v0"][:, q * 13:(q + 1) * 13, :],
                                                                 in_=dtmp[:, 0:13 * 128].rearrange("p (a b) -> p a b", b=128)),
                                  reads=["dtmp"], dsem=sem_out)
                NPB = 6
                PT = [sb(sB, "pt%d" % i, [128, 2, CH], BF16) for i in range(NPB)]
                s12 = [sb(sB, "s12_%d" % i, [128, 2, CH], BF16) for i in range(3)]
                ssum = [sb(sB, "ssum%d" % i, [128, CH], BF16) for i in range(3)]
                osb = [sb(sB, "osb%d" % i, [128, CH], F32) for i in range(2)]
                rec = [sb(sB, "rec%d" % i, [128, CH], F32) for i in range(2)]
                NB = NKT // 2
                batches = []
                for h in range(8):
                    for c in range(NCH):
                        for b in range(NB):
                            batches.append((h, c, b))
                scale = 128.0 ** -0.5

                def qk(i):
                    h, c, b = batches[i]
                    g = h // 4
                    sbuf_i = i % 3
                    for u in range(2):
                        j = 2 * b + u
                        S.add("pe", lambda e, g=g, j=j, h=h, c=c, sbuf_i=sbuf_i, u=u: e.matmul(
                            PS[sbuf_i][:, u, 0:CH], lhsT=KK[g][:, j * 128:(j + 1) * 128], rhs=QA[:, h, c * CH:(c + 1) * CH],
                            start=True, stop=True), reads=[("QA", h, c), ("K", g, j // TPP)], writes=[bk(2 * sbuf_i + u)])

                def ex(i):
                    sbuf_i = i % 3
                    pi = i % NPB
                    S.add("act", lambda e, sbuf_i=sbuf_i, pi=pi: e.activation(
                        out=PT[pi][:], in_=PS[sbuf_i][:, :, 0:CH], func=AF.Exp, bias=nbias[:], scale=scale),
                        reads=[bk(2 * sbuf_i), bk(2 * sbuf_i + 1)], writes=[("pt", pi)])

                def pv(i):
                    h, c, b = batches[i]
                    g = h // 4
                    pi = i % NPB
                    po = 6
                    for u in range(2):
                        j = 2 * b + u
                        first = (b == 0 and u == 0)
                        last = (b == NB - 1 and u == 1)
                        S.add("pe", lambda e, g=g, j=j, pi=pi, u=u, po=po, first=first, last=last: e.matmul(
                            bank(po)[:, 0:CH], lhsT=VA[:, j, g * 128:(g + 1) * 128], rhs=PT[pi][:, u, :], start=first, stop=last),
                            reads=[("pt", pi), ("V", g, j // TPP)], writes=[bk(po)])

                sum_n = [0]

                def dve_sum(i):
                    h, c, b = batches[i]
                    k = sum_n[0] % 3
                    sum_n[0] += 1
                    pi = i % NPB
                    if b % 2 == 1:
                        pj = (i - 1) % NPB
                        S.add("dve", lambda e, pi=pi, pj=pj, k=k: e.tensor_tensor(out=s12[k][:], in0=PT[pj][:], in1=PT[pi][:], op=ALU.add),
                              reads=[("pt", pi), ("pt", pj)], writes=[("s12", k)])
                        S.add("dve", lambda e, k=k: e.tensor_tensor(out=ssum[k][:], in0=s12[k][:, 0, :], in1=s12[k][:, 1, :], op=ALU.add),
                              reads=[("s12", k)], writes=[("ssum", k)])
                    else:
                        S.add("dve", lambda e, pi=pi, k=k: e.tensor_tensor(out=ssum[k][:], in0=PT[pi][:, 0, :], in1=PT[pi][:, 1, :], op=ALU.add),
                              reads=[("pt", pi)], writes=[("ssum", k)])
                    return (k, 7, b == 1, b == NB - 1)

                def sum_mm(desc):
                    k, psm, first, last = desc
                    S.add("pe", lambda e, k=k, psm=psm, first=first, last=last: e.matmul(
                        bank(psm)[:, 0:CH], lhsT=ones_bf[:], rhs=ssum[k][:], start=first, stop=last),
                        reads=[("ssum", k)], writes=[bk(psm)])

                def norm(i):
                    h, c, b = batches[i]
                    itn = h * NCH + c
                    ob = itn % 2
                    po, psm = 6, 7
                    S.add("dve", lambda e, ob=ob: e.tensor_copy(out=osb[ob][:], in_=bank(6)[:, 0:CH]),
                          reads=[bk(6)], writes=[("osb", ob)])
                    S.add("dve", lambda e, ob=ob: e.tensor_copy(out=rec[ob][:], in_=bank(7)[:, 0:CH]),
                          reads=[bk(7)], writes=[("rec", ob)])
                    S.add("dve", lambda e, ob=ob: e.reciprocal(out=rec[ob][:], in_=rec[ob][:]),
                          reads=[("rec", ob)], writes=[("rec", ob)])
                    S.add("dve", lambda e, ob=ob, h=h, c=c: e.tensor_tensor(
                        out=QA[:, h, c * CH:(c + 1) * CH], in0=osb[ob][:], in1=rec[ob][:], op=ALU.mult),
                        reads=[("osb", ob), ("rec", ob)], writes=[("QA", h, c)])

                NBT = len(batches)
                qk(0)
                ex(0)
                qk(1)
                ex(1)
                pending = None
                for i in range(NBT):
                    if i + 2 < NBT:
                        qk(i + 2)
                        ex(i + 2)
                    pv(i)
                    if pending is not None:
                        sum_mm(pending)
                        pending = None
                    b = batches[i][2]
                    if b % 2 == 1 or b == NB - 1:
                        pending = dve_sum(i)
                    if b == NB - 1:
                        sum_mm(pending)
                        pending = None
                        norm(i)
                if debug:
                    with ExitStack() as sd:
                        dtmp = sb(sd, "dtmp2", [128, 2080], F32)
                        for h in range(8):
                            S.add("dve", lambda e, dtmp=dtmp, h=h: e.tensor_copy(out=dtmp[:, 0:E], in_=QA[:, h, :]),
                                  reads=[("QA", h, c) for c in range(NCH)], writes=["dtmp"])
                            S.add("sp", lambda e, dtmp=dtmp, h=h: e.dma_start(out=dbg["at"][:, h, :], in_=dtmp[:, 0:E]), reads=["dtmp"], dsem=sem_out)
                S.phase_end("B")
            with ExitStack() as sC:
                PADH = 16
                ycT = sb(sC, "ycT", [128, 4, E], BF16)
                sC23 = ExitStack()
                hT = sb(sC23, "hT", [128, 4, E + 2 * PADH], BF16)
                S.add("pool", lambda e: e.memset(hT[:, :, 0:PADH], 0.0), writes=["hTpadL"])
                S.add("pool", lambda e: e.memset(hT[:, :, PADH + E:], 0.0), writes=["hTpadR"])
                with ExitStack() as sC2:
                    wu_t = sb(sC2, "wu_t", [128, KC, D], BF16)
                    sem_w = newsem()
                    for kc in range(KC):
                        S.add("pool", lambda e, kc=kc: e.dma_start(out=wu_t[:, kc], in_=wu[:, kc]), writes=["wu"], dsem=sem_w)
                    xs_c2 = [sb(sC2, "cxs%d" % i, [128, KC, CH], F32) for i in range(2)]
                    xs_sem = [newsem() for _ in range(2)]
                    sq = sb(sC2, "csq", [128, KC, CH], BF16)
                    tmp = sb(sC2, "ctmp", [128, 4, CH], F32)
                    rstd = sb(sC2, "crstd", [128, CH], F32)
                    hx_b = [sb(sC2, "chx%d" % i, [128, KC, CH], BF16) for i in range(2)]
                    sg = [sb(sC2, "sg%d" % i, [128, CH], F32) for i in range(2)]
                    hh = [sb(sC2, "hh%d" % i, [128, CH], F32) for i in range(2)]
                    for c in range(NCH):
                        bi = c % 2
                        S.add("sp", lambda e, c=c, bi=bi: e.dma_start(out=xs_c2[bi][:], in_=xT_own[:, :, c * CH:(c + 1) * CH]),
                              writes=[("xs", bi)], dsem=xs_sem[bi])
                        front("C", xs_c2[bi], CH, sq, rstd, tmp, hx_b[bi], a1, b1, 0, ("xs", bi), ("hx", bi))
                        hxk = [(("hx", bi), kc) for kc in range(KC)]
                        for fc in range(4):
                            fb = fc % 2
                            pa, pb_ = 2 + 2 * fb, 3 + 2 * fb
                            for half, pp in ((0, pa), (1, pb_)):
                                col = half * 512 + fc * 128
                                for kc in range(KC):
                                    S.add("pe", lambda e, kc=kc, pp=pp, col=col, bi=bi: e.matmul(
                                        bank(pp)[:, 0:CH], lhsT=wu_t[:, kc, col:col + 128], rhs=hx_b[bi][:, kc, :],
                                        start=(kc == 0), stop=(kc == KC - 1)), reads=["wu"] + hxk, writes=[bk(pp)])
                            S.add("act", lambda e, fb=fb, pb_=pb_: e.activation(out=sg[fb][:], in_=bank(pb_)[:, 0:CH], func=AF.Sigmoid),
                                  reads=[bk(pb_)], writes=[("sg", fb)])
                            S.add("dve", lambda e, fb=fb, pa=pa, fc=fc, c=c: e.tensor_tensor(
                                out=hT[:, fc, PADH + c * CH:PADH + (c + 1) * CH], in0=bank(pa)[:, 0:CH], in1=sg[fb][:], op=ALU.mult),
                                reads=[bk(pa), ("sg", fb)], writes=[("hT", fc, c)])
                            if c == 0 or c == NCH - 1:
                                e0 = 0 if c == 0 else E - HALO
                                m0 = 0 if c == 0 else HALO
                                S.add("dve", lambda e, fc=fc, e0=e0, m0=m0: e.tensor_tensor(
                                    out=hT[:, fc, PADH + e0:PADH + e0 + HALO], in0=hT[:, fc, PADH + e0:PADH + e0 + HALO],
                                    in1=mask_t[:, m0:m0 + HALO], op=ALU.mult),
                                    reads=[("hT", fc, c)], writes=[("hT", fc, c)])
                    S.phase_end("C2")
                with ExitStack() as sC3:
                    cdiag = sb(sC3, "cdiag", [128, 4, 31, 128], BF16)
                    cdw_t = sb(sC3, "cdw_t", [128, 4, 31], F32)
                    sem_w = newsem()
                    S.add("sp", lambda e: e.dma_start(out=cdw_t[:], in_=cdw), writes=["cdw"], dsem=sem_w)
                    for fc in range(4):
                        for j in range(31):
                            S.add("dve", lambda e, fc=fc, j=j: e.tensor_scalar(out=cdiag[:, fc, j, :], in0=ident_bf[:],
                                                                              scalar1=cdw_t[:, fc, j:j + 1], scalar2=None, op0=ALU.mult),
                                  reads=["cdw"], writes=[("cdiag", fc)])
                    cvf = sb(sC3, "cvf", [128, 4, CH], F32)
                    cvb = sb(sC3, "cvb", [128, 4, CH], BF16)
                    csq = sb(sC3, "csq3", [128, 4, CH], BF16)
                    mean = sb(sC3, "mean", [128, CH], F32)
                    msq = sb(sC3, "msq", [128, CH], F32)
                    var = sb(sC3, "var", [128, CH], F32)
                    tt_ = [sb(sC3, "tt%d" % i, [128, CH], F32) for i in range(2)]
                    for c in range(NCH):
                        for fc in range(4):
                            pc = 2 + (fc % 2)
                            for j in range(31):
                                off = c * CH + 1 + j
                                S.add("pe", lambda e, fc=fc, j=j, pc=pc, off=off: e.matmul(
                                    bank(pc)[:, 0:CH], lhsT=cdiag[:, fc, j, :], rhs=hT[:, fc, off:off + CH],
                                    start=(j == 0), stop=(j == 30)), reads=[("cdiag", fc)], writes=[bk(pc)])
                            S.add("act", lambda e, fc=fc, pc=pc: e.activation(out=cvf[:, fc, :], in_=bank(pc)[:, 0:CH], func=AF.Identity,
                                                                              bias=cvec_t[:, 0, fc:fc + 1], scale=1.0),
                                  reads=[bk(pc)], writes=[("cvf", fc)])
                            S.add("dve", lambda e, fc=fc: e.tensor_copy(out=cvb[:, fc, :], in_=cvf[:, fc, :]),
                                  reads=[("cvf", fc)], writes=[("cvb", fc)])
                            S.add("act", lambda e, fc=fc: e.activation(out=csq[:, fc, :], in_=cvf[:, fc, :], func=AF.Square),
                                  reads=[("cvf", fc)], writes=[("csq", fc)])
                        for fc in range(4):
                            S.add("pe", lambda e, fc=fc: e.matmul(bank(0)[:, 0:CH], lhsT=ones_bf[:], rhs=cvb[:, fc, :],
                                                                   start=(fc == 0), stop=(fc == 3)), reads=[("cvb", fc)], writes=[bk(0)])
                        for fc in range(4):
                            S.add("pe", lambda e, fc=fc: e.matmul(bank(1)[:, 0:CH], lhsT=ones_bf[:], rhs=csq[:, fc, :],
                                                                   start=(fc == 0), stop=(fc == 3)), reads=[("csq", fc)], writes=[bk(1)])
                        S.add("dve", lambda e: e.tensor_scalar(out=mean[:], in0=bank(0)[:, 0:CH], scalar1=1.0 / 512, scalar2=None, op0=ALU.mult),
                              reads=[bk(0)], writes=["mean"])
                        S.add("dve", lambda e: e.tensor_tensor(out=msq[:], in0=mean[:], in1=mean[:], op=ALU.mult), reads=["mean"], writes=["msq"])
                        S.add("dve", lambda e: e.scalar_tensor_tensor(out=var[:], in0=bank(1)[:, 0:CH], scalar=1.0 / 512, in1=msq[:],
                                                                      op0=ALU.mult, op1=ALU.subtract), reads=[bk(1), "msq"], writes=["var"])
                        rsqrt_act(var[:], var[:], 1.0, ["var"], "var")
                        for fc in range(4):
                            tb = fc % 2
                            S.add("dve", lambda e, fc=fc, tb=tb: e.tensor_tensor(out=tt_[tb][:], in0=cvf[:, fc, :], in1=mean[:], op=ALU.subtract),
                                  reads=[("cvf", fc), "mean"], writes=[("tt", tb)])
                            S.add("dve", lambda e, tb=tb: e.tensor_tensor(out=tt_[tb][:], in0=tt_[tb][:], in1=var[:], op=ALU.mult),
                                  reads=[("tt", tb), "var"], writes=[("tt", tb)])
                            S.add("act", lambda e, fc=fc, tb=tb, c=c: e.activation(out=ycT[:, fc, c * CH:(c + 1) * CH], in_=tt_[tb][:], func=AF.Silu,
                                                                                 bias=cvec_t[:, 2, fc:fc + 1], scale=cvec_t[:, 1, fc:fc + 1]),
                                  reads=[("tt", tb)], writes=[("ycT", fc, c)])
                    if debug:
                        with ExitStack() as sd:
                            dtmp = sb(sd, "dtmp3", [128, 2080], F32)
                            for fc in range(4):
                                S.add("dve", lambda e, dtmp=dtmp, fc=fc: e.tensor_copy(out=dtmp[:], in_=hT[:, fc, PADH:PADH + E]), writes=["dtmp"])
                                S.add("sp", lambda e, dtmp=dtmp, fc=fc: e.dma_start(out=dbg["hT"][:, fc, :], in_=dtmp[:]), reads=["dtmp"], dsem=sem_out)
                            for fc in range(4):
                                S.add("dve", lambda e, dtmp=dtmp, fc=fc: e.tensor_copy(out=dtmp[:], in_=ycT[:, fc, :]),
                                      reads=[("ycT", fc, c) for c in range(NCH)], writes=["dtmp"])
                                S.add("sp", lambda e, dtmp=dtmp, fc=fc: e.dma_start(out=dbg["yc"][:, fc, :], in_=dtmp[:]), reads=["dtmp"], dsem=sem_out)
                    S.phase_end("C3")
                sC23.close()
                with ExitStack() as sC4:
                    wg_t = sb(sC4, "wg_t", [128, KC, 2 * D], BF16)
                    wao_t = sb(sC4, "wao_t", [128, KC, D], BF16)
                    wco_t = sb(sC4, "wco_t", [128, 4, D], BF16)
                    wout_t = sb(sC4, "wout_t", [128, KC, D], BF16)
                    sem_w = newsem()
                    for kc in range(KC):
                        S.add("pool", lambda e, kc=kc: e.dma_start(out=wg_t[:, kc], in_=wg[:, kc]), writes=["w4"], dsem=sem_w)
                    S.add("pool", lambda e: e.dma_start(out=wao_t[:], in_=wao), writes=["w4"], dsem=sem_w)
                    S.add("pool", lambda e: e.dma_start(out=wco_t[:], in_=wco), writes=["w4"], dsem=sem_w)
                    S.add("pool", lambda e: e.dma_start(out=wout_t[:], in_=wout), writes=["w4"], dsem=sem_w)
                    xs_b = [sb(sC4, "dxs%d" % i, [128, KC, CH], F32) for i in range(2)]
                    xs_sem = [newsem() for _ in range(2)]
                    sq = sb(sC4, "dsq", [128, KC, CH], BF16)
                    tmp = sb(sC4, "dtmp_", [128, 4, CH], F32)
                    rstd = sb(sC4, "drstd", [128, CH], F32)
                    hx = sb(sC4, "dhx", [128, KC, CH], BF16)
                    mT = sb(sC4, "mT", [128, KC, CH], BF16)
                    sa = [sb(sC4, "sa%d" % i, [128, CH], F32) for i in range(2)]
                    sb_ = [sb(sC4, "sbb%d" % i, [128, CH], F32) for i in range(2)]
                    u1 = [sb(sC4, "u1%d" % i, [128, CH], F32) for i in range(2)]
                    u2 = [sb(sC4, "u2%d" % i, [128, CH], F32) for i in range(2)]
                    x1o_sem = [newsem() for _ in range(2)]
                    def d_load(c):
                        bi = c % 2
                        cs = slice(c * CH, (c + 1) * CH)
                        S.add("sp", lambda e, cs=cs, bi=bi: e.dma_start(out=xs_b[bi][:], in_=xT_own[:, :, cs]),
                              writes=[("xs", bi)], dsem=xs_sem[bi])

                    def d_front(c):
                        bi = c % 2
                        cs = slice(c * CH, (c + 1) * CH)
                        front("D", xs_b[bi], CH, sq, rstd, tmp, hx, a1, b1, 0, ("xs", bi), "hx")

                    def d_gates(c):
                        bi = c % 2
                        cs = slice(c * CH, (c + 1) * CH)
                        hxk = [("hx", kc) for kc in range(KC)]
                        for oc in range(KC):
                            ob = oc % 2
                            pga, pgb, pya, pyb = 4 * ob, 4 * ob + 1, 4 * ob + 2, 4 * ob + 3
                            for half, pp in ((0, pga), (1, pgb)):
                                col = half * D + oc * 128
                                for kc in range(KC):
                                    S.add("pe", lambda e, kc=kc, pp=pp, col=col: e.matmul(
                                        bank(pp)[:, 0:CH], lhsT=wg_t[:, kc, col:col + 128], rhs=hx[:, kc, :],
                                        start=(kc == 0), stop=(kc == KC - 1)), reads=["w4"] + hxk, writes=[bk(pp)])
                            for kc in range(KC):
                                S.add("pe", lambda e, kc=kc, pya=pya, oc=oc, cs=cs: e.matmul(
                                    bank(pya)[:, 0:CH], lhsT=wao_t[:, kc, oc * 128:(oc + 1) * 128], rhs=QA[:, kc, cs],
                                    start=(kc == 0), stop=(kc == KC - 1)), reads=["w4"], writes=[bk(pya)])
                            for fc in range(4):
                                S.add("pe", lambda e, fc=fc, pyb=pyb, oc=oc, cs=cs: e.matmul(
                                    bank(pyb)[:, 0:CH], lhsT=wco_t[:, fc, oc * 128:(oc + 1) * 128], rhs=ycT[:, fc, cs],
                                    start=(fc == 0), stop=(fc == 3)), reads=["w4"], writes=[bk(pyb)])
                            S.add("act", lambda e, ob=ob, pga=pga: e.activation(out=sa[ob][:], in_=bank(pga)[:, 0:CH], func=AF.Sigmoid),
                                  reads=[bk(pga)], writes=[("sa", ob)])
                            S.add("act", lambda e, ob=ob, pgb=pgb: e.activation(out=sb_[ob][:], in_=bank(pgb)[:, 0:CH], func=AF.Sigmoid),
                                  reads=[bk(pgb)], writes=[("sb", ob)])
                            S.add("dve", lambda e, ob=ob, pya=pya: e.tensor_tensor(out=u1[ob][:], in0=bank(pya)[:, 0:CH], in1=sa[ob][:], op=ALU.mult),
                                  reads=[bk(pya), ("sa", ob)], writes=[("u1", ob)])
                            S.add("dve", lambda e, ob=ob, pyb=pyb: e.tensor_tensor(out=u2[ob][:], in0=bank(pyb)[:, 0:CH], in1=sb_[ob][:], op=ALU.mult),
                                  reads=[bk(pyb), ("sb", ob)], writes=[("u2", ob)])
                            S.add("dve", lambda e, ob=ob, oc=oc: e.tensor_tensor(out=mT[:, oc, :], in0=u1[ob][:], in1=u2[ob][:], op=ALU.add),
                                  reads=[("u1", ob), ("u2", ob)], writes=[("mT", oc)])

                    def d_out(c):
                        bi = c % 2
                        cs = slice(c * CH, (c + 1) * CH)
                        hxk = [("hx", kc) for kc in range(KC)]
                        mk = [("mT", oc) for oc in range(KC)]
                        for oc in range(KC):
                            pm = 2 + (oc % 2)
                            for kc in range(KC):
                                S.add("pe", lambda e, kc=kc, pm=pm, oc=oc: e.matmul(
                                    bank(pm)[:, 0:CH], lhsT=wout_t[:, kc, oc * 128:(oc + 1) * 128], rhs=mT[:, kc, :],
                                    start=(kc == 0), stop=(kc == KC - 1)), reads=["w4"] + mk, writes=[bk(pm)])
                            S.add("dve", lambda e, pm=pm, oc=oc, bi=bi: e.scalar_tensor_tensor(
                                out=xs_b[bi][:, oc, :], in0=bank(pm)[:, 0:CH], scalar=g1(oc), in1=xs_b[bi][:, oc, :], op0=ALU.mult, op1=ALU.add),
                                reads=[bk(pm), ("xs", bi)], writes=[("x1o", bi, oc)])
                        S.add("pool", lambda e, bi=bi, cs=cs: e.dma_start(out=x1scr[:, :, cs], in_=xs_b[bi][:]),
                              reads=[("x1o", bi, oc) for oc in range(KC)] + [("xs", bi)], writes=[("x1scr", c)], dsem=x1o_sem[bi])

                    d_load(0)
                    d_front(0)
                    for c in range(NCH):
                        if c + 1 < NCH:
                            d_load(c + 1)
                        d_gates(c)
                        if c + 1 < NCH:
                            d_front(c + 1)
                        d_out(c)
                    S.phase_end("C45")
        with ExitStack() as sF:
            wup_t = sb(sF, "wup_t", [128, KC, 2 * FF], BF16)
            wdn_t = sb(sF, "wdn_t", [128, NFC, D], BF16)
            fdw_t = sb(sF, "fdw_t", [128, NFC, 3], F32)
            sem_w = newsem()
            for kc in range(KC):
                S.add("pool", lambda e, kc=kc: e.dma_start(out=wup_t[:, kc], in_=wup[:, kc]), writes=["w6"], dsem=sem_w)
            for q in range(2):
                S.add("pool", lambda e, q=q: e.dma_start(out=wdn_t[:, q * 11:(q + 1) * 11], in_=wdn[:, q * 11:(q + 1) * 11]), writes=["w6"], dsem=sem_w)
            S.add("sp", lambda e: e.dma_start(out=fdw_t[:], in_=fdw), writes=["w6"], dsem=sem_w)
            W6 = ["w6"]
            CW = CH + 2
            x1c = [sb(sF, "x1c%d" % i, [128, KC, CW], F32) for i in range(2)]
            x1_sem = [newsem() for _ in range(2)]
            sq = sb(sF, "fsq", [128, KC, CW], BF16)
            tmp = sb(sF, "ftmp", [128, 4, CW], F32)
            rstd = sb(sF, "frstd", [128, CW], F32)
            hx2 = sb(sF, "hx2", [128, KC, CW], BF16)
            aT = [sb(sF, "aT%d" % i, [128, CW], BF16) for i in range(2)]
            fdg = [sb(sF, "fdg%d" % i, [128, 3, 128], BF16) for i in range(2)]
            gl = [sb(sF, "gl%d" % i, [128, CH], F32) for i in range(2)]
            hid = sb(sF, "hid", [128, NFC, CH], BF16)
            fin = sb(sF, "fin", [128, CH], F32)
            for i in range(2):
                S.add("pool", lambda e, i=i: e.memset(aT[i][:], 0.0), writes=[("aT", i)])

            class _V:
                def __init__(self, t, o):
                    self.t, self.o = t, o

                def __getitem__(self, key):
                    a, b, sl = key
                    return self.t[a, b, self.o + sl.start:self.o + sl.stop]

            class _V2:
                def __init__(self, t, o):
                    self.t, self.o = t, o

                def __getitem__(self, key):
                    a, sl = key
                    return self.t[a, self.o + sl.start:self.o + sl.stop]

            def win(c):
                lo = max(c * CH - 1, 0)
                hi = min((c + 1) * CH + 1, E)
                return lo, hi, hi - lo, lo - (c * CH - 1)

            def f_load(c):
                lo, hi, n, o0 = win(c)
                xb = x1c[c % 2]
                S.add("sp", lambda e, lo=lo, hi=hi, o0=o0, n=n, xb=xb: e.dma_start(out=xb[:, :, o0:o0 + n], in_=x1scr[:, :, lo:hi]),
                      writes=[("x1c", c % 2)], dsem=x1_sem[c % 2])

            def f_front(c):
                lo, hi, n, o0 = win(c)
                front("F", _V(x1c[c % 2], o0), n, _V(sq, o0), _V2(rstd, o0), _V(tmp, o0), _V(hx2, o0), a2, b2, 0, ("x1c", c % 2), "hx2")

            def f_up(c):
                lo, hi, n, o0 = win(c)
                hxk = [("hx2", kc) for kc in range(KC)]
                for fc in range(NFC):
                    fb = fc % 2
                    pa, pb_, pcv = 2 + 3 * fb, 3 + 3 * fb, 4 + 3 * fb
                    for j in range(3):
                        S.add("act", lambda e, fc=fc, j=j, fb=fb: e.activation(out=fdg[fb][:, j, :], in_=ident_bf[:], func=AF.Identity,
                                                                             scale=fdw_t[:, fc, j:j + 1]),
                              reads=W6, writes=[("fdg", fb)])
                    for kc in range(KC):
                        S.add("pe", lambda e, kc=kc, pa=pa, fc=fc, o0=o0, n=n: e.matmul(
                            bank(pa)[:, 0:n], lhsT=wup_t[:, kc, fc * 128:(fc + 1) * 128], rhs=hx2[:, kc, o0:o0 + n],
                            start=(kc == 0), stop=(kc == KC - 1)), reads=W6 + hxk, writes=[bk(pa)])
                    S.add("act", lambda e, fb=fb, pa=pa, o0=o0, n=n: e.activation(
                        out=aT[fb][:, o0:o0 + n], in_=bank(pa)[:, 0:n], func=AF.Identity),
                        reads=[bk(pa)], writes=[("aT", fb)])
                    if c == 0 or c == NCH - 1:
                        w0 = (o0 + 0) if c == 0 else (E - HALO - lo + o0)
                        m0 = 0 if c == 0 else HALO
                        S.add("dve", lambda e, fb=fb, w0=w0, m0=m0: e.tensor_tensor(
                            out=aT[fb][:, w0:w0 + HALO], in0=aT[fb][:, w0:w0 + HALO], in1=mask_t[:, m0:m0 + HALO], op=ALU.mult),
                            reads=[("aT", fb)], writes=[("aT", fb)])
                    for kc in range(KC):
                        S.add("pe", lambda e, kc=kc, pb_=pb_, fc=fc: e.matmul(
                            bank(pb_)[:, 0:CH], lhsT=wup_t[:, kc, FF + fc * 128:FF + (fc + 1) * 128], rhs=hx2[:, kc, 1:1 + CH],
                            start=(kc == 0), stop=(kc == KC - 1)), reads=W6 + hxk, writes=[bk(pb_)])
                    for j in range(3):
                        S.add("pe", lambda e, j=j, pcv=pcv, fb=fb: e.matmul(
                            bank(pcv)[:, 0:CH], lhsT=fdg[fb][:, j, :], rhs=aT[fb][:, j:j + CH],
                            start=(j == 0), stop=(j == 2)), reads=[("fdg", fb), ("aT", fb)], writes=[bk(pcv)])
                    S.add("act", lambda e, fb=fb, pcv=pcv, fc=fc: e.activation(out=gl[fb][:], in_=bank(pcv)[:, 0:CH], func=AF.Gelu_apprx_tanh,
                                                                              bias=fdb_t[:, fc:fc + 1], scale=1.0),
                          reads=[bk(pcv)], writes=[("gl", fb)])
                    S.add("dve", lambda e, fb=fb, pb_=pb_, fc=fc: e.tensor_tensor(out=hid[:, fc, :], in0=bank(pb_)[:, 0:CH], in1=gl[fb][:], op=ALU.mult),
                          reads=[bk(pb_), ("gl", fb)], writes=[("hid", fc)])

            def f_down(c):
                xb = x1c[c % 2]
                xk = ("x1c", c % 2)
                hk = [("hid", fc) for fc in range(NFC)]
                for oc in range(KC):
                    pd = oc % 2
                    for fc in range(NFC):
                        S.add("pe", lambda e, fc=fc, pd=pd, oc=oc: e.matmul(
                            bank(pd)[:, 0:CH], lhsT=wdn_t[:, fc, oc * 128:(oc + 1) * 128], rhs=hid[:, fc, :],
                            start=(fc == 0), stop=(fc == NFC - 1)), reads=W6 + hk, writes=[bk(pd)])
                    S.add("dve", lambda e, pd=pd, oc=oc, xb=xb: e.scalar_tensor_tensor(
                        out=xb[:, oc, 1:1 + CH], in0=bank(pd)[:, 0:CH], scalar=g2(oc), in1=xb[:, oc, 1:1 + CH], op0=ALU.mult, op1=ALU.add),
                        reads=[bk(pd), xk], writes=[("x2", c % 2, oc)])
                x2k = [("x2", c % 2, oc) for oc in range(KC)]
                S.add("act", lambda e, xb=xb: e.activation(out=sq[:, :, 1:1 + CH], in_=xb[:, :, 1:1 + CH], func=AF.Square),
                      reads=x2k, writes=["Fsq"])
                for kc in range(KC):
                    S.add("pe", lambda e, kc=kc: e.matmul(bank(2)[:, 0:CH], lhsT=ones_bf[:], rhs=sq[:, kc, 1:1 + CH],
                                                         start=(kc == 0), stop=(kc == KC - 1)), reads=["Fsq"], writes=[bk(2)])
                rsqrt_act(fin[:], bank(2)[:, 0:CH], 1.0 / D, [bk(2)], "fin")
                for oc in range(KC):
                    S.add("dve", lambda e, oc=oc, xb=xb: e.scalar_tensor_tensor(
                        out=xb[:, oc, 1:1 + CH], in0=xb[:, oc, 1:1 + CH], scalar=vec8_t[:, 2, oc:oc + 1], in1=fin[:], op0=ALU.mult, op1=ALU.mult),
                        reads=[("x2", c % 2, oc), "fin"], writes=[("x2", c % 2, oc)])
                elo = max(c * CH, HALO)
                ehi = min((c + 1) * CH, HALO + OWN)
                S.add("sp", lambda e, c=c, elo=elo, ehi=ehi, xb=xb: e.dma_start(
                    out=yT[:, :, elo - HALO:ehi - HALO], in_=xb[:, :, 1 + elo - c * CH:1 + ehi - c * CH]),
                    reads=x2k + [xk], dsem=sem_out)

            f_load(0)
            f_front(0)
            for c in range(NCH):
                if c + 1 < NCH:
                    f_load(c + 1)
                f_up(c)
                if c + 1 < NCH:
                    f_front(c + 1)
                f_down(c)

        stats = S.finalize(nc, engsem, final_sems=[sem_out])
    return nc, stats


def _fm(w):
    K, N = w.shape
    return np.ascontiguousarray(w.reshape(K // 128, 128, N).transpose(1, 0, 2))


def _pv(v):
    return np.ascontiguousarray(v.reshape(-1, 128).T)


def _rope_tables():
    half = 64
    inv_freq = (10000.0 ** (-np.arange(0, half, 2, dtype=np.float32) / half)).astype(np.float32)
    t = np.arange(L)
    r = (t // 64).astype(np.float32)
    col = (t % 64).astype(np.float32)
    ang = np.concatenate([r[:, None] * inv_freq, col[:, None] * inv_freq], axis=-1).astype(np.float32)
    cos = np.cos(ang).astype(np.float32)
    sin = np.sin(ang).astype(np.float32)
    d = np.arange(128)
    C = cos[:, d // 2].T
    sgn = np.where(d % 2 == 0, -1.0, 1.0).astype(np.float32)
    Sg = (sin[:, d // 2] * sgn[None, :]).T
    return np.ascontiguousarray(C), np.ascontiguousarray(Sg)


_CACHE = {}


def _prep(x, c, ctx, c_ctx, w_mod, b_mod, norm1_g, w_in, q_norm_g, k_norm_g, w_attn_out,
          conv_dw_w, conv_dw_b, conv_ln_g, conv_ln_b, w_conv_out, w_out, norm2_g,
          w_up, ffn_dw_w, ffn_dw_b, w_down, final_g):
    f = lambda a: np.asarray(a, dtype=np.float32)
    x, c, ctx, c_ctx = f(x), f(c), f(ctx), f(c_ctx)
    xT = np.ascontiguousarray(x[0].T)
    xT_all = np.ascontiguousarray(xT.reshape(KC, 128, L // 512, 512).transpose(2, 1, 0, 3).reshape(L // 512, 128, KC * 512))
    ctxT = np.ascontiguousarray(f(ctx)[0].T.reshape(KC, 128, 256).transpose(1, 0, 2))
    C, Sg = _rope_tables()
    ropeK = np.zeros((128, 2, L + 512), np.float32)
    ropeK[:, 0, :L] = C
    ropeK[:, 1, :L] = Sg
    ropeK[:, 0, L:] = 1.0
    ropeK = np.ascontiguousarray(ropeK.reshape(128, 2, L // 512 + 1, 512).transpose(2, 0, 1, 3).reshape(L // 512 + 1, 128, 1024))
    w_in0 = f(w_in)[0]
    sw = np.arange(D).reshape(-1, 2)[:, ::-1].reshape(-1)
    wq = w_in0[:, 0:D]
    wk = w_in0[:, D:D + 256]
    swk = np.arange(256).reshape(-1, 2)[:, ::-1].reshape(-1)
    wq2 = np.ascontiguousarray(np.stack([_fm(wq), _fm(wq[:, sw])], axis=2))
    wk2 = np.ascontiguousarray(np.stack([_fm(wk), _fm(wk[:, swk])], axis=2))
    sw128 = np.arange(128).reshape(-1, 2)[:, ::-1].reshape(-1)
    gq = f(q_norm_g)[0]
    gk = f(k_norm_g)[0]
    gqk = np.ascontiguousarray(np.stack([gq, gq[sw128], gk, gk[sw128]], axis=1))
    common = {
        "xT_all": xT_all, "ctxT": ctxT, "ropeK": ropeK,
        "ccin": np.ascontiguousarray(np.stack([_pv(c[0]), _pv(c_ctx)], axis=2)),
        "wmod": _fm(f(w_mod)[0]), "bmod": _pv(f(b_mod)[0]),
        "vec8": np.ascontiguousarray(np.stack([_pv(f(norm1_g)[0]), _pv(f(norm2_g)[0]), _pv(f(final_g))], axis=1)),
        "gqk": gqk, "gqk_row": np.ascontiguousarray(np.broadcast_to(np.concatenate([gq, gk])[None, :], (128, 256))),
        "wq2": wq2, "wk2": wk2, "wv": _fm(w_in0[:, D + 256:D + 512]),
        "wu": _fm(w_in0[:, D + 512:2 * D + 512]), "wg": _fm(w_in0[:, 2 * D + 512:]),
        "wao": _fm(f(w_attn_out)[0]), "wco": _fm(f(w_conv_out)[0]), "wout": _fm(f(w_out)[0]),
        "wup": _fm(f(w_up)[0]), "wdn": _fm(f(w_down)[0]),
        "cdw": np.ascontiguousarray(f(conv_dw_w)[0].T.reshape(4, 128, 31).transpose(1, 0, 2)),
        "cvec": np.ascontiguousarray(np.stack([_pv(f(conv_dw_b)[0]), _pv(f(conv_ln_g)[0]), _pv(f(conv_ln_b)[0])], axis=1)),
        "fdw": np.ascontiguousarray(f(ffn_dw_w)[0].T.reshape(NFC, 128, 3).transpose(1, 0, 2)),
        "fdb": _pv(f(ffn_dw_b)[0]),
        "ident": np.eye(128, dtype=np.float32),
    }
    in_maps = []
    for i in range(NCORES):
        t0 = i * OWN - HALO
        idx = np.arange(t0, t0 + E)
        valid = (idx >= 0) & (idx < L)
        idc = np.clip(idx, 0, L - 1)
        xo = xT[:, idc] * 0 if False else xT[:, idc].copy()
        xo[:, ~valid] = 0.0
        m = dict(common)
        m["xT_own"] = np.ascontiguousarray(xo.reshape(KC, 128, E).transpose(1, 0, 2))
        rq = np.zeros((128, 2, E), np.float32)
        rq[:, 0, :] = C[:, idc]
        rq[:, 1, :] = Sg[:, idc]
        m["ropeQ"] = rq
        vh = np.concatenate([valid[:HALO], valid[E - HALO:]]).astype(np.float32)
        m["maskE"] = np.ascontiguousarray(np.broadcast_to(vh[None, :], (128, 2 * HALO)))
        in_maps.append(m)
    return in_maps


def kernel(**inputs):
    in_maps = _prep(**inputs)
    if "main" not in _CACHE:
        _CACHE["main"] = build_program()[0]
    nc = _CACHE["main"]
    res = run_bass_kernel_spmd(nc, in_maps, core_ids=list(range(NCORES)))
    out = np.empty((1, L, D), np.float32)
    for i in range(NCORES):
        y = res.results[i]["yT"]
        out[0, i * OWN:(i + 1) * OWN, :] = y.transpose(2, 1, 0).reshape(OWN, D)
    return out
```

```python
import numpy as np
import concourse.bass as bass
import concourse.mybir as mybir
from concourse.bass_utils import run_bass_kernel_spmd

F32 = mybir.dt.float32
BF16 = mybir.dt.bfloat16
ALU = mybir.AluOpType
AF = mybir.ActivationFunctionType
AX = mybir.AxisListType

NCORES = 8
L = 16384
D = 1024
KC = 8
OWN = 2048
HALO = 16
E = OWN + 2 * HALO
CH = 416
NCH = E // CH
NKEY = L + 256
NKT = NKEY // 128
FF = 2816
NFC = FF // 128
EPS = 1e-6

ENGS = ("pe", "act", "dve", "pool", "sp")


class Op:
    __slots__ = ("eng", "fn", "deps", "sig", "sigval", "dsem", "dval", "idx")


class Sched:
    def __init__(self):
        self.ops = {e: [] for e in ENGS}
        self.lastw = {}
        self.readers = {}
        self.dma_count = {}
        self.n = 0
        self.pending_barrier = {}
        self.live = True
        self.stop_after = None

    def phase_end(self, name):
        self.barrier()
        if self.stop_after == name:
            self.live = False

    def add(self, eng, fn, reads=(), writes=(), dsem=None):
        if not self.live:
            return None
        op = Op()
        op.eng = eng
        op.fn = fn
        op.dsem = dsem
        op.sig = False
        op.sigval = 0
        op.dval = 0
        op.idx = self.n
        self.n += 1
        deps = {}
        is_dma = dsem is not None

        def need(d, kind):
            if d is op:
                return
            d_dma = d.dsem is not None
            if not is_dma and not d_dma and d.eng == eng:
                if eng == "pe":
                    return
                if kind == "war":
                    return
            deps[d.idx] = d

        for k in reads:
            w = self.lastw.get(k)
            if w is not None:
                need(w, "raw")
            if isinstance(k, tuple) and k[0] == "ps":
                for r in self.readers.get(k, ()):
                    if r.eng != eng:
                        need(r, "rar")
        for k in writes:
            w = self.lastw.get(k)
            if w is not None:
                need(w, "waw")
            for r in self.readers.get(k, ()):
                need(r, "war")
        bar = self.pending_barrier.pop(eng, None)
        if bar is not None:
            for d in bar:
                if d is not op and not (d.eng == eng and d.dsem is None and not is_dma):
                    deps[d.idx] = d
        op.deps = list(deps.values())
        for k in writes:
            self.lastw[k] = op
            self.readers[k] = []
        for k in reads:
            self.readers.setdefault(k, []).append(op)
        if is_dma:
            c = self.dma_count.get(dsem, 0) + 1
            self.dma_count[dsem] = c
            op.dval = 16 * c
        self.ops[eng].append(op)
        return op

    def barrier(self):
        if not self.live:
            return
        lst = []
        for e in ENGS:
            last_c = None
            last_d = {}
            for op in self.ops[e]:
                if op.dsem is None:
                    last_c = op
                else:
                    last_d[id(op.dsem)] = op
            if last_c is not None:
                lst.append(last_c)
            lst.extend(last_d.values())
        for e in ENGS:
            prev = self.pending_barrier.get(e, [])
            self.pending_barrier[e] = prev + lst
        self.lastw = {}
        self.readers = {}

    def finalize(self, nc, engsem, final_sems=()):
        for e in ENGS:
            for op in self.ops[e]:
                for d in op.deps:
                    if d.dsem is None:
                        d.sig = True
        for e in ENGS:
            c = 0
            for op in self.ops[e]:
                if op.dsem is None and op.sig:
                    c += 1
                    op.sigval = c
        stats = {}

        def run(e, engine):
            waited = {}
            nw = 0
            for op in self.ops[e]:
                want = {}
                for d in op.deps:
                    if d.dsem is not None:
                        s, v = d.dsem, d.dval
                    else:
                        s, v = engsem[d.eng], d.sigval
                    key = id(s)
                    if key not in want or want[key][1] < v:
                        want[key] = (s, v)
                for key, (s, v) in want.items():
                    if waited.get(key, 0) >= v:
                        continue
                    engine.wait_ge(s, v)
                    waited[key] = v
                    nw += 1
                ins = op.fn(engine)
                if op.dsem is not None:
                    ins.then_inc(op.dsem, 16)
                elif op.sig:
                    ins.then_inc(engsem[e], 1)
            stats[e] = (len(self.ops[e]), nw)

        with nc.Block() as block:

            @block.tensor
            def _(eng):
                run("pe", eng)

            @block.scalar
            def _(eng):
                run("act", eng)

            @block.vector
            def _(eng):
                run("dve", eng)

            @block.gpsimd
            def _(eng):
                run("pool", eng)

            @block.sync
            def _(eng):
                run("sp", eng)
                for s in final_sems:
                    if s in self.dma_count:
                        eng.wait_ge(s, 16 * self.dma_count[s])

        return stats


import os
SKIP = os.environ.get('KSKIP', '').split(',')
SCRKIND = os.environ.get('KSCR', 'Internal')


def build_program(debug=False, stop_after=None):
    from contextlib import ExitStack

    nc = bass.Bass("TRN2", target_bir_lowering=False)
    S = Sched()
    S.stop_after = stop_after

    def din(name, shape):
        return nc.dram_tensor(name, list(shape), F32, kind="ExternalInput").ap()

    xT_all = din("xT_all", [L // 512, 128, KC * 512])
    ctxT = din("ctxT", [128, KC, 256])
    xT_own = din("xT_own", [128, KC, E])
    ropeK = din("ropeK", [L // 512 + 1, 128, 2 * 512])
    ropeQ = din("ropeQ", [128, 2, E])
    maskE = din("maskE", [128, 2 * HALO])
    ccin = din("ccin", [128, KC, 2])
    wmod = din("wmod", [128, KC, 6 * D])
    bmod = din("bmod", [128, 48])
    vec8 = din("vec8", [128, 3, KC])
    gqk = din("gqk", [128, 4])
    gqk_row = din("gqk_row", [128, 256])
    wq2 = din("wq2", [128, KC, 2, D])
    wk2 = din("wk2", [128, KC, 2, 256])
    wv = din("wv", [128, KC, 256])
    wu = din("wu", [128, KC, D])
    wg = din("wg", [128, KC, 2 * D])
    wao = din("wao", [128, KC, D])
    wco = din("wco", [128, 4, D])
    wout = din("wout", [128, KC, D])
    wup = din("wup", [128, KC, 2 * FF])
    wdn = din("wdn", [128, NFC, D])
    cdw = din("cdw", [128, 4, 31])
    cvec = din("cvec", [128, 3, 4])
    fdw = din("fdw", [128, NFC, 3])
    fdb = din("fdb", [128, NFC])
    ident_in = din("ident", [128, 128])

    yT = nc.dram_tensor("yT", [128, KC, OWN], F32, kind="ExternalOutput").ap()
    kscr = [nc.dram_tensor("kscr%d" % g, [128, NKEY], BF16, kind=SCRKIND).ap() for g in range(2)]
    vscr = nc.dram_tensor("vscr", [128, NKT, 256], BF16, kind=SCRKIND).ap()
    x1scr = nc.dram_tensor("x1scr", [128, KC, E], F32, kind=SCRKIND).ap()
    dbg = {}
    if debug:
        dbg["qT"] = nc.dram_tensor("dbg_qT", [128, KC, E], F32, kind="ExternalOutput").ap()
        dbg["k0"] = nc.dram_tensor("dbg_k0", [128, NKEY], F32, kind="ExternalOutput").ap()
        dbg["v0"] = nc.dram_tensor("dbg_v0", [128, NKT, 128], F32, kind="ExternalOutput").ap()
        dbg["at"] = nc.dram_tensor("dbg_at", [128, KC, E], F32, kind="ExternalOutput").ap()
        dbg["mod"] = nc.dram_tensor("dbg_mod", [128, 48, 2], F32, kind="ExternalOutput").ap()
        dbg["hT"] = nc.dram_tensor("dbg_hT", [128, 4, E], F32, kind="ExternalOutput").ap()
        dbg["yc"] = nc.dram_tensor("dbg_yc", [128, 4, E], F32, kind="ExternalOutput").ap()

    es = ExitStack()
    sem_n = [0]

    def newsem(st=None):
        sem_n[0] += 1
        return es.enter_context(nc.semaphore("sm%d" % sem_n[0]))

    def sb(st, name, shape, dt):
        return st.enter_context(nc.sbuf_tensor(name, list(shape), dt))

    with es:
        engsem = {e: newsem() for e in ("pe", "act", "dve", "pool")}
        sem_out = newsem()
        sem_scr = newsem()
        sem_const = newsem()

        PS = [es.enter_context(nc.psum_tensor("ps%d" % i, [128, 2, 512], F32)) for i in range(4)]

        def bank(b):
            return PS[b // 2][:, b % 2, :]

        def bk(b):
            return ("ps", b)

        ones_bf = sb(es, "ones_bf", [128, 128], BF16)
        ident_bf = sb(es, "ident_bf", [128, 128], BF16)
        eps_t = sb(es, "eps_t", [128, 1], F32)
        modsb = sb(es, "modsb", [128, 48, 2], F32)
        bmod_t = sb(es, "bmod_t", [128, 48], F32)
        vec8_t = sb(es, "vec8_t", [128, 3, KC], F32)
        a1 = sb(es, "a1", [128, KC], F32)
        a1c = sb(es, "a1c", [128, KC], F32)
        a2 = sb(es, "a2", [128, KC], F32)
        gqk_t = sb(es, "gqk_t", [128, 4], F32)
        nbias = sb(es, "nbias", [128, 1], F32)
        cvec_t = sb(es, "cvec_t", [128, 3, 4], F32)
        fdb_t = sb(es, "fdb_t", [128, NFC], F32)
        mask_t = sb(es, "mask_t", [128, 2 * HALO], F32)

        def b1(kc):
            return modsb[:, 0 + kc, 0:1]

        def b1c(kc):
            return modsb[:, 0 + kc, 1:2]

        def g1(kc):
            return modsb[:, 16 + kc, 0:1]

        def b2(kc):
            return modsb[:, 24 + kc, 0:1]

        def g2(kc):
            return modsb[:, 40 + kc, 0:1]

        S.add("pool", lambda e: e.memset(ones_bf[:], 1.0), writes=["ones"])
        S.add("pool", lambda e: e.memset(eps_t[:], EPS), writes=["eps"])
        S.add("pool", lambda e: e.dma_start(out=ident_bf[:], in_=ident_in), writes=["ident"], dsem=sem_scr)
        S.add("sp", lambda e: e.dma_start(out=bmod_t[:], in_=bmod), writes=["bmod"], dsem=sem_const)
        S.add("sp", lambda e: e.dma_start(out=vec8_t[:], in_=vec8), writes=["vec8"], dsem=sem_const)
        S.add("sp", lambda e: e.dma_start(out=gqk_t[:], in_=gqk), writes=["gqk"], dsem=sem_const)
        S.add("sp", lambda e: e.dma_start(out=cvec_t[:], in_=cvec), writes=["cvec"], dsem=sem_const)
        S.add("sp", lambda e: e.dma_start(out=fdb_t[:], in_=fdb), writes=["fdb"], dsem=sem_const)
        S.add("sp", lambda e: e.dma_start(out=mask_t[:], in_=maskE), writes=["mask"], dsem=sem_const)
        CONSTK = ["ident", "bmod", "vec8", "gqk", "cvec", "fdb", "mask"]

        with ExitStack() as p0:
            cc_t = sb(p0, "cc_t", [128, KC, 2], F32)
            sc_t = sb(p0, "sc_t", [128, KC, 2], F32)
            wm = [sb(p0, "wm%d" % i, [128, KC, D], F32) for i in range(2)]
            wm_sem = [newsem() for _ in range(2)]
            grow = sb(p0, "grow", [128, 256], F32)
            gmax = sb(p0, "gmax", [128, 2], F32)
            gprod = sb(p0, "gprod", [1, 1], F32)
            ones_f = sb(p0, "ones_f", [1, 128], F32)
            S.add("sp", lambda e: e.dma_start(out=cc_t[:], in_=ccin), writes=["cc"], dsem=sem_const)
            S.add("sp", lambda e: e.dma_start(out=grow[:], in_=gqk_row), writes=["grow"], dsem=sem_const)
            ALLC = CONSTK + ["cc", "grow"]
            S.add("act", lambda e: e.activation(out=sc_t[:], in_=cc_t[:], func=AF.Silu), reads=ALLC, writes=["sc"])
            mps = bank(0)
            mps3 = PS[0][:, 0, 0:96].rearrange("p (a b) -> p a b", b=2)
            for v in range(6):
                bi = v % 2
                S.add("sp", lambda e, v=v, bi=bi: e.dma_start(out=wm[bi][:], in_=wmod[:, :, v * D:(v + 1) * D]),
                      writes=[("wm", bi)], dsem=wm_sem[bi])
                for fcol in range(8):
                    for kc in range(KC):
                        S.add("pe", lambda e, v=v, bi=bi, fcol=fcol, kc=kc: e.matmul(
                            mps3[:, v * 8 + fcol, :], lhsT=wm[bi][:, kc, fcol * 128:(fcol + 1) * 128],
                            rhs=sc_t[:, kc, :], start=(kc == 0), stop=(kc == KC - 1)),
                            reads=[("wm", bi), "sc"], writes=[bk(0)])
            for i in range(2):
                S.add("dve", lambda e, i=i: e.tensor_tensor(out=modsb[:, :, i], in0=mps3[:, :, i], in1=bmod_t[:], op=ALU.add),
                      reads=[bk(0)] + ALLC, writes=["modsb"])
            S.add("dve", lambda e: e.scalar_tensor_tensor(out=a1[:], in0=modsb[:, 8:16, 0], scalar=1.0, in1=vec8_t[:, 0, :],
                                                          op0=ALU.add, op1=ALU.mult), reads=["modsb"] + ALLC, writes=["a1"])
            S.add("dve", lambda e: e.scalar_tensor_tensor(out=a1c[:], in0=modsb[:, 8:16, 1], scalar=1.0, in1=vec8_t[:, 0, :],
                                                          op0=ALU.add, op1=ALU.mult), reads=["modsb"] + ALLC, writes=["a1c"])
            S.add("dve", lambda e: e.scalar_tensor_tensor(out=a2[:], in0=modsb[:, 32:40, 0], scalar=1.0, in1=vec8_t[:, 1, :],
                                                          op0=ALU.add, op1=ALU.mult), reads=["modsb"] + ALLC, writes=["a2"])
            import os
            if os.environ.get('KSKIP') == 'shift':
                S.add('pool', lambda e: e.memset(nbias[:], -12.0), writes=['nbias'])
            else:
                S.add("act", lambda e: e.activation(out=grow[:], in_=grow[:], func=AF.Abs),
                      reads=ALLC, writes=["grow"])
                S.add("dve", lambda e: e.tensor_reduce(out=gmax[:], in_=grow[:].rearrange("p (a b) -> p a b", a=2), axis=AX.X,
                                                       op=ALU.max), reads=["grow"], writes=["gmax"])
                S.add("dve", lambda e: e.scalar_tensor_tensor(out=nbias[:], in0=gmax[:, 0:1], scalar=-(128.0 ** 0.5), in1=gmax[:, 1:2],
                                                              op0=ALU.mult, op1=ALU.mult), reads=["gmax"], writes=["nbias"])
            if debug:
                S.add("sp", lambda e: e.dma_start(out=dbg["mod"], in_=modsb[:]), reads=["modsb"], dsem=sem_out)
            S.phase_end("P0")

        MODK = []

        def rsqrt_act(out_ap, in_ap, scl, rd, key):
            S.add("act", lambda e: e.activation(out=out_ap, in_=in_ap, func=AF.Ln, bias=eps_t[:], scale=scl),
                  reads=rd, writes=[key])
            S.add("act", lambda e: e.activation(out=out_ap, in_=out_ap, func=AF.Exp, scale=-0.5),
                  reads=[key], writes=[key])

        def front(tag, xs, n, sq, rstd, tmp, hx, avec, bfun, psb, xs_key, hx_key, part="ab", stag=None):
            stag = stag or tag
            if "a" in part:
                S.add("act", lambda e: e.activation(out=sq[:, :, 0:n], in_=xs[:, :, 0:n], func=AF.Square),
                      reads=[xs_key], writes=[stag + "sq"])
                for kc in range(KC):
                    S.add("pe", lambda e, kc=kc: e.matmul(bank(psb)[:, 0:n], lhsT=ones_bf[:], rhs=sq[:, kc, 0:n],
                                                         start=(kc == 0), stop=(kc == KC - 1)),
                          reads=[stag + "sq"], writes=[bk(psb)])
                rsqrt_act(rstd[:, 0:n], bank(psb)[:, 0:n], 1.0 / D, [bk(psb)], stag + "rstd")
            if "b" not in part:
                return
            for kc in range(KC):
                tb = kc % 4
                S.add("dve", lambda e, kc=kc, tb=tb: e.scalar_tensor_tensor(out=tmp[:, tb, 0:n], in0=xs[:, kc, 0:n],
                                                                              scalar=avec[:, kc:kc + 1], in1=rstd[:, 0:n],
                                                                              op0=ALU.mult, op1=ALU.mult),
                      reads=[xs_key, stag + "rstd"], writes=[(tag + "tmp", tb)])
                if kc % 2 == 0:
                    S.add("dve", lambda e, kc=kc, tb=tb: e.tensor_scalar(out=hx[:, kc, 0:n], in0=tmp[:, tb, 0:n], scalar1=bfun(kc),
                                                                          scalar2=None, op0=ALU.add),
                          reads=[(tag + "tmp", tb)], writes=[(hx_key, kc)])
                else:
                    S.add("act", lambda e, kc=kc, tb=tb: e.activation(out=hx[:, kc, 0:n], in_=tmp[:, tb, 0:n], func=AF.Identity,
                                                                       bias=bfun(kc), scale=1.0),
                          reads=[(tag + "tmp", tb)], writes=[(hx_key, kc)])

        def head_norm_rope(tag, pA, pB, n, gcol, rope_t, rope_key, sqh, rsth, t1, t2, out_ap, out_key, pss):
            S.add("act", lambda e: e.activation(out=sqh[:, 0:n], in_=bank(pA)[:, 0:n], func=AF.Square),
                  reads=[bk(pA)], writes=[tag + "sqh"])
            S.add("pe", lambda e: e.matmul(bank(pss)[:, 0:n], lhsT=ones_bf[:], rhs=sqh[:, 0:n], start=True, stop=True),
                  reads=[tag + "sqh"], writes=[bk(pss)])
            rsqrt_act(rsth[:, 0:n], bank(pss)[:, 0:n], 1.0 / 128, [bk(pss)], tag + "rsth")
            S.add("dve", lambda e: e.scalar_tensor_tensor(out=t1[:, 0:n], in0=bank(pA)[:, 0:n], scalar=gqk_t[:, gcol:gcol + 1],
                                                          in1=rope_t[:, 0, 0:n], op0=ALU.mult, op1=ALU.mult),
                  reads=[bk(pA), rope_key], writes=[tag + "t1"])
            S.add("dve", lambda e: e.scalar_tensor_tensor(out=t2[:, 0:n], in0=bank(pB)[:, 0:n], scalar=gqk_t[:, gcol + 1:gcol + 2],
                                                          in1=rope_t[:, 1, 0:n], op0=ALU.mult, op1=ALU.mult),
                  reads=[bk(pB), rope_key], writes=[tag + "t2"])
            S.add("dve", lambda e: e.tensor_tensor(out=t1[:, 0:n], in0=t1[:, 0:n], in1=t2[:, 0:n], op=ALU.add),
                  reads=[tag + "t1", tag + "t2"], writes=[tag + "t1"])
            S.add("dve", lambda e: e.tensor_tensor(out=out_ap, in0=t1[:, 0:n], in1=rsth[:, 0:n], op=ALU.mult),
                  reads=[tag + "t1", tag + "rsth"], writes=[out_key])

        with ExitStack() as sQA:
            QA = sb(sQA, "QA", [128, KC, E], BF16)

            def alloc_front(st, pfx, W):
                d = {}
                d["xs"] = [sb(st, pfx + "xs%d" % i, [128, KC, W], F32) for i in range(2)]
                d["xs_sem"] = [newsem() for _ in range(2)]
                d["rp"] = [sb(st, pfx + "rp%d" % i, [128, 2, W], F32) for i in range(2)]
                d["rp_sem"] = [newsem() for _ in range(2)]
                d["sq"] = sb(st, pfx + "sq", [128, KC, W], BF16)
                d["tmp"] = sb(st, pfx + "tmp", [128, 4, W], F32)
                d["rstd"] = sb(st, pfx + "rstd", [128, W], F32)
                d["hx"] = [sb(st, pfx + "hx%d" % i, [128, KC, W], BF16) for i in range(2)]
                d["sqh"] = [sb(st, pfx + "sqh%d" % i, [128, W], BF16) for i in range(2)]
                d["rsth"] = [sb(st, pfx + "rsth%d" % i, [128, W], F32) for i in range(2)]
                d["t1"] = [sb(st, pfx + "t1%d" % i, [128, W], F32) for i in range(2)]
                d["t2"] = [sb(st, pfx + "t2%d" % i, [128, W], F32) for i in range(2)]
                return d

            with ExitStack() as sA:
                wq_t = sb(sA, "wq_t", [128, KC, 2, D], BF16)
                sem_w = newsem()
                for kc in range(KC):
                    S.add("pool", lambda e, kc=kc: e.dma_start(out=wq_t[:, kc], in_=wq2[:, kc]), writes=["wq"], dsem=sem_w)
                WA = ["wq"]
                fa = alloc_front(sA, "a", CH)
                def a1_front(c):
                    bi = c % 2
                    xs_, rp_, hx_ = fa["xs"][bi], fa["rp"][bi], fa["hx"][bi]
                    S.add("sp", lambda e, c=c, xs_=xs_: e.dma_start(out=xs_[:], in_=xT_own[:, :, c * CH:(c + 1) * CH]),
                          writes=[("xs", bi)], dsem=fa["xs_sem"][bi])
                    S.add("sp", lambda e, c=c, rp_=rp_: e.dma_start(out=rp_[:], in_=ropeQ[:, :, c * CH:(c + 1) * CH]),
                          writes=[("rp", bi)], dsem=fa["rp_sem"][bi])
                    front("A", xs_, CH, fa["sq"], fa["rstd"], fa["tmp"], hx_, a1, b1, 0, ("xs", bi), ("hx", bi))

                def a1_heads(c):
                    bi = c % 2
                    n = CH
                    rp_, hx_ = fa["rp"][bi], fa["hx"][bi]
                    hxk = [(("hx", bi), kc) for kc in range(KC)]
                    for h in range(8):
                        hb = h % 2
                        pA, pB, pss = 2 + 2 * hb, 3 + 2 * hb, 1
                        for ver, pb_ in ((0, pA), (1, pB)):
                            for kc in range(KC):
                                S.add("pe", lambda e, kc=kc, ver=ver, pb_=pb_, h=h, hx_=hx_, n=n: e.matmul(
                                    bank(pb_)[:, 0:n], lhsT=wq_t[:, kc, ver, h * 128:(h + 1) * 128], rhs=hx_[:, kc, 0:n],
                                    start=(kc == 0), stop=(kc == KC - 1)), reads=WA + hxk, writes=[bk(pb_)])
                        head_norm_rope("q%d" % hb, pA, pB, n, 0, rp_, ("rp", bi), fa["sqh"][hb], fa["rsth"][hb],
                                       fa["t1"][hb], fa["t2"][hb], QA[:, h, c * CH:(c + 1) * CH], ("QA", h, c), pss)

                a1_front(0)
                for c in range(NCH):
                    if c + 1 < NCH:
                        a1_front(c + 1)
                    a1_heads(c)
                S.phase_end("A1")
            with ExitStack() as sA:
                wk_t = sb(sA, "wk_t", [128, KC, 2, 256], BF16)
                wv_t = sb(sA, "wv_t", [128, KC, 256], BF16)
                sem_w = newsem()
                S.add("pool", lambda e: e.dma_start(out=wk_t[:], in_=wk2), writes=["wk"], dsem=sem_w)
                S.add("pool", lambda e: e.dma_start(out=wv_t[:], in_=wv), writes=["wv"], dsem=sem_w)
                WA = ["wk", "wv"]
                fa = alloc_front(sA, "b", 512)
                fa["xs"].append(sb(sA, "bxs2", [128, KC, 512], F32))
                fa["xs_sem"].append(newsem())
                fa["sq2"] = [fa["sq"], sb(sA, "bsq2", [128, KC, 512], BF16)]
                fa["rstd2"] = [fa["rstd"], sb(sA, "brstd2", [128, 512], F32)]
                kst = [[sb(sA, "kst%d_%d" % (g, i), [128, 512], BF16) for i in range(2)] for g in range(2)]
                vst = [sb(sA, "vst%d" % i, [128, 4, 256], BF16) for i in range(2)]
                kst_sem = [[newsem() for i in range(2)] for g in range(2)]
                vst_sem = [[newsem() for i in range(2)] for g in range(2)]
                NJ = L // 512 + 1
                JL = [int(v) for v in os.environ['KJ'].split(',')] if os.environ.get('KJ') else list(range(NJ))
                def a2_front(j):
                    xi = j % 3
                    bi = j % 2
                    isctx = j == NJ - 1
                    n = 256 if isctx else 512
                    xs_ = fa["xs"][xi]
                    if isctx:
                        S.add("sp", lambda e, xs_=xs_: e.dma_start(out=xs_[:, :, 0:256], in_=ctxT), writes=[("xs", xi)], dsem=fa["xs_sem"][xi])
                    else:
                        S.add("sp", lambda e, xs_=xs_, j=j: e.dma_start(out=xs_[:].rearrange("p a b -> p (a b)"), in_=xT_all[j]),
                              writes=[("xs", xi)], dsem=fa["xs_sem"][xi])
                    front("A", xs_, n, fa["sq2"][bi], fa["rstd2"][bi], fa["tmp"], fa["hx"][bi], a1c if isctx else a1, b1c if isctx else b1, 0,
                          ("xs", xi), ("hx", bi), part="a", stag="A%d" % bi)

                def a2_front_b(j):
                    xi = j % 3
                    bi = j % 2
                    isctx = j == NJ - 1
                    n = 256 if isctx else 512
                    xs_, rp_, hx_ = fa["xs"][xi], fa["rp"][bi], fa["hx"][bi]
                    S.add("sp", lambda e, rp_=rp_, j=j: e.dma_start(out=rp_[:].rearrange("p a b -> p (a b)"), in_=ropeK[j]),
                          writes=[("rp", bi)], dsem=fa["rp_sem"][bi])
                    front("A", xs_, n, fa["sq2"][bi], fa["rstd2"][bi], fa["tmp"], hx_, a1c if isctx else a1, b1c if isctx else b1, 0,
                          ("xs", xi), ("hx", bi), part="b", stag="A%d" % bi)

                def a2_kv(j, which):
                    bi = j % 2
                    isctx = j == NJ - 1
                    n = 256 if isctx else 512
                    k0 = j * 512
                    xs_, rp_, hx_ = fa["xs"][bi], fa["rp"][bi], fa["hx"][bi]
                    hxk = [(("hx", bi), kc) for kc in range(KC)]
                    for kvh in range(2 if which == "k" else 0):
                        pA, pB, pss = 2 + 2 * kvh, 3 + 2 * kvh, 1
                        for ver, pb_ in ((0, pA), (1, pB)):
                            for kc in range(KC):
                                S.add("pe", lambda e, kc=kc, ver=ver, pb_=pb_, kvh=kvh, hx_=hx_, n=n: e.matmul(
                                    bank(pb_)[:, 0:n], lhsT=wk_t[:, kc, ver, kvh * 128:(kvh + 1) * 128], rhs=hx_[:, kc, 0:n],
                                    start=(kc == 0), stop=(kc == KC - 1)), reads=WA + hxk, writes=[bk(pb_)])
                        ks_ = kst[kvh][bi]
                        head_norm_rope("k%d" % kvh, pA, pB, n, 2, rp_, ("rp", bi), fa["sqh"][kvh], fa["rsth"][kvh],
                                       fa["t1"][kvh], fa["t2"][kvh], ks_[:, 0:n], ("kst", kvh, bi), pss)
                        if 'kscr' not in SKIP:
                            S.add("pool", lambda e, ks_=ks_, kvh=kvh, k0=k0, n=n: e.dma_start(out=kscr[kvh][:, k0:k0 + n], in_=ks_[:, 0:n]),
                                  reads=[("kst", kvh, bi)], dsem=kst_sem[kvh][bi])
                    if which != "v":
                        return
                    nt = n // 128
                    for tt in range(0 if 'v' in SKIP else nt):
                        pv_ = int(os.environ.get("KPV", "6")) + (tt % 2)
                        for kc in range(KC):
                            S.add("pe", lambda e, kc=kc, tt=tt, pv_=pv_, hx_=hx_: e.matmul(
                                bank(pv_)[:, 0:256], lhsT=hx_[:, kc, tt * 128:(tt + 1) * 128], rhs=wv_t[:, kc, :],
                                start=(kc == 0), stop=(kc == KC - 1)), reads=WA + hxk, writes=[bk(pv_)])
                        if tt % 2 == 0:
                            S.add("act", lambda e, pv_=pv_, tt=tt, bi=bi: e.activation(out=vst[bi][:, tt, :], in_=bank(pv_)[:, 0:256], func=AF.Identity),
                                  reads=[bk(pv_)], writes=[("vst", bi)])
                        else:
                            S.add("dve", lambda e, pv_=pv_, tt=tt, bi=bi: e.tensor_copy(out=vst[bi][:, tt, :], in_=bank(pv_)[:, 0:256]),
                                  reads=[bk(pv_)], writes=[("vst", bi)])
                    S.add("pool", lambda e, bi=bi, j=j, nt=nt: e.dma_start(out=vscr[:, j * 4:j * 4 + nt, :], in_=vst[bi][:, 0:nt, :]),
                          reads=[("vst", bi)], dsem=vst_sem[0][bi])

                a2_front(JL[0])
                if len(JL) > 1:
                    a2_front(JL[1])
                a2_front_b(JL[0])
                for jj, j in enumerate(JL):
                    if jj + 2 < len(JL):
                        a2_front(JL[jj + 2])
                    a2_kv(j, "k")
                    if jj + 1 < len(JL):
                        a2_front_b(JL[jj + 1])
                    a2_kv(j, "v")
                if debug and 'dump' not in SKIP:
                    with ExitStack() as sd:
                        dtmp = sb(sd, "dtmp", [128, 2080], F32)
                        for h in range(8):
                            S.add("dve", lambda e, dtmp=dtmp, h=h: e.tensor_copy(out=dtmp[:, 0:E], in_=QA[:, h, :]), writes=["dtmp"])
                            S.add("sp", lambda e, dtmp=dtmp, h=h: e.dma_start(out=dbg["qT"][:, h, :], in_=dtmp[:, 0:E]), reads=["dtmp"], dsem=sem_out)
                S.phase_end("A2")

            with ExitStack() as sB:
                KK = [sb(sB, "K%d" % g, [128, NKEY], BF16) for g in range(2)]
                VA = sb(sB, "VA", [128, NKT, 256], BF16)
                NPC = 5
                TPP = NKT // NPC
                kv_sem = [[newsem() for p in range(NPC)] for g in range(2)]
                vv_sem = [[newsem() for p in range(NPC)] for g in range(2)]
                for g in range(2):
                    for p in range(NPC):
                        S.add("sp", lambda e, g=g, p=p: e.dma_start(out=KK[g][:, p * TPP * 128:(p + 1) * TPP * 128],
                                                                    in_=kscr[g][:, p * TPP * 128:(p + 1) * TPP * 128]),
                              writes=[("K", g, p)], dsem=kv_sem[g][p])
                        if g == 0:
                            S.add("sp", lambda e, p=p: e.dma_start(out=VA[:, p * TPP:(p + 1) * TPP, :],
                                                                   in_=vscr[:, p * TPP:(p + 1) * TPP, :]),
                                  writes=[("V", 0, p), ("V", 1, p)], dsem=vv_sem[g][p])
                if debug:
                    if True:
                        dtmp = sb(sB, "dtmp1", [128, 2080], F32)
                        for q in range(8):
                            S.add("dve", lambda e, dtmp=dtmp, q=q: e.tensor_copy(out=dtmp[:], in_=KK[0][:, q * 2080:(q + 1) * 2080]),
                                  reads=[("K", 0, p) for p in range(NPC)], writes=["dtmp"])
                            S.add("sp", lambda e, dtmp=dtmp, q=q: e.dma_start(out=dbg["k0"][:, q * 2080:(q + 1) * 2080], in_=dtmp[:]), reads=["dtmp"], dsem=sem_out)
                        for q in range(10):
                            S.add("dve", lambda e, dtmp=dtmp, q=q: e.tensor_copy(out=dtmp[:, 0:13 * 128].rearrange("p (a b) -> p a b", b=128),
                                                                        in_=VA[:, q * 13:(q + 1) * 13, 0:128]),
                                  reads=[("V", 0, p) for p in range(NPC)], writes=["dtmp"])
                            S.add("sp", lambda e, q=q, dtmp=dtmp: e.dma_start(out=dbg["v0"][:, q * 13:(q + 1) * 13, :],
                                                                 in_=dtmp[:, 0:13 * 128].rearrange("p (a b) -> p a b", b=128)),
                                  reads=["dtmp"], dsem=sem_out)
                NPB = 6
                PT = [sb(sB, "pt%d" % i, [128, 2, CH], BF16) for i in range(NPB)]
                s12 = [sb(sB, "s12_%d" % i, [128, 2, CH], BF16) for i in range(3)]
                ssum = [sb(sB, "ssum%d" % i, [128, CH], BF16) for i in range(3)]
                osb = [sb(sB, "osb%d" % i, [128, CH], F32) for i in range(2)]
                rec = [sb(sB, "rec%d" % i, [128, CH], F32) for i in range(2)]
                NB = NKT // 2
                batches = []
                for h in range(8):
                    for c in range(NCH):
                        for b in range(NB):
                            batches.append((h, c, b))
                scale = 128.0 ** -0.5

                def qk(i):
                    h, c, b = batches[i]
                    g = h // 4
                    sbuf_i = i % 3
                    for u in range(2):
                        j = 2 * b + u
                        S.add("pe", lambda e, g=g, j=j, h=h, c=c, sbuf_i=sbuf_i, u=u: e.matmul(
                            PS[sbuf_i][:, u, 0:CH], lhsT=KK[g][:, j * 128:(j + 1) * 128], rhs=QA[:, h, c * CH:(c + 1) * CH],
                            start=True, stop=True), reads=[("QA", h, c), ("K", g, j // TPP)], writes=[bk(2 * sbuf_i + u)])

                def ex(i):
                    sbuf_i = i % 3
                    pi = i % NPB
                    S.add("act", lambda e, sbuf_i=sbuf_i, pi=pi: e.activation(
                        out=PT[pi][:], in_=PS[sbuf_i][:, :, 0:CH], func=AF.Exp, bias=nbias[:], scale=scale),
                        reads=[bk(2 * sbuf_i), bk(2 * sbuf_i + 1)], writes=[("pt", pi)])

                def pv(i):
                    h, c, b = batches[i]
                    g = h // 4
                    pi = i % NPB
                    po = 6
                    for u in range(2):
                        j = 2 * b + u
                        first = (b == 0 and u == 0)
                        last = (b == NB - 1 and u == 1)
                        S.add("pe", lambda e, g=g, j=j, pi=pi, u=u, po=po, first=first, last=last: e.matmul(
                            bank(po)[:, 0:CH], lhsT=VA[:, j, g * 128:(g + 1) * 128], rhs=PT[pi][:, u, :], start=first, stop=last),
                            reads=[("pt", pi), ("V", g, j // TPP)], writes=[bk(po)])

                sum_n = [0]

                def dve_sum(i):
                    h, c, b = batches[i]
                    k = sum_n[0] % 3
                    sum_n[0] += 1
                    pi = i % NPB
                    if b % 2 == 1:
                        pj = (i - 1) % NPB
                        S.add("dve", lambda e, pi=pi, pj=pj, k=k: e.tensor_tensor(out=s12[k][:], in0=PT[pj][:], in1=PT[pi][:], op=ALU.add),
                              reads=[("pt", pi), ("pt", pj)], writes=[("s12", k)])
                        S.add("dve", lambda e, k=k: e.tensor_tensor(out=ssum[k][:], in0=s12[k][:, 0, :], in1=s12[k][:, 1, :], op=ALU.add),
                              reads=[("s12", k)], writes=[("ssum", k)])
                    else:
                        S.add("dve", lambda e, pi=pi, k=k: e.tensor_tensor(out=ssum[k][:], in0=PT[pi][:, 0, :], in1=PT[pi][:, 1, :], op=ALU.add),
                              reads=[("pt", pi)], writes=[("ssum", k)])
                    return (k, 7, b == 1, b == NB - 1)

                def sum_mm(desc):
                    k, psm, first, last = desc
                    S.add("pe", lambda e, k=k, psm=psm, first=first, last=last: e.matmul(
                        bank(psm)[:, 0:CH], lhsT=ones_bf[:], rhs=ssum[k][:], start=first, stop=last),
                        reads=[("ssum", k)], writes=[bk(psm)])

                def norm(i):
                    h, c, b = batches[i]
                    itn = h * NCH + c
                    ob = itn % 2
                    po, psm = 6, 7
                    S.add("dve", lambda e, ob=ob: e.tensor_copy(out=osb[ob][:], in_=bank(6)[:, 0:CH]),
                          reads=[bk(6)], writes=[("osb", ob)])
                    S.add("dve", lambda e, ob=ob: e.tensor_copy(out=rec[ob][:], in_=bank(7)[:, 0:CH]),
                          reads=[bk(7)], writes=[("rec", ob)])
                    S.add("dve", lambda e, ob=ob: e.reciprocal(out=rec[ob][:], in_=rec[ob][:]),
                          reads=[("rec", ob)], writes=[("rec", ob)])
                    S.add("dve", lambda e, ob=ob, h=h, c=c: e.tensor_tensor(
                        out=QA[:, h, c * CH:(c + 1) * CH], in0=osb[ob][:], in1=rec[ob][:], op=ALU.mult),
                        reads=[("osb", ob), ("rec", ob)], writes=[("QA", h, c)])

                NBT = len(batches)
                qk(0)
                ex(0)
                qk(1)
                ex(1)
                pending = None
                for i in range(NBT):
                    if i + 2 < NBT:
                        qk(i + 2)
                        ex(i + 2)
                    pv(i)
                    if pending is not None:
                        sum_mm(pending)
                        pending = None
                    b = batches[i][2]
                    if b % 2 == 1 or b == NB - 1:
                        pending = dve_sum(i)
                    if b == NB - 1:
                        sum_mm(pending)
                        pending = None
                        norm(i)
                if debug:
                    with ExitStack() as sd:
                        dtmp = sb(sd, "dtmp2", [128, 2080], F32)
                        for h in range(8):
                            S.add("dve", lambda e, dtmp=dtmp, h=h: e.tensor_copy(out=dtmp[:, 0:E], in_=QA[:, h, :]),
                                  reads=[("QA", h, c) for c in range(NCH)], writes=["dtmp"])
                            S.add("sp", lambda e, dtmp=dtmp, h=h: e.dma_start(out=dbg["at"][:, h, :], in_=dtmp[:, 0:E]), reads=["dtmp"], dsem=sem_out)
                S.phase_end("B")
            with ExitStack() as sC:
                PADH = 16
                ycT = sb(sC, "ycT", [128, 4, E], BF16)
                sC23 = ExitStack()
                hT = sb(sC23, "hT", [128, 4, E + 2 * PADH], BF16)
                S.add("pool", lambda e: e.memset(hT[:, :, 0:PADH], 0.0), writes=["hTpadL"])
                S.add("pool", lambda e: e.memset(hT[:, :, PADH + E:], 0.0), writes=["hTpadR"])
                with ExitStack() as sC2:
                    wu_t = sb(sC2, "wu_t", [128, KC, D], BF16)
                    sem_w = newsem()
                    for kc in range(KC):
                        S.add("pool", lambda e, kc=kc: e.dma_start(out=wu_t[:, kc], in_=wu[:, kc]), writes=["wu"], dsem=sem_w)
                    xs_c2 = [sb(sC2, "cxs%d" % i, [128, KC, CH], F32) for i in range(2)]
                    xs_sem = [newsem() for _ in range(2)]
                    sq = sb(sC2, "csq", [128, KC, CH], BF16)
                    tmp = sb(sC2, "ctmp", [128, 4, CH], F32)
                    rstd = sb(sC2, "crstd", [128, CH], F32)
                    hx_b = [sb(sC2, "chx%d" % i, [128, KC, CH], BF16) for i in range(2)]
                    sg = [sb(sC2, "sg%d" % i, [128, CH], F32) for i in range(2)]
                    hh = [sb(sC2, "hh%d" % i, [128, CH], F32) for i in range(2)]
                    for c in range(NCH):
                        bi = c % 2
                        S.add("sp", lambda e, c=c, bi=bi: e.dma_start(out=xs_c2[bi][:], in_=xT_own[:, :, c * CH:(c + 1) * CH]),
                              writes=[("xs", bi)], dsem=xs_sem[bi])
                        front("C", xs_c2[bi], CH, sq, rstd, tmp, hx_b[bi], a1, b1, 0, ("xs", bi), ("hx", bi))
                        hxk = [(("hx", bi), kc) for kc in range(KC)]
                        for fc in range(4):
                            fb = fc % 2
                            pa, pb_ = 2 + 2 * fb, 3 + 2 * fb
                            for half, pp in ((0, pa), (1, pb_)):
                                col = half * 512 + fc * 128
                                for kc in range(KC):
                                    S.add("pe", lambda e, kc=kc, pp=pp, col=col, bi=bi: e.matmul(
                                        bank(pp)[:, 0:CH], lhsT=wu_t[:, kc, col:col + 128], rhs=hx_b[bi][:, kc, :],
                                        start=(kc == 0), stop=(kc == KC - 1)), reads=["wu"] + hxk, writes=[bk(pp)])
                            S.add("act", lambda e, fb=fb, pb_=pb_: e.activation(out=sg[fb][:], in_=bank(pb_)[:, 0:CH], func=AF.Sigmoid),
                                  reads=[bk(pb_)], writes=[("sg", fb)])
                            S.add("dve", lambda e, fb=fb, pa=pa, fc=fc, c=c: e.tensor_tensor(
                                out=hT[:, fc, PADH + c * CH:PADH + (c + 1) * CH], in0=bank(pa)[:, 0:CH], in1=sg[fb][:], op=ALU.mult),
                                reads=[bk(pa), ("sg", fb)], writes=[("hT", fc, c)])
                            if c == 0 or c == NCH - 1:
                                e0 = 0 if c == 0 else E - HALO
                                m0 = 0 if c == 0 else HALO
                                S.add("dve", lambda e, fc=fc, e0=e0, m0=m0: e.tensor_tensor(
                                    out=hT[:, fc, PADH + e0:PADH + e0 + HALO], in0=hT[:, fc, PADH + e0:PADH + e0 + HALO],
                                    in1=mask_t[:, m0:m0 + HALO], op=ALU.mult),
                                    reads=[("hT", fc, c)], writes=[("hT", fc, c)])
                    S.phase_end("C2")
                with ExitStack() as sC3:
                    cdiag = sb(sC3, "cdiag", [128, 4, 31, 128], BF16)
                    cdw_t = sb(sC3, "cdw_t", [128, 4, 31], F32)
                    sem_w = newsem()
                    S.add("sp", lambda e: e.dma_start(out=cdw_t[:], in_=cdw), writes=["cdw"], dsem=sem_w)
                    for fc in range(4):
                        for j in range(31):
                            S.add("dve", lambda e, fc=fc, j=j: e.tensor_scalar(out=cdiag[:, fc, j, :], in0=ident_bf[:],
                                                                              scalar1=cdw_t[:, fc, j:j + 1], scalar2=None, op0=ALU.mult),
                                  reads=["cdw"], writes=[("cdiag", fc)])
                    cvf = sb(sC3, "cvf", [128, 4, CH], F32)
                    cvb = sb(sC3, "cvb", [128, 4, CH], BF16)
                    csq = sb(sC3, "csq3", [128, 4, CH], BF16)
                    mean = sb(sC3, "mean", [128, CH], F32)
                    msq = sb(sC3, "msq", [128, CH], F32)
                    var = sb(sC3, "var", [128, CH], F32)
                    tt_ = [sb(sC3, "tt%d" % i, [128, CH], F32) for i in range(2)]
                    for c in range(NCH):
                        for fc in range(4):
                            pc = 2 + (fc % 2)
                            for j in range(31):
                                off = c * CH + 1 + j
                                S.add("pe", lambda e, fc=fc, j=j, pc=pc, off=off: e.matmul(
                                    bank(pc)[:, 0:CH], lhsT=cdiag[:, fc, j, :], rhs=hT[:, fc, off:off + CH],
                                    start=(j == 0), stop=(j == 30)), reads=[("cdiag", fc)], writes=[bk(pc)])
                            S.add("act", lambda e, fc=fc, pc=pc: e.activation(out=cvf[:, fc, :], in_=bank(pc)[:, 0:CH], func=AF.Identity,
                                                                              bias=cvec_t[:, 0, fc:fc + 1], scale=1.0),
                                  reads=[bk(pc)], writes=[("cvf", fc)])
                            S.add("dve", lambda e, fc=fc: e.tensor_copy(out=cvb[:, fc, :], in_=cvf[:, fc, :]),
                                  reads=[("cvf", fc)], writes=[("cvb", fc)])
                            S.add("act", lambda e, fc=fc: e.activation(out=csq[:, fc, :], in_=cvf[:, fc, :], func=AF.Square),
                                  reads=[("cvf", fc)], writes=[("csq", fc)])
                        for fc in range(4):
                            S.add("pe", lambda e, fc=fc: e.matmul(bank(0)[:, 0:CH], lhsT=ones_bf[:], rhs=cvb[:, fc, :],
                                                                   start=(fc == 0), stop=(fc == 3)), reads=[("cvb", fc)], writes=[bk(0)])
                        for fc in range(4):
                            S.add("pe", lambda e, fc=fc: e.matmul(bank(1)[:, 0:CH], lhsT=ones_bf[:], rhs=csq[:, fc, :],
                                                                   start=(fc == 0), stop=(fc == 3)), reads=[("csq", fc)], writes=[bk(1)])
                        S.add("dve", lambda e: e.tensor_scalar(out=mean[:], in0=bank(0)[:, 0:CH], scalar1=1.0 / 512, scalar2=None, op0=ALU.mult),
                              reads=[bk(0)], writes=["mean"])
                        S.add("dve", lambda e: e.tensor_tensor(out=msq[:], in0=mean[:], in1=mean[:], op=ALU.mult), reads=["mean"], writes=["msq"])
                        S.add("dve", lambda e: e.scalar_tensor_tensor(out=var[:], in0=bank(1)[:, 0:CH], scalar=1.0 / 512, in1=msq[:],
                                                                      op0=ALU.mult, op1=ALU.subtract), reads=[bk(1), "msq"], writes=["var"])
                        rsqrt_act(var[:], var[:], 1.0, ["var"], "var")
                        for fc in range(4):
                            tb = fc % 2
                            S.add("dve", lambda e, fc=fc, tb=tb: e.tensor_tensor(out=tt_[tb][:], in0=cvf[:, fc, :], in1=mean[:], op=ALU.subtract),
                                  reads=[("cvf", fc), "mean"], writes=[("tt", tb)])
                            S.add("dve", lambda e, tb=tb: e.tensor_tensor(out=tt_[tb][:], in0=tt_[tb][:], in1=var[:], op=ALU.mult),
                                  reads=[("tt", tb), "var"], writes=[("tt", tb)])
                            S.add("act", lambda e, fc=fc, tb=tb, c=c: e.activation(out=ycT[:, fc, c * CH:(c + 1) * CH], in_=tt_[tb][:], func=AF.Silu,
                                                                                 bias=cvec_t[:, 2, fc:fc + 1], scale=cvec_t[:, 1, fc:fc + 1]),
                                  reads=[("tt", tb)], writes=[("ycT", fc, c)])
                    if debug:
                        with ExitStack() as sd:
                            dtmp = sb(sd, "dtmp3", [128, 2080], F32)
                            for fc in range(4):
                                S.add("dve", lambda e, dtmp=dtmp, fc=fc: e.tensor_copy(out=dtmp[:], in_=hT[:, fc, PADH:PADH + E]), writes=["dtmp"])
                                S.add("sp", lambda e, dtmp=dtmp, fc=fc: e.dma_start(out=dbg["hT"][:, fc, :], in_=dtmp[:]), reads=["dtmp"], dsem=sem_out)
                            for fc in range(4):
                                S.add("dve", lambda e, dtmp=dtmp, fc=fc: e.tensor_copy(out=dtmp[:], in_=ycT[:, fc, :]),
                                      reads=[("ycT", fc, c) for c in range(NCH)], writes=["dtmp"])
                                S.add("sp", lambda e, dtmp=dtmp, fc=fc: e.dma_start(out=dbg["yc"][:, fc, :], in_=dtmp[:]), reads=["dtmp"], dsem=sem_out)
                    S.phase_end("C3")
                sC23.close()
                with ExitStack() as sC4:
                    wg_t = sb(sC4, "wg_t", [128, KC, 2 * D], BF16)
                    wao_t = sb(sC4, "wao_t", [128, KC, D], BF16)
                    wco_t = sb(sC4, "wco_t", [128, 4, D], BF16)
                    wout_t = sb(sC4, "wout_t", [128, KC, D], BF16)
                    sem_w = newsem()
                    for kc in range(KC):
                        S.add("pool", lambda e, kc=kc: e.dma_start(out=wg_t[:, kc], in_=wg[:, kc]), writes=["w4"], dsem=sem_w)
                    S.add("pool", lambda e: e.dma_start(out=wao_t[:], in_=wao), writes=["w4"], dsem=sem_w)
                    S.add("pool", lambda e: e.dma_start(out=wco_t[:], in_=wco), writes=["w4"], dsem=sem_w)
                    S.add("pool", lambda e: e.dma_start(out=wout_t[:], in_=wout), writes=["w4"], dsem=sem_w)
                    xs_b = [sb(sC4, "dxs%d" % i, [128, KC, CH], F32) for i in range(2)]
                    xs_sem = [newsem() for _ in range(2)]
                    sq = sb(sC4, "dsq", [128, KC, CH], BF16)
                    tmp = sb(sC4, "dtmp_", [128, 4, CH], F32)
                    rstd = sb(sC4, "drstd", [128, CH], F32)
                    hx = sb(sC4, "dhx", [128, KC, CH], BF16)
                    mT = sb(sC4, "mT", [128, KC, CH], BF16)
                    sa = [sb(sC4, "sa%d" % i, [128, CH], F32) for i in range(2)]
                    sb_ = [sb(sC4, "sbb%d" % i, [128, CH], F32) for i in range(2)]
                    u1 = [sb(sC4, "u1%d" % i, [128, CH], F32) for i in range(2)]
                    u2 = [sb(sC4, "u2%d" % i, [128, CH], F32) for i in range(2)]
                    x1o_sem = [newsem() for _ in range(2)]
                    def d_load(c):
                        bi = c % 2
                        cs = slice(c * CH, (c + 1) * CH)
                        S.add("sp", lambda e, cs=cs, bi=bi: e.dma_start(out=xs_b[bi][:], in_=xT_own[:, :, cs]),
                              writes=[("xs", bi)], dsem=xs_sem[bi])

                    def d_front(c):
                        bi = c % 2
                        cs = slice(c * CH, (c + 1) * CH)
                        front("D", xs_b[bi], CH, sq, rstd, tmp, hx, a1, b1, 0, ("xs", bi), "hx")

                    def d_gates(c):
                        bi = c % 2
                        cs = slice(c * CH, (c + 1) * CH)
                        hxk = [("hx", kc) for kc in range(KC)]
                        for oc in range(KC):
                            ob = oc % 2
                            pga, pgb, pya, pyb = 4 * ob, 4 * ob + 1, 4 * ob + 2, 4 * ob + 3
                            for half, pp in ((0, pga), (1, pgb)):
                                col = half * D + oc * 128
                                for kc in range(KC):
                                    S.add("pe", lambda e, kc=kc, pp=pp, col=col: e.matmul(
                                        bank(pp)[:, 0:CH], lhsT=wg_t[:, kc, col:col + 128], rhs=hx[:, kc, :],
                                        start=(kc == 0), stop=(kc == KC - 1)), reads=["w4"] + hxk, writes=[bk(pp)])
                            for kc in range(KC):
                                S.add("pe", lambda e, kc=kc, pya=pya, oc=oc, cs=cs: e.matmul(
                                    bank(pya)[:, 0:CH], lhsT=wao_t[:, kc, oc * 128:(oc + 1) * 128], rhs=QA[:, kc, cs],
                                    start=(kc == 0), stop=(kc == KC - 1)), reads=["w4"], writes=[bk(pya)])
                            for fc in range(4):
                                S.add("pe", lambda e, fc=fc, pyb=pyb, oc=oc, cs=cs: e.matmul(
                                    bank(pyb)[:, 0:CH], lhsT=wco_t[:, fc, oc * 128:(oc + 1) * 128], rhs=ycT[:, fc, cs],
                                    start=(fc == 0), stop=(fc == 3)), reads=["w4"], writes=[bk(pyb)])
                            S.add("act", lambda e, ob=ob, pga=pga: e.activation(out=sa[ob][:], in_=bank(pga)[:, 0:CH], func=AF.Sigmoid),
                                  reads=[bk(pga)], writes=[("sa", ob)])
                            S.add("act", lambda e, ob=ob, pgb=pgb: e.activation(out=sb_[ob][:], in_=bank(pgb)[:, 0:CH], func=AF.Sigmoid),
                                  reads=[bk(pgb)], writes=[("sb", ob)])
                            S.add("dve", lambda e, ob=ob, pya=pya: e.tensor_tensor(out=u1[ob][:], in0=bank(pya)[:, 0:CH], in1=sa[ob][:], op=ALU.mult),
                                  reads=[bk(pya), ("sa", ob)], writes=[("u1", ob)])
                            S.add("dve", lambda e, ob=ob, pyb=pyb: e.tensor_tensor(out=u2[ob][:], in0=bank(pyb)[:, 0:CH], in1=sb_[ob][:], op=ALU.mult),
                                  reads=[bk(pyb), ("sb", ob)], writes=[("u2", ob)])
                            S.add("dve", lambda e, ob=ob, oc=oc: e.tensor_tensor(out=mT[:, oc, :], in0=u1[ob][:], in1=u2[ob][:], op=ALU.add),
                                  reads=[("u1", ob), ("u2", ob)], writes=[("mT", oc)])

                    def d_out(c):
                        bi = c % 2
                        cs = slice(c * CH, (c + 1) * CH)
                        hxk = [("hx", kc) for kc in range(KC)]
                        mk = [("mT", oc) for oc in range(KC)]
                        for oc in range(KC):
                            pm = 2 + (oc % 2)
                            for kc in range(KC):
                                S.add("pe", lambda e, kc=kc, pm=pm, oc=oc: e.matmul(
                                    bank(pm)[:, 0:CH], lhsT=wout_t[:, kc, oc * 128:(oc + 1) * 128], rhs=mT[:, kc, :],
                                    start=(kc == 0), stop=(kc == KC - 1)), reads=["w4"] + mk, writes=[bk(pm)])
                            S.add("dve", lambda e, pm=pm, oc=oc, bi=bi: e.scalar_tensor_tensor(
                                out=xs_b[bi][:, oc, :], in0=bank(pm)[:, 0:CH], scalar=g1(oc), in1=xs_b[bi][:, oc, :], op0=ALU.mult, op1=ALU.add),
                                reads=[bk(pm), ("xs", bi)], writes=[("x1o", bi, oc)])
                        S.add("pool", lambda e, bi=bi, cs=cs: e.dma_start(out=x1scr[:, :, cs], in_=xs_b[bi][:]),
                              reads=[("x1o", bi, oc) for oc in range(KC)] + [("xs", bi)], writes=[("x1scr", c)], dsem=x1o_sem[bi])

                    d_load(0)
                    d_front(0)
                    for c in range(NCH):
                        if c + 1 < NCH:
                            d_load(c + 1)
                        d_gates(c)
                        if c + 1 < NCH:
                            d_front(c + 1)
                        d_out(c)
                    S.phase_end("C45")
        with ExitStack() as sF:
            wup_t = sb(sF, "wup_t", [128, KC, 2 * FF], BF16)
            wdn_t = sb(sF, "wdn_t", [128, NFC, D], BF16)
            fdw_t = sb(sF, "fdw_t", [128, NFC, 3], F32)
            sem_w = newsem()
            for kc in range(KC):
                S.add("pool", lambda e, kc=kc: e.dma_start(out=wup_t[:, kc], in_=wup[:, kc]), writes=["w6"], dsem=sem_w)
            for q in range(2):
                S.add("pool", lambda e, q=q: e.dma_start(out=wdn_t[:, q * 11:(q + 1) * 11], in_=wdn[:, q * 11:(q + 1) * 11]), writes=["w6"], dsem=sem_w)
            sem_w2 = newsem()
            S.add("sp", lambda e: e.dma_start(out=fdw_t[:], in_=fdw), writes=["w6b"], dsem=sem_w2)
            W6 = ["w6", "w6b"]
            CW = CH + 2
            x1c = [sb(sF, "x1c%d" % i, [128, KC, CW], F32) for i in range(2)]
            x1_sem = [newsem() for _ in range(2)]
            sq = sb(sF, "fsq", [128, KC, CW], BF16)
            tmp = sb(sF, "ftmp", [128, 4, CW], F32)
            rstd = sb(sF, "frstd", [128, CW], F32)
            hx2 = sb(sF, "hx2", [128, KC, CW], BF16)
            aT = [sb(sF, "aT%d" % i, [128, CW], BF16) for i in range(2)]
            fdg = [sb(sF, "fdg%d" % i, [128, 3, 128], BF16) for i in range(2)]
            gl = [sb(sF, "gl%d" % i, [128, CH], F32) for i in range(2)]
            hid = sb(sF, "hid", [128, NFC, CH], BF16)
            fin = sb(sF, "fin", [128, CH], F32)
            for i in range(2):
                S.add("pool", lambda e, i=i: e.memset(aT[i][:], 0.0), writes=[("aT", i)])

            class _V:
                def __init__(self, t, o):
                    self.t, self.o = t, o

                def __getitem__(self, key):
                    a, b, sl = key
                    return self.t[a, b, self.o + sl.start:self.o + sl.stop]

            class _V2:
                def __init__(self, t, o):
                    self.t, self.o = t, o

                def __getitem__(self, key):
                    a, sl = key
                    return self.t[a, self.o + sl.start:self.o + sl.stop]

            def win(c):
                lo = max(c * CH - 1, 0)
                hi = min((c + 1) * CH + 1, E)
                return lo, hi, hi - lo, lo - (c * CH - 1)

            def f_load(c):
                lo, hi, n, o0 = win(c)
                xb = x1c[c % 2]
                S.add("sp", lambda e, lo=lo, hi=hi, o0=o0, n=n, xb=xb: e.dma_start(out=xb[:, :, o0:o0 + n], in_=x1scr[:, :, lo:hi]),
                      writes=[("x1c", c % 2)], dsem=x1_sem[c % 2])

            def f_front(c):
                lo, hi, n, o0 = win(c)
                front("F", _V(x1c[c % 2], o0), n, _V(sq, o0), _V2(rstd, o0), _V(tmp, o0), _V(hx2, o0), a2, b2, 0, ("x1c", c % 2), "hx2")

            def f_up(c):
                lo, hi, n, o0 = win(c)
                hxk = [("hx2", kc) for kc in range(KC)]
                for fc in range(NFC):
                    fb = fc % 2
                    pa, pb_, pcv = 2 + 3 * fb, 3 + 3 * fb, 4 + 3 * fb
                    for j in range(3):
                        S.add("act", lambda e, fc=fc, j=j, fb=fb: e.activation(out=fdg[fb][:, j, :], in_=ident_bf[:], func=AF.Identity,
                                                                             scale=fdw_t[:, fc, j:j + 1]),
                              reads=W6, writes=[("fdg", fb)])
                    for kc in range(KC):
                        S.add("pe", lambda e, kc=kc, pa=pa, fc=fc, o0=o0, n=n: e.matmul(
                            bank(pa)[:, 0:n], lhsT=wup_t[:, kc, fc * 128:(fc + 1) * 128], rhs=hx2[:, kc, o0:o0 + n],
                            start=(kc == 0), stop=(kc == KC - 1)), reads=W6 + hxk, writes=[bk(pa)])
                    S.add("act", lambda e, fb=fb, pa=pa, o0=o0, n=n: e.activation(
                        out=aT[fb][:, o0:o0 + n], in_=bank(pa)[:, 0:n], func=AF.Identity),
                        reads=[bk(pa)], writes=[("aT", fb)])
                    if c == 0 or c == NCH - 1:
                        w0 = (o0 + 0) if c == 0 else (E - HALO - lo + o0)
                        m0 = 0 if c == 0 else HALO
                        S.add("dve", lambda e, fb=fb, w0=w0, m0=m0: e.tensor_tensor(
                            out=aT[fb][:, w0:w0 + HALO], in0=aT[fb][:, w0:w0 + HALO], in1=mask_t[:, m0:m0 + HALO], op=ALU.mult),
                            reads=[("aT", fb)], writes=[("aT", fb)])
                    for kc in range(KC):
                        S.add("pe", lambda e, kc=kc, pb_=pb_, fc=fc: e.matmul(
                            bank(pb_)[:, 0:CH], lhsT=wup_t[:, kc, FF + fc * 128:FF + (fc + 1) * 128], rhs=hx2[:, kc, 1:1 + CH],
                            start=(kc == 0), stop=(kc == KC - 1)), reads=W6 + hxk, writes=[bk(pb_)])
                    for j in range(3):
                        S.add("pe", lambda e, j=j, pcv=pcv, fb=fb: e.matmul(
                            bank(pcv)[:, 0:CH], lhsT=fdg[fb][:, j, :], rhs=aT[fb][:, j:j + CH],
                            start=(j == 0), stop=(j == 2)), reads=[("fdg", fb), ("aT", fb)], writes=[bk(pcv)])
                    S.add("act", lambda e, fb=fb, pcv=pcv, fc=fc: e.activation(out=gl[fb][:], in_=bank(pcv)[:, 0:CH], func=AF.Gelu_apprx_tanh,
                                                                              bias=fdb_t[:, fc:fc + 1], scale=1.0),
                          reads=[bk(pcv)], writes=[("gl", fb)])
                    S.add("dve", lambda e, fb=fb, pb_=pb_, fc=fc: e.tensor_tensor(out=hid[:, fc, :], in0=bank(pb_)[:, 0:CH], in1=gl[fb][:], op=ALU.mult),
                          reads=[bk(pb_), ("gl", fb)], writes=[("hid", fc)])

            def f_down(c):
                xb = x1c[c % 2]
                xk = ("x1c", c % 2)
                hk = [("hid", fc) for fc in range(NFC)]
                for oc in range(KC):
                    pd = oc % 2
                    for fc in range(NFC):
                        S.add("pe", lambda e, fc=fc, pd=pd, oc=oc: e.matmul(
                            bank(pd)[:, 0:CH], lhsT=wdn_t[:, fc, oc * 128:(oc + 1) * 128], rhs=hid[:, fc, :],
                            start=(fc == 0), stop=(fc == NFC - 1)), reads=W6 + hk, writes=[bk(pd)])
                    S.add("dve", lambda e, pd=pd, oc=oc, xb=xb: e.scalar_tensor_tensor(
                        out=xb[:, oc, 1:1 + CH], in0=bank(pd)[:, 0:CH], scalar=g2(oc), in1=xb[:, oc, 1:1 + CH], op0=ALU.mult, op1=ALU.add),
                        reads=[bk(pd), xk], writes=[("x2", c % 2, oc)])
                x2k = [("x2", c % 2, oc) for oc in range(KC)]
                S.add("act", lambda e, xb=xb: e.activation(out=sq[:, :, 1:1 + CH], in_=xb[:, :, 1:1 + CH], func=AF.Square),
                      reads=x2k, writes=["Fsq"])
                for kc in range(KC):
                    S.add("pe", lambda e, kc=kc: e.matmul(bank(2)[:, 0:CH], lhsT=ones_bf[:], rhs=sq[:, kc, 1:1 + CH],
                                                         start=(kc == 0), stop=(kc == KC - 1)), reads=["Fsq"], writes=[bk(2)])
                rsqrt_act(fin[:], bank(2)[:, 0:CH], 1.0 / D, [bk(2)], "fin")
                for oc in range(KC):
                    S.add("dve", lambda e, oc=oc, xb=xb: e.scalar_tensor_tensor(
                        out=xb[:, oc, 1:1 + CH], in0=xb[:, oc, 1:1 + CH], scalar=vec8_t[:, 2, oc:oc + 1], in1=fin[:], op0=ALU.mult, op1=ALU.mult),
                        reads=[("x2", c % 2, oc), "fin"], writes=[("x2", c % 2, oc)])
                elo = max(c * CH, HALO)
                ehi = min((c + 1) * CH, HALO + OWN)
                S.add("sp", lambda e, c=c, elo=elo, ehi=ehi, xb=xb: e.dma_start(
                    out=yT[:, :, elo - HALO:ehi - HALO], in_=xb[:, :, 1 + elo - c * CH:1 + ehi - c * CH]),
                    reads=x2k + [xk], dsem=sem_out)

            f_load(0)
            f_front(0)
            for c in range(NCH):
                if c + 1 < NCH:
                    f_load(c + 1)
                f_up(c)
                if c + 1 < NCH:
                    f_front(c + 1)
                f_down(c)

        stats = S.finalize(nc, engsem, final_sems=[sem_out])
    return nc, stats


def _fm(w):
    K, N = w.shape
    return np.ascontiguousarray(w.reshape(K // 128, 128, N).transpose(1, 0, 2))


def _pv(v):
    return np.ascontiguousarray(v.reshape(-1, 128).T)


def _rope_tables():
    half = 64
    inv_freq = (10000.0 ** (-np.arange(0, half, 2, dtype=np.float32) / half)).astype(np.float32)
    t = np.arange(L)
    r = (t // 64).astype(np.float32)
    col = (t % 64).astype(np.float32)
    ang = np.concatenate([r[:, None] * inv_freq, col[:, None] * inv_freq], axis=-1).astype(np.float32)
    cos = np.cos(ang).astype(np.float32)
    sin = np.sin(ang).astype(np.float32)
    d = np.arange(128)
    C = cos[:, d // 2].T
    sgn = np.where(d % 2 == 0, -1.0, 1.0).astype(np.float32)
    Sg = (sin[:, d // 2] * sgn[None, :]).T
    return np.ascontiguousarray(C), np.ascontiguousarray(Sg)


_CACHE = {}


def _prep(x, c, ctx, c_ctx, w_mod, b_mod, norm1_g, w_in, q_norm_g, k_norm_g, w_attn_out,
          conv_dw_w, conv_dw_b, conv_ln_g, conv_ln_b, w_conv_out, w_out, norm2_g,
          w_up, ffn_dw_w, ffn_dw_b, w_down, final_g):
    f = lambda a: np.asarray(a, dtype=np.float32)
    x, c, ctx, c_ctx = f(x), f(c), f(ctx), f(c_ctx)
    xT = np.ascontiguousarray(x[0].T)
    xT_all = np.ascontiguousarray(xT.reshape(KC, 128, L // 512, 512).transpose(2, 1, 0, 3).reshape(L // 512, 128, KC * 512))
    ctxT = np.ascontiguousarray(f(ctx)[0].T.reshape(KC, 128, 256).transpose(1, 0, 2))
    C, Sg = _rope_tables()
    ropeK = np.zeros((128, 2, L + 512), np.float32)
    ropeK[:, 0, :L] = C
    ropeK[:, 1, :L] = Sg
    ropeK[:, 0, L:] = 1.0
    ropeK = np.ascontiguousarray(ropeK.reshape(128, 2, L // 512 + 1, 512).transpose(2, 0, 1, 3).reshape(L // 512 + 1, 128, 1024))
    w_in0 = f(w_in)[0]
    sw = np.arange(D).reshape(-1, 2)[:, ::-1].reshape(-1)
    wq = w_in0[:, 0:D]
    wk = w_in0[:, D:D + 256]
    swk = np.arange(256).reshape(-1, 2)[:, ::-1].reshape(-1)
    wq2 = np.ascontiguousarray(np.stack([_fm(wq), _fm(wq[:, sw])], axis=2))
    wk2 = np.ascontiguousarray(np.stack([_fm(wk), _fm(wk[:, swk])], axis=2))
    sw128 = np.arange(128).reshape(-1, 2)[:, ::-1].reshape(-1)
    gq = f(q_norm_g)[0]
    gk = f(k_norm_g)[0]
    gqk = np.ascontiguousarray(np.stack([gq, gq[sw128], gk, gk[sw128]], axis=1))
    common = {
        "xT_all": xT_all, "ctxT": ctxT, "ropeK": ropeK,
        "ccin": np.ascontiguousarray(np.stack([_pv(c[0]), _pv(c_ctx)], axis=2)),
        "wmod": _fm(f(w_mod)[0]), "bmod": _pv(f(b_mod)[0]),
        "vec8": np.ascontiguousarray(np.stack([_pv(f(norm1_g)[0]), _pv(f(norm2_g)[0]), _pv(f(final_g))], axis=1)),
        "gqk": gqk, "gqk_row": np.ascontiguousarray(np.broadcast_to(np.concatenate([gq, gk])[None, :], (128, 256))),
        "wq2": wq2, "wk2": wk2, "wv": _fm(w_in0[:, D + 256:D + 512]),
        "wu": _fm(w_in0[:, D + 512:2 * D + 512]), "wg": _fm(w_in0[:, 2 * D + 512:]),
        "wao": _fm(f(w_attn_out)[0]), "wco": _fm(f(w_conv_out)[0]), "wout": _fm(f(w_out)[0]),
        "wup": _fm(f(w_up)[0]), "wdn": _fm(f(w_down)[0]),
        "cdw": np.ascontiguousarray(f(conv_dw_w)[0].T.reshape(4, 128, 31).transpose(1, 0, 2)),
        "cvec": np.ascontiguousarray(np.stack([_pv(f(conv_dw_b)[0]), _pv(f(conv_ln_g)[0]), _pv(f(conv_ln_b)[0])], axis=1)),
        "fdw": np.ascontiguousarray(f(ffn_dw_w)[0].T.reshape(NFC, 128, 3).transpose(1, 0, 2)),
        "fdb": _pv(f(ffn_dw_b)[0]),
        "ident": np.eye(128, dtype=np.float32),
    }
    in_maps = []
    for i in range(NCORES):
        t0 = i * OWN - HALO
        idx = np.arange(t0, t0 + E)
        valid = (idx >= 0) & (idx < L)
        idc = np.clip(idx, 0, L - 1)
        xo = xT[:, idc] * 0 if False else xT[:, idc].copy()
        xo[:, ~valid] = 0.0
        m = dict(common)
        m["xT_own"] = np.ascontiguousarray(xo.reshape(KC, 128, E).transpose(1, 0, 2))
        rq = np.zeros((128, 2, E), np.float32)
        rq[:, 0, :] = C[:, idc]
        rq[:, 1, :] = Sg[:, idc]
        m["ropeQ"] = rq
        vh = np.concatenate([valid[:HALO], valid[E - HALO:]]).astype(np.float32)
        m["maskE"] = np.ascontiguousarray(np.broadcast_to(vh[None, :], (128, 2 * HALO)))
        in_maps.append(m)
    return in_maps


def kernel(**inputs):
    in_maps = _prep(**inputs)
    if "main" not in _CACHE:
        _CACHE["main"] = build_program()[0]
    nc = _CACHE["main"]
    res = run_bass_kernel_spmd(nc, in_maps, core_ids=list(range(NCORES)))
    out = np.empty((1, L, D), np.float32)
    for i in range(NCORES):
        y = res.results[i]["yT"]
        out[0, i * OWN:(i + 1) * OWN, :] = y.transpose(2, 1, 0).reshape(OWN, D)
    return out
```

```python
import numpy as np
import concourse.bass as bass
import concourse.mybir as mybir
from concourse.bass_utils import run_bass_kernel_spmd

F32 = mybir.dt.float32
BF16 = mybir.dt.bfloat16
ALU = mybir.AluOpType
AF = mybir.ActivationFunctionType
AX = mybir.AxisListType

NCORES = 8
L = 16384
D = 1024
KC = 8
OWN = 2048
HALO = 16
E = OWN + 2 * HALO
CH = 416
NCH = E // CH
NKEY = L + 256
NKT = NKEY // 128
FF = 2816
NFC = FF // 128
EPS = 1e-6

ENGS = ("pe", "act", "dve", "pool", "sp")


class Op:
    __slots__ = ("eng", "fn", "deps", "sig", "sigval", "dsem", "dval", "idx")


class Sched:
    def __init__(self):
        self.ops = {e: [] for e in ENGS}
        self.lastw = {}
        self.readers = {}
        self.dma_count = {}
        self.n = 0
        self.pending_barrier = {}
        self.live = True
        self.stop_after = None
        self.nobar = set()

    def phase_end(self, name):
        self.barrier()
        if self.stop_after == name:
            self.live = False

    def add(self, eng, fn, reads=(), writes=(), dsem=None):
        if not self.live:
            return None
        op = Op()
        op.eng = eng
        op.fn = fn
        op.dsem = dsem
        op.sig = False
        op.sigval = 0
        op.dval = 0
        op.idx = self.n
        self.n += 1
        deps = {}
        is_dma = dsem is not None

        def need(d, kind):
            if d is op:
                return
            d_dma = d.dsem is not None
            if not is_dma and not d_dma and d.eng == eng:
                if eng == "pe":
                    return
                if kind == "war":
                    return
            deps[d.idx] = d

        for k in reads:
            w = self.lastw.get(k)
            if w is not None:
                need(w, "raw")
            if isinstance(k, tuple) and k[0] == "ps":
                for r in self.readers.get(k, ()):
                    if r.eng != eng:
                        need(r, "rar")
        for k in writes:
            w = self.lastw.get(k)
            if w is not None:
                need(w, "waw")
            for r in self.readers.get(k, ()):
                need(r, "war")
        bar = self.pending_barrier.pop(eng, None)
        if bar is not None:
            for d in bar:
                if d is not op and not (d.eng == eng and d.dsem is None and not is_dma):
                    deps[d.idx] = d
        op.deps = list(deps.values())
        for k in writes:
            self.lastw[k] = op
            self.readers[k] = []
        for k in reads:
            self.readers.setdefault(k, []).append(op)
        if is_dma:
            c = self.dma_count.get(dsem, 0) + 1
            self.dma_count[dsem] = c
            op.dval = 16 * c
        self.ops[eng].append(op)
        return op

    def barrier(self):
        if not self.live:
            return
        lst = []
        for e in ENGS:
            last_c = None
            last_d = {}
            for op in self.ops[e]:
                if op.dsem is None:
                    last_c = op
                elif id(op.dsem) not in self.nobar:
                    last_d[id(op.dsem)] = op
            if last_c is not None:
                lst.append(last_c)
            lst.extend(last_d.values())
        for e in ENGS:
            prev = self.pending_barrier.get(e, [])
            self.pending_barrier[e] = prev + lst
        keep = {k: op for k, op in self.lastw.items() if op.dsem is not None and id(op.dsem) in self.nobar}
        self.lastw = keep
        self.readers = {}

    def finalize(self, nc, engsem, final_sems=()):
        for e in ENGS:
            for op in self.ops[e]:
                for d in op.deps:
                    if d.dsem is None:
                        d.sig = True
        for e in ENGS:
            c = 0
            for op in self.ops[e]:
                if op.dsem is None and op.sig:
                    c += 1
                    op.sigval = c
        stats = {}

        def run(e, engine):
            waited = {}
            nw = 0
            for op in self.ops[e]:
                want = {}
                for d in op.deps:
                    if d.dsem is not None:
                        s, v = d.dsem, d.dval
                    else:
                        s, v = engsem[d.eng], d.sigval
                    key = id(s)
                    if key not in want or want[key][1] < v:
                        want[key] = (s, v)
                for key, (s, v) in want.items():
                    if waited.get(key, 0) >= v:
                        continue
                    engine.wait_ge(s, v)
                    waited[key] = v
                    nw += 1
                ins = op.fn(engine)
                if op.dsem is not None:
                    ins.then_inc(op.dsem, 16)
                elif op.sig:
                    ins.then_inc(engsem[e], 1)
            stats[e] = (len(self.ops[e]), nw)

        with nc.Block() as block:

            @block.tensor
            def _(eng):
                run("pe", eng)

            @block.scalar
            def _(eng):
                run("act", eng)

            @block.vector
            def _(eng):
                run("dve", eng)

            @block.gpsimd
            def _(eng):
                run("pool", eng)

            @block.sync
            def _(eng):
                run("sp", eng)
                for s in final_sems:
                    if s in self.dma_count:
                        eng.wait_ge(s, 16 * self.dma_count[s])

        return stats


import os
SKIP = os.environ.get('KSKIP', '').split(',')
SCRKIND = os.environ.get('KSCR', 'Internal')


def build_program(debug=False, stop_after=None):
    from contextlib import ExitStack

    nc = bass.Bass("TRN2", target_bir_lowering=False)
    S = Sched()
    S.stop_after = stop_after

    def din(name, shape):
        return nc.dram_tensor(name, list(shape), F32, kind="ExternalInput").ap()

    xT_all = din("xT_all", [L // 512, 128, KC * 512])
    ctxT = din("ctxT", [128, KC, 256])
    xT_own = din("xT_own", [128, KC, E])
    ropeK = din("ropeK", [L // 512 + 1, 128, 2 * 512])
    ropeQ = din("ropeQ", [128, 2, E])
    maskE = din("maskE", [128, 2 * HALO])
    ccin = din("ccin", [128, KC, 2])
    wmod = din("wmod", [128, KC, 6 * D])
    bmod = din("bmod", [128, 48])
    vec8 = din("vec8", [128, 3, KC])
    gqk = din("gqk", [128, 4])
    gqk_row = din("gqk_row", [128, 256])
    wq2 = din("wq2", [128, KC, 2, D])
    wk2 = din("wk2", [128, KC, 2, 256])
    wv = din("wv", [128, KC, 256])
    wu = din("wu", [128, KC, D])
    wg = din("wg", [128, KC, 2 * D])
    wao = din("wao", [128, KC, D])
    wco = din("wco", [128, 4, D])
    wout = din("wout", [128, KC, D])
    wup = din("wup", [128, KC, 2 * FF])
    wdn = din("wdn", [128, NFC, D])
    cdw = din("cdw", [128, 4, 31])
    cvec = din("cvec", [128, 3, 4])
    fdw = din("fdw", [128, NFC, 3])
    fdb = din("fdb", [128, NFC])
    ident_in = din("ident", [128, 128])

    yT = nc.dram_tensor("yT", [128, KC, OWN], F32, kind="ExternalOutput").ap()
    kscr = [nc.dram_tensor("kscr%d" % g, [128, NKEY], BF16, kind=SCRKIND).ap() for g in range(2)]
    vscr = nc.dram_tensor("vscr", [128, NKT, 256], BF16, kind=SCRKIND).ap()
    x1scr = nc.dram_tensor("x1scr", [128, KC, E], F32, kind=SCRKIND).ap()
    dbg = {}
    if debug:
        dbg["qT"] = nc.dram_tensor("dbg_qT", [128, KC, E], F32, kind="ExternalOutput").ap()
        dbg["k0"] = nc.dram_tensor("dbg_k0", [128, NKEY], F32, kind="ExternalOutput").ap()
        dbg["v0"] = nc.dram_tensor("dbg_v0", [128, NKT, 128], F32, kind="ExternalOutput").ap()
        dbg["at"] = nc.dram_tensor("dbg_at", [128, KC, E], F32, kind="ExternalOutput").ap()
        dbg["mod"] = nc.dram_tensor("dbg_mod", [128, 48, 2], F32, kind="ExternalOutput").ap()
        dbg["hT"] = nc.dram_tensor("dbg_hT", [128, 4, E], F32, kind="ExternalOutput").ap()
        dbg["yc"] = nc.dram_tensor("dbg_yc", [128, 4, E], F32, kind="ExternalOutput").ap()

    es = ExitStack()
    sem_n = [0]

    def newsem(st=None):
        sem_n[0] += 1
        return es.enter_context(nc.semaphore("sm%d" % sem_n[0]))

    def sb(st, name, shape, dt):
        return st.enter_context(nc.sbuf_tensor(name, list(shape), dt))

    with es:
        engsem = {e: newsem() for e in ("pe", "act", "dve", "pool")}
        sem_out = newsem()
        sem_scr = newsem()
        sem_const = newsem()

        PS = [es.enter_context(nc.psum_tensor("ps%d" % i, [128, 2, 512], F32)) for i in range(4)]

        def bank(b):
            return PS[b // 2][:, b % 2, :]

        def bk(b):
            return ("ps", b)

        ones_bf = sb(es, "ones_bf", [128, 128], BF16)
        ident_bf = sb(es, "ident_bf", [128, 128], BF16)
        eps_t = sb(es, "eps_t", [128, 1], F32)
        modsb = sb(es, "modsb", [128, 48, 2], F32)
        bmod_t = sb(es, "bmod_t", [128, 48], F32)
        vec8_t = sb(es, "vec8_t", [128, 3, KC], F32)
        a1 = sb(es, "a1", [128, KC], F32)
        a1c = sb(es, "a1c", [128, KC], F32)
        a2 = sb(es, "a2", [128, KC], F32)
        gqk_t = sb(es, "gqk_t", [128, 4], F32)
        nbias = sb(es, "nbias", [128, 1], F32)
        cvec_t = sb(es, "cvec_t", [128, 3, 4], F32)
        fdb_t = sb(es, "fdb_t", [128, NFC], F32)
        mask_t = sb(es, "mask_t", [128, 2 * HALO], F32)

        def b1(kc):
            return modsb[:, 0 + kc, 0:1]

        def b1c(kc):
            return modsb[:, 0 + kc, 1:2]

        def g1(kc):
            return modsb[:, 16 + kc, 0:1]

        def b2(kc):
            return modsb[:, 24 + kc, 0:1]

        def g2(kc):
            return modsb[:, 40 + kc, 0:1]

        S.add("pool", lambda e: e.memset(ones_bf[:], 1.0), writes=["ones"])
        S.add("pool", lambda e: e.memset(eps_t[:], EPS), writes=["eps"])
        S.add("pool", lambda e: e.dma_start(out=ident_bf[:], in_=ident_in), writes=["ident"], dsem=sem_scr)
        S.add("sp", lambda e: e.dma_start(out=bmod_t[:], in_=bmod), writes=["bmod"], dsem=sem_const)
        S.add("sp", lambda e: e.dma_start(out=vec8_t[:], in_=vec8), writes=["vec8"], dsem=sem_const)
        S.add("sp", lambda e: e.dma_start(out=gqk_t[:], in_=gqk), writes=["gqk"], dsem=sem_const)
        S.add("sp", lambda e: e.dma_start(out=cvec_t[:], in_=cvec), writes=["cvec"], dsem=sem_const)
        S.add("sp", lambda e: e.dma_start(out=fdb_t[:], in_=fdb), writes=["fdb"], dsem=sem_const)
        S.add("sp", lambda e: e.dma_start(out=mask_t[:], in_=maskE), writes=["mask"], dsem=sem_const)
        CONSTK = ["ident", "bmod", "vec8", "gqk", "cvec", "fdb", "mask"]

        with ExitStack() as p0:
            cc_t = sb(p0, "cc_t", [128, KC, 2], F32)
            sc_t = sb(p0, "sc_t", [128, KC, 2], F32)
            wm = [sb(p0, "wm%d" % i, [128, KC, D], F32) for i in range(2)]
            wm_sem = [newsem() for _ in range(2)]
            grow = sb(p0, "grow", [128, 256], F32)
            gmax = sb(p0, "gmax", [128, 2], F32)
            gprod = sb(p0, "gprod", [1, 1], F32)
            ones_f = sb(p0, "ones_f", [1, 128], F32)
            S.add("sp", lambda e: e.dma_start(out=cc_t[:], in_=ccin), writes=["cc"], dsem=sem_const)
            S.add("sp", lambda e: e.dma_start(out=grow[:], in_=gqk_row), writes=["grow"], dsem=sem_const)
            ALLC = CONSTK + ["cc", "grow"]
            S.add("act", lambda e: e.activation(out=sc_t[:], in_=cc_t[:], func=AF.Silu), reads=ALLC, writes=["sc"])
            mps = bank(0)
            mps3 = PS[0][:, 0, 0:96].rearrange("p (a b) -> p a b", b=2)
            for v in range(6):
                bi = v % 2
                S.add("sp", lambda e, v=v, bi=bi: e.dma_start(out=wm[bi][:], in_=wmod[:, :, v * D:(v + 1) * D]),
                      writes=[("wm", bi)], dsem=wm_sem[bi])
                for fcol in range(8):
                    for kc in range(KC):
                        S.add("pe", lambda e, v=v, bi=bi, fcol=fcol, kc=kc: e.matmul(
                            mps3[:, v * 8 + fcol, :], lhsT=wm[bi][:, kc, fcol * 128:(fcol + 1) * 128],
                            rhs=sc_t[:, kc, :], start=(kc == 0), stop=(kc == KC - 1)),
                            reads=[("wm", bi), "sc"], writes=[bk(0)])
            for i in range(2):
                S.add("dve", lambda e, i=i: e.tensor_tensor(out=modsb[:, :, i], in0=mps3[:, :, i], in1=bmod_t[:], op=ALU.add),
                      reads=[bk(0)] + ALLC, writes=["modsb"])
            S.add("dve", lambda e: e.scalar_tensor_tensor(out=a1[:], in0=modsb[:, 8:16, 0], scalar=1.0, in1=vec8_t[:, 0, :],
                                                          op0=ALU.add, op1=ALU.mult), reads=["modsb"] + ALLC, writes=["a1"])
            S.add("dve", lambda e: e.scalar_tensor_tensor(out=a1c[:], in0=modsb[:, 8:16, 1], scalar=1.0, in1=vec8_t[:, 0, :],
                                                          op0=ALU.add, op1=ALU.mult), reads=["modsb"] + ALLC, writes=["a1c"])
            S.add("dve", lambda e: e.scalar_tensor_tensor(out=a2[:], in0=modsb[:, 32:40, 0], scalar=1.0, in1=vec8_t[:, 1, :],
                                                          op0=ALU.add, op1=ALU.mult), reads=["modsb"] + ALLC, writes=["a2"])
            import os
            if os.environ.get('KSKIP') == 'shift':
                S.add('pool', lambda e: e.memset(nbias[:], -12.0), writes=['nbias'])
            else:
                S.add("act", lambda e: e.activation(out=grow[:], in_=grow[:], func=AF.Abs),
                      reads=ALLC, writes=["grow"])
                S.add("dve", lambda e: e.tensor_reduce(out=gmax[:], in_=grow[:].rearrange("p (a b) -> p a b", a=2), axis=AX.X,
                                                       op=ALU.max), reads=["grow"], writes=["gmax"])
                S.add("dve", lambda e: e.scalar_tensor_tensor(out=nbias[:], in0=gmax[:, 0:1], scalar=-(128.0 ** 0.5), in1=gmax[:, 1:2],
                                                              op0=ALU.mult, op1=ALU.mult), reads=["gmax"], writes=["nbias"])
            if debug:
                S.add("sp", lambda e: e.dma_start(out=dbg["mod"], in_=modsb[:]), reads=["modsb"], dsem=sem_out)
            S.phase_end("P0")

        MODK = []

        def rsqrt_act(out_ap, in_ap, scl, rd, key):
            S.add("act", lambda e: e.activation(out=out_ap, in_=in_ap, func=AF.Ln, bias=eps_t[:], scale=scl),
                  reads=rd, writes=[key])
            S.add("act", lambda e: e.activation(out=out_ap, in_=out_ap, func=AF.Exp, scale=-0.5),
                  reads=[key], writes=[key])

        def front(tag, xs, n, sq, rstd, tmp, hx, avec, bfun, psb, xs_key, hx_key, part="ab", stag=None):
            stag = stag or tag
            if "a" in part:
                S.add("act", lambda e: e.activation(out=sq[:, :, 0:n], in_=xs[:, :, 0:n], func=AF.Square),
                      reads=[xs_key], writes=[stag + "sq"])
                for kc in range(KC):
                    S.add("pe", lambda e, kc=kc: e.matmul(bank(psb)[:, 0:n], lhsT=ones_bf[:], rhs=sq[:, kc, 0:n],
                                                         start=(kc == 0), stop=(kc == KC - 1)),
                          reads=[stag + "sq"], writes=[bk(psb)])
                rsqrt_act(rstd[:, 0:n], bank(psb)[:, 0:n], 1.0 / D, [bk(psb)], stag + "rstd")
            if "b" not in part:
                return
            for kc in range(KC):
                tb = kc % 4
                S.add("dve", lambda e, kc=kc, tb=tb: e.scalar_tensor_tensor(out=tmp[:, tb, 0:n], in0=xs[:, kc, 0:n],
                                                                              scalar=avec[:, kc:kc + 1], in1=rstd[:, 0:n],
                                                                              op0=ALU.mult, op1=ALU.mult),
                      reads=[xs_key, stag + "rstd"], writes=[(tag + "tmp", tb)])
                if kc % 2 == 0:
                    S.add("dve", lambda e, kc=kc, tb=tb: e.tensor_scalar(out=hx[:, kc, 0:n], in0=tmp[:, tb, 0:n], scalar1=bfun(kc),
                                                                          scalar2=None, op0=ALU.add),
                          reads=[(tag + "tmp", tb)], writes=[(hx_key, kc)])
                else:
                    S.add("act", lambda e, kc=kc, tb=tb: e.activation(out=hx[:, kc, 0:n], in_=tmp[:, tb, 0:n], func=AF.Identity,
                                                                       bias=bfun(kc), scale=1.0),
                          reads=[(tag + "tmp", tb)], writes=[(hx_key, kc)])

        def head_norm_rope(tag, pA, pB, n, gcol, rope_t, rope_key, sqh, rsth, t1, t2, out_ap, out_key, pss):
            S.add("act", lambda e: e.activation(out=sqh[:, 0:n], in_=bank(pA)[:, 0:n], func=AF.Square),
                  reads=[bk(pA)], writes=[tag + "sqh"])
            S.add("pe", lambda e: e.matmul(bank(pss)[:, 0:n], lhsT=ones_bf[:], rhs=sqh[:, 0:n], start=True, stop=True),
                  reads=[tag + "sqh"], writes=[bk(pss)])
            rsqrt_act(rsth[:, 0:n], bank(pss)[:, 0:n], 1.0 / 128, [bk(pss)], tag + "rsth")
            S.add("dve", lambda e: e.scalar_tensor_tensor(out=t1[:, 0:n], in0=bank(pA)[:, 0:n], scalar=gqk_t[:, gcol:gcol + 1],
                                                          in1=rope_t[:, 0, 0:n], op0=ALU.mult, op1=ALU.mult),
                  reads=[bk(pA), rope_key], writes=[tag + "t1"])
            S.add("dve", lambda e: e.scalar_tensor_tensor(out=t2[:, 0:n], in0=bank(pB)[:, 0:n], scalar=gqk_t[:, gcol + 1:gcol + 2],
                                                          in1=rope_t[:, 1, 0:n], op0=ALU.mult, op1=ALU.mult),
                  reads=[bk(pB), rope_key], writes=[tag + "t2"])
            S.add("dve", lambda e: e.tensor_tensor(out=t1[:, 0:n], in0=t1[:, 0:n], in1=t2[:, 0:n], op=ALU.add),
                  reads=[tag + "t1", tag + "t2"], writes=[tag + "t1"])
            S.add("dve", lambda e: e.tensor_tensor(out=out_ap, in0=t1[:, 0:n], in1=rsth[:, 0:n], op=ALU.mult),
                  reads=[tag + "t1", tag + "rsth"], writes=[out_key])

        with ExitStack() as sQA:
            QA = sb(sQA, "QA", [128, KC, E], BF16)

            def alloc_front(st, pfx, W):
                d = {}
                d["xs"] = [sb(st, pfx + "xs%d" % i, [128, KC, W], F32) for i in range(2)]
                d["xs_sem"] = [newsem() for _ in range(2)]
                d["rp"] = [sb(st, pfx + "rp%d" % i, [128, 2, W], F32) for i in range(2)]
                d["rp_sem"] = [newsem() for _ in range(2)]
                d["sq"] = sb(st, pfx + "sq", [128, KC, W], BF16)
                d["tmp"] = sb(st, pfx + "tmp", [128, 4, W], F32)
                d["rstd"] = sb(st, pfx + "rstd", [128, W], F32)
                d["hx"] = [sb(st, pfx + "hx%d" % i, [128, KC, W], BF16) for i in range(2)]
                d["sqh"] = [sb(st, pfx + "sqh%d" % i, [128, W], BF16) for i in range(2)]
                d["rsth"] = [sb(st, pfx + "rsth%d" % i, [128, W], F32) for i in range(2)]
                d["t1"] = [sb(st, pfx + "t1%d" % i, [128, W], F32) for i in range(2)]
                d["t2"] = [sb(st, pfx + "t2%d" % i, [128, W], F32) for i in range(2)]
                return d

            with ExitStack() as sA:
                wq_t = sb(sA, "wq_t", [128, KC, 2, D], BF16)
                sem_w = newsem()
                for kc in range(KC):
                    S.add("pool", lambda e, kc=kc: e.dma_start(out=wq_t[:, kc], in_=wq2[:, kc]), writes=["wq"], dsem=sem_w)
                WA = ["wq"]
                fa = alloc_front(sA, "a", CH)
                def a1_front(c):
                    bi = c % 2
                    xs_, rp_, hx_ = fa["xs"][bi], fa["rp"][bi], fa["hx"][bi]
                    S.add("sp", lambda e, c=c, xs_=xs_: e.dma_start(out=xs_[:], in_=xT_own[:, :, c * CH:(c + 1) * CH]),
                          writes=[("xs", bi)], dsem=fa["xs_sem"][bi])
                    S.add("sp", lambda e, c=c, rp_=rp_: e.dma_start(out=rp_[:], in_=ropeQ[:, :, c * CH:(c + 1) * CH]),
                          writes=[("rp", bi)], dsem=fa["rp_sem"][bi])
                    front("A", xs_, CH, fa["sq"], fa["rstd"], fa["tmp"], hx_, a1, b1, 0, ("xs", bi), ("hx", bi))

                def a1_heads(c):
                    bi = c % 2
                    n = CH
                    rp_, hx_ = fa["rp"][bi], fa["hx"][bi]
                    hxk = [(("hx", bi), kc) for kc in range(KC)]
                    for h in range(8):
                        hb = h % 2
                        pA, pB, pss = 2 + 2 * hb, 3 + 2 * hb, 1
                        for ver, pb_ in ((0, pA), (1, pB)):
                            for kc in range(KC):
                                S.add("pe", lambda e, kc=kc, ver=ver, pb_=pb_, h=h, hx_=hx_, n=n: e.matmul(
                                    bank(pb_)[:, 0:n], lhsT=wq_t[:, kc, ver, h * 128:(h + 1) * 128], rhs=hx_[:, kc, 0:n],
                                    start=(kc == 0), stop=(kc == KC - 1)), reads=WA + hxk, writes=[bk(pb_)])
                        head_norm_rope("q%d" % hb, pA, pB, n, 0, rp_, ("rp", bi), fa["sqh"][hb], fa["rsth"][hb],
                                       fa["t1"][hb], fa["t2"][hb], QA[:, h, c * CH:(c + 1) * CH], ("QA", h, c), pss)

                a1_front(0)
                for c in range(NCH):
                    if c + 1 < NCH:
                        a1_front(c + 1)
                    a1_heads(c)
                S.phase_end("A1")
            with ExitStack() as sA:
                wk_t = sb(sA, "wk_t", [128, KC, 2, 256], BF16)
                wv_t = sb(sA, "wv_t", [128, KC, 256], BF16)
                sem_w = newsem()
                S.add("pool", lambda e: e.dma_start(out=wk_t[:], in_=wk2), writes=["wk"], dsem=sem_w)
                S.add("pool", lambda e: e.dma_start(out=wv_t[:], in_=wv), writes=["wv"], dsem=sem_w)
                WA = ["wk", "wv"]
                fa = alloc_front(sA, "b", 512)
                fa["xs"].append(sb(sA, "bxs2", [128, KC, 512], F32))
                fa["xs_sem"].append(newsem())
                fa["sq2"] = [fa["sq"], sb(sA, "bsq2", [128, KC, 512], BF16)]
                fa["rstd2"] = [fa["rstd"], sb(sA, "brstd2", [128, 512], F32)]
                kst = [[sb(sA, "kst%d_%d" % (g, i), [128, 512], BF16) for i in range(2)] for g in range(2)]
                vst = [sb(sA, "vst%d" % i, [128, 4, 256], BF16) for i in range(2)]
                kst_sem = [[newsem() for i in range(2)] for g in range(2)]
                vst_sem = [[newsem() for i in range(2)] for g in range(2)]
                NJ = L // 512 + 1
                JL = [int(v) for v in os.environ['KJ'].split(',')] if os.environ.get('KJ') else list(range(NJ))
                def a2_front(j):
                    xi = j % 3
                    bi = j % 2
                    isctx = j == NJ - 1
                    n = 256 if isctx else 512
                    xs_ = fa["xs"][xi]
                    if isctx:
                        S.add("sp", lambda e, xs_=xs_: e.dma_start(out=xs_[:, :, 0:256], in_=ctxT), writes=[("xs", xi)], dsem=fa["xs_sem"][xi])
                    else:
                        S.add("sp", lambda e, xs_=xs_, j=j: e.dma_start(out=xs_[:].rearrange("p a b -> p (a b)"), in_=xT_all[j]),
                              writes=[("xs", xi)], dsem=fa["xs_sem"][xi])
                    front("A", xs_, n, fa["sq2"][bi], fa["rstd2"][bi], fa["tmp"], fa["hx"][bi], a1c if isctx else a1, b1c if isctx else b1, 0,
                          ("xs", xi), ("hx", bi), part="a", stag="A%d" % bi)

                def a2_front_b(j):
                    xi = j % 3
                    bi = j % 2
                    isctx = j == NJ - 1
                    n = 256 if isctx else 512
                    xs_, rp_, hx_ = fa["xs"][xi], fa["rp"][bi], fa["hx"][bi]
                    S.add("sp", lambda e, rp_=rp_, j=j: e.dma_start(out=rp_[:].rearrange("p a b -> p (a b)"), in_=ropeK[j]),
                          writes=[("rp", bi)], dsem=fa["rp_sem"][bi])
                    front("A", xs_, n, fa["sq2"][bi], fa["rstd2"][bi], fa["tmp"], hx_, a1c if isctx else a1, b1c if isctx else b1, 0,
                          ("xs", xi), ("hx", bi), part="b", stag="A%d" % bi)

                def a2_kv(j, which):
                    bi = j % 2
                    isctx = j == NJ - 1
                    n = 256 if isctx else 512
                    k0 = j * 512
                    xs_, rp_, hx_ = fa["xs"][bi], fa["rp"][bi], fa["hx"][bi]
                    hxk = [(("hx", bi), kc) for kc in range(KC)]
                    for kvh in range(2 if which == "k" else 0):
                        pA, pB, pss = 2 + 2 * kvh, 3 + 2 * kvh, 1
                        for ver, pb_ in ((0, pA), (1, pB)):
                            for kc in range(KC):
                                S.add("pe", lambda e, kc=kc, ver=ver, pb_=pb_, kvh=kvh, hx_=hx_, n=n: e.matmul(
                                    bank(pb_)[:, 0:n], lhsT=wk_t[:, kc, ver, kvh * 128:(kvh + 1) * 128], rhs=hx_[:, kc, 0:n],
                                    start=(kc == 0), stop=(kc == KC - 1)), reads=WA + hxk, writes=[bk(pb_)])
                        ks_ = kst[kvh][bi]
                        head_norm_rope("k%d" % kvh, pA, pB, n, 2, rp_, ("rp", bi), fa["sqh"][kvh], fa["rsth"][kvh],
                                       fa["t1"][kvh], fa["t2"][kvh], ks_[:, 0:n], ("kst", kvh, bi), pss)
                        if 'kscr' not in SKIP:
                            S.add("pool", lambda e, ks_=ks_, kvh=kvh, k0=k0, n=n: e.dma_start(out=kscr[kvh][:, k0:k0 + n], in_=ks_[:, 0:n]),
                                  reads=[("kst", kvh, bi)], dsem=kst_sem[kvh][bi])
                    if which != "v":
                        return
                    nt = n // 128
                    for tt in range(0 if 'v' in SKIP else nt):
                        pv_ = int(os.environ.get("KPV", "6")) + (tt % 2)
                        for kc in range(KC):
                            S.add("pe", lambda e, kc=kc, tt=tt, pv_=pv_, hx_=hx_: e.matmul(
                                bank(pv_)[:, 0:256], lhsT=hx_[:, kc, tt * 128:(tt + 1) * 128], rhs=wv_t[:, kc, :],
                                start=(kc == 0), stop=(kc == KC - 1)), reads=WA + hxk, writes=[bk(pv_)])
                        if tt % 2 == 0:
                            S.add("act", lambda e, pv_=pv_, tt=tt, bi=bi: e.activation(out=vst[bi][:, tt, :], in_=bank(pv_)[:, 0:256], func=AF.Identity),
                                  reads=[bk(pv_)], writes=[("vst", bi)])
                        else:
                            S.add("dve", lambda e, pv_=pv_, tt=tt, bi=bi: e.tensor_copy(out=vst[bi][:, tt, :], in_=bank(pv_)[:, 0:256]),
                                  reads=[bk(pv_)], writes=[("vst", bi)])
                    S.add("pool", lambda e, bi=bi, j=j, nt=nt: e.dma_start(out=vscr[:, j * 4:j * 4 + nt, :], in_=vst[bi][:, 0:nt, :]),
                          reads=[("vst", bi)], dsem=vst_sem[0][bi])

                a2_front(JL[0])
                if len(JL) > 1:
                    a2_front(JL[1])
                a2_front_b(JL[0])
                for jj, j in enumerate(JL):
                    if jj + 2 < len(JL):
                        a2_front(JL[jj + 2])
                    a2_kv(j, "k")
                    if jj + 1 < len(JL):
                        a2_front_b(JL[jj + 1])
                    a2_kv(j, "v")
                if debug and 'dump' not in SKIP:
                    with ExitStack() as sd:
                        dtmp = sb(sd, "dtmp", [128, 2080], F32)
                        for h in range(8):
                            S.add("dve", lambda e, dtmp=dtmp, h=h: e.tensor_copy(out=dtmp[:, 0:E], in_=QA[:, h, :]), writes=["dtmp"])
                            S.add("sp", lambda e, dtmp=dtmp, h=h: e.dma_start(out=dbg["qT"][:, h, :], in_=dtmp[:, 0:E]), reads=["dtmp"], dsem=sem_out)
                S.phase_end("A2")

            with ExitStack() as sB:
                KK = [sb(sB, "K%d" % g, [128, NKEY], BF16) for g in range(2)]
                VA = sb(sB, "VA", [128, NKT, 256], BF16)
                NPC = 5
                TPP = NKT // NPC
                kv_sem = [[newsem() for p in range(NPC)] for g in range(2)]
                vv_sem = [[newsem() for p in range(NPC)] for g in range(2)]
                for g in range(2):
                    for p in range(NPC):
                        S.add("sp", lambda e, g=g, p=p: e.dma_start(out=KK[g][:, p * TPP * 128:(p + 1) * TPP * 128],
                                                                    in_=kscr[g][:, p * TPP * 128:(p + 1) * TPP * 128]),
                              writes=[("K", g, p)], dsem=kv_sem[g][p])
                        if g == 0:
                            S.add("sp", lambda e, p=p: e.dma_start(out=VA[:, p * TPP:(p + 1) * TPP, :],
                                                                   in_=vscr[:, p * TPP:(p + 1) * TPP, :]),
                                  writes=[("V", 0, p), ("V", 1, p)], dsem=vv_sem[g][p])
                if debug:
                    if True:
                        dtmp = sb(sB, "dtmp1", [128, 2080], F32)
                        for q in range(8):
                            S.add("dve", lambda e, dtmp=dtmp, q=q: e.tensor_copy(out=dtmp[:], in_=KK[0][:, q * 2080:(q + 1) * 2080]),
                                  reads=[("K", 0, p) for p in range(NPC)], writes=["dtmp"])
                            S.add("sp", lambda e, dtmp=dtmp, q=q: e.dma_start(out=dbg["k0"][:, q * 2080:(q + 1) * 2080], in_=dtmp[:]), reads=["dtmp"], dsem=sem_out)
                        for q in range(10):
                            S.add("dve", lambda e, dtmp=dtmp, q=q: e.tensor_copy(out=dtmp[:, 0:13 * 128].rearrange("p (a b) -> p a b", b=128),
                                                                        in_=VA[:, q * 13:(q + 1) * 13, 0:128]),
                                  reads=[("V", 0, p) for p in range(NPC)], writes=["dtmp"])
                            S.add("sp", lambda e, q=q, dtmp=dtmp: e.dma_start(out=dbg["v0"][:, q * 13:(q + 1) * 13, :],
                                                                 in_=dtmp[:, 0:13 * 128].rearrange("p (a b) -> p a b", b=128)),
                                  reads=["dtmp"], dsem=sem_out)
                NPB = 6
                PT = [sb(sB, "pt%d" % i, [128, 2, CH], BF16) for i in range(NPB)]
                s12 = [sb(sB, "s12_%d" % i, [128, 2, CH], BF16) for i in range(3)]
                ssum = [sb(sB, "ssum%d" % i, [128, CH], BF16) for i in range(3)]
                osb = [sb(sB, "osb%d" % i, [128, CH], F32) for i in range(2)]
                rec = [sb(sB, "rec%d" % i, [128, CH], F32) for i in range(2)]
                NB = NKT // 2
                batches = []
                for h in range(8):
                    for c in range(NCH):
                        for b in range(NB):
                            batches.append((h, c, b))
                scale = 128.0 ** -0.5

                def qk(i):
                    h, c, b = batches[i]
                    g = h // 4
                    sbuf_i = i % 3
                    for u in range(2):
                        j = 2 * b + u
                        S.add("pe", lambda e, g=g, j=j, h=h, c=c, sbuf_i=sbuf_i, u=u: e.matmul(
                            PS[sbuf_i][:, u, 0:CH], lhsT=KK[g][:, j * 128:(j + 1) * 128], rhs=QA[:, h, c * CH:(c + 1) * CH],
                            start=True, stop=True), reads=[("QA", h, c), ("K", g, j // TPP)], writes=[bk(2 * sbuf_i + u)])

                def ex(i):
                    sbuf_i = i % 3
                    pi = i % NPB
                    S.add("act", lambda e, sbuf_i=sbuf_i, pi=pi: e.activation(
                        out=PT[pi][:], in_=PS[sbuf_i][:, :, 0:CH], func=AF.Exp, bias=nbias[:], scale=scale),
                        reads=[bk(2 * sbuf_i), bk(2 * sbuf_i + 1)], writes=[("pt", pi)])

                def pv(i):
                    h, c, b = batches[i]
                    g = h // 4
                    pi = i % NPB
                    po = 6
                    for u in range(2):
                        j = 2 * b + u
                        first = (b == 0 and u == 0)
                        last = (b == NB - 1 and u == 1)
                        S.add("pe", lambda e, g=g, j=j, pi=pi, u=u, po=po, first=first, last=last: e.matmul(
                            bank(po)[:, 0:CH], lhsT=VA[:, j, g * 128:(g + 1) * 128], rhs=PT[pi][:, u, :], start=first, stop=last),
                            reads=[("pt", pi), ("V", g, j // TPP)], writes=[bk(po)])

                sum_n = [0]

                def dve_sum(i):
                    h, c, b = batches[i]
                    k = sum_n[0] % 3
                    sum_n[0] += 1
                    pi = i % NPB
                    if b % 2 == 1:
                        pj = (i - 1) % NPB
                        S.add("dve", lambda e, pi=pi, pj=pj, k=k: e.tensor_tensor(out=s12[k][:], in0=PT[pj][:], in1=PT[pi][:], op=ALU.add),
                              reads=[("pt", pi), ("pt", pj)], writes=[("s12", k)])
                        S.add("dve", lambda e, k=k: e.tensor_tensor(out=ssum[k][:], in0=s12[k][:, 0, :], in1=s12[k][:, 1, :], op=ALU.add),
                              reads=[("s12", k)], writes=[("ssum", k)])
                    else:
                        S.add("dve", lambda e, pi=pi, k=k: e.tensor_tensor(out=ssum[k][:], in0=PT[pi][:, 0, :], in1=PT[pi][:, 1, :], op=ALU.add),
                              reads=[("pt", pi)], writes=[("ssum", k)])
                    return (k, 7, b == 1, b == NB - 1)

                def sum_mm(desc):
                    k, psm, first, last = desc
                    S.add("pe", lambda e, k=k, psm=psm, first=first, last=last: e.matmul(
                        bank(psm)[:, 0:CH], lhsT=ones_bf[:], rhs=ssum[k][:], start=first, stop=last),
                        reads=[("ssum", k)], writes=[bk(psm)])

                def norm(i):
                    h, c, b = batches[i]
                    itn = h * NCH + c
                    ob = itn % 2
                    po, psm = 6, 7
                    S.add("dve", lambda e, ob=ob: e.tensor_copy(out=osb[ob][:], in_=bank(6)[:, 0:CH]),
                          reads=[bk(6)], writes=[("osb", ob)])
                    S.add("dve", lambda e, ob=ob: e.tensor_copy(out=rec[ob][:], in_=bank(7)[:, 0:CH]),
                          reads=[bk(7)], writes=[("rec", ob)])
                    S.add("dve", lambda e, ob=ob: e.reciprocal(out=rec[ob][:], in_=rec[ob][:]),
                          reads=[("rec", ob)], writes=[("rec", ob)])
                    S.add("dve", lambda e, ob=ob, h=h, c=c: e.tensor_tensor(
                        out=QA[:, h, c * CH:(c + 1) * CH], in0=osb[ob][:], in1=rec[ob][:], op=ALU.mult),
                        reads=[("osb", ob), ("rec", ob)], writes=[("QA", h, c)])

                NBT = len(batches)
                qk(0)
                ex(0)
                qk(1)
                ex(1)
                pending = None
                for i in range(NBT):
                    if i + 2 < NBT:
                        qk(i + 2)
                        ex(i + 2)
                    pv(i)
                    if pending is not None:
                        sum_mm(pending)
                        pending = None
                    b = batches[i][2]
                    if b % 2 == 1 or b == NB - 1:
                        pending = dve_sum(i)
                    if b == NB - 1:
                        sum_mm(pending)
                        pending = None
                        norm(i)
                if debug:
                    with ExitStack() as sd:
                        dtmp = sb(sd, "dtmp2", [128, 2080], F32)
                        for h in range(8):
                            S.add("dve", lambda e, dtmp=dtmp, h=h: e.tensor_copy(out=dtmp[:, 0:E], in_=QA[:, h, :]),
                                  reads=[("QA", h, c) for c in range(NCH)], writes=["dtmp"])
                            S.add("sp", lambda e, dtmp=dtmp, h=h: e.dma_start(out=dbg["at"][:, h, :], in_=dtmp[:, 0:E]), reads=["dtmp"], dsem=sem_out)
                S.phase_end("B")
            with ExitStack() as sC:
                PADH = 16
                ycT = sb(sC, "ycT", [128, 4, E], BF16)
                wg_t = sb(sC, "wg_t", [128, KC, 2 * D], BF16)
                wao_t = sb(sC, "wao_t", [128, KC, D], BF16)
                wout_t = sb(sC, "wout_t", [128, KC, D], BF16)
                sem_w4 = newsem()
                S.nobar.add(id(sem_w4))
                sC23 = ExitStack()
                hT = sb(sC23, "hT", [128, 4, E + 2 * PADH], BF16)
                S.add("pool", lambda e: e.memset(hT[:, :, 0:PADH], 0.0), writes=["hTpadL"])
                S.add("pool", lambda e: e.memset(hT[:, :, PADH + E:], 0.0), writes=["hTpadR"])
                with ExitStack() as sC2:
                    wu_t = sb(sC2, "wu_t", [128, KC, D], BF16)
                    sem_w = newsem()
                    for kc in range(KC):
                        S.add("pool", lambda e, kc=kc: e.dma_start(out=wu_t[:, kc], in_=wu[:, kc]), writes=["wu"], dsem=sem_w)
                    for kc in range(KC):
                        S.add("pool", lambda e, kc=kc: e.dma_start(out=wg_t[:, kc], in_=wg[:, kc]), writes=["w4"], dsem=sem_w4)
                    S.add("pool", lambda e: e.dma_start(out=wao_t[:], in_=wao), writes=["w4"], dsem=sem_w4)
                    S.add("pool", lambda e: e.dma_start(out=wout_t[:], in_=wout), writes=["w4"], dsem=sem_w4)
                    xs_c2 = [sb(sC2, "cxs%d" % i, [128, KC, CH], F32) for i in range(2)]
                    xs_sem = [newsem() for _ in range(2)]
                    sq = sb(sC2, "csq", [128, KC, CH], BF16)
                    tmp = sb(sC2, "ctmp", [128, 4, CH], F32)
                    rstd = sb(sC2, "crstd", [128, CH], F32)
                    hx_b = [sb(sC2, "chx%d" % i, [128, KC, CH], BF16) for i in range(2)]
                    sg = [sb(sC2, "sg%d" % i, [128, CH], F32) for i in range(2)]
                    for c in range(NCH):
                        bi = c % 2
                        S.add("sp", lambda e, c=c, bi=bi: e.dma_start(out=xs_c2[bi][:], in_=xT_own[:, :, c * CH:(c + 1) * CH]),
                              writes=[("xs", bi)], dsem=xs_sem[bi])
                        front("C", xs_c2[bi], CH, sq, rstd, tmp, hx_b[bi], a1, b1, 0, ("xs", bi), ("hx", bi))
                        hxk = [(("hx", bi), kc) for kc in range(KC)]
                        for fc in range(4):
                            fb = fc % 2
                            pa, pb_ = 2 + 2 * fb, 3 + 2 * fb
                            for half, pp in ((0, pa), (1, pb_)):
                                col = half * 512 + fc * 128
                                for kc in range(KC):
                                    S.add("pe", lambda e, kc=kc, pp=pp, col=col, bi=bi: e.matmul(
                                        bank(pp)[:, 0:CH], lhsT=wu_t[:, kc, col:col + 128], rhs=hx_b[bi][:, kc, :],
                                        start=(kc == 0), stop=(kc == KC - 1)), reads=["wu"] + hxk, writes=[bk(pp)])
                            S.add("act", lambda e, fb=fb, pb_=pb_: e.activation(out=sg[fb][:], in_=bank(pb_)[:, 0:CH], func=AF.Sigmoid),
                                  reads=[bk(pb_)], writes=[("sg", fb)])
                            S.add("dve", lambda e, fb=fb, pa=pa, fc=fc, c=c: e.tensor_tensor(
                                out=hT[:, fc, PADH + c * CH:PADH + (c + 1) * CH], in0=bank(pa)[:, 0:CH], in1=sg[fb][:], op=ALU.mult),
                                reads=[bk(pa), ("sg", fb)], writes=[("hT", fc, c)])
                            if c == 0 or c == NCH - 1:
                                e0 = 0 if c == 0 else E - HALO
                                m0 = 0 if c == 0 else HALO
                                S.add("dve", lambda e, fc=fc, e0=e0, m0=m0: e.tensor_tensor(
                                    out=hT[:, fc, PADH + e0:PADH + e0 + HALO], in0=hT[:, fc, PADH + e0:PADH + e0 + HALO],
                                    in1=mask_t[:, m0:m0 + HALO], op=ALU.mult),
                                    reads=[("hT", fc, c)], writes=[("hT", fc, c)])
                    S.phase_end("C2")
                with ExitStack() as sC3:
                    cdiag = sb(sC3, "cdiag", [128, 4, 31, 128], BF16)
                    cdw_t = sb(sC3, "cdw_t", [128, 4, 31], F32)
                    sem_w = newsem()
                    S.add("sp", lambda e: e.dma_start(out=cdw_t[:], in_=cdw), writes=["cdw"], dsem=sem_w)
                    for fc in range(4):
                        for j in range(31):
                            S.add("dve", lambda e, fc=fc, j=j: e.tensor_scalar(out=cdiag[:, fc, j, :], in0=ident_bf[:],
                                                                              scalar1=cdw_t[:, fc, j:j + 1], scalar2=None, op0=ALU.mult),
                                  reads=["cdw"], writes=[("cdiag", fc)])
                    cvf = sb(sC3, "cvf", [128, 4, CH], F32)
                    cvb = sb(sC3, "cvb", [128, 4, CH], BF16)
                    csq = sb(sC3, "csq3", [128, 4, CH], BF16)
                    mean = sb(sC3, "mean", [128, CH], F32)
                    msq = sb(sC3, "msq", [128, CH], F32)
                    var = sb(sC3, "var", [128, CH], F32)
                    tt_ = [sb(sC3, "tt%d" % i, [128, CH], F32) for i in range(2)]
                    for c in range(NCH):
                        for fc in range(4):
                            pc = 2 + (fc % 2)
                            for j in range(31):
                                off = c * CH + 1 + j
                                S.add("pe", lambda e, fc=fc, j=j, pc=pc, off=off: e.matmul(
                                    bank(pc)[:, 0:CH], lhsT=cdiag[:, fc, j, :], rhs=hT[:, fc, off:off + CH],
                                    start=(j == 0), stop=(j == 30)), reads=[("cdiag", fc)], writes=[bk(pc)])
                            S.add("act", lambda e, fc=fc, pc=pc: e.activation(out=cvf[:, fc, :], in_=bank(pc)[:, 0:CH], func=AF.Identity,
                                                                              bias=cvec_t[:, 0, fc:fc + 1], scale=1.0),
                                  reads=[bk(pc)], writes=[("cvf", fc)])
                            S.add("dve", lambda e, fc=fc: e.tensor_copy(out=cvb[:, fc, :], in_=cvf[:, fc, :]),
                                  reads=[("cvf", fc)], writes=[("cvb", fc)])
                            S.add("act", lambda e, fc=fc: e.activation(out=csq[:, fc, :], in_=cvf[:, fc, :], func=AF.Square),
                                  reads=[("cvf", fc)], writes=[("csq", fc)])
                        for fc in range(4):
                            S.add("pe", lambda e, fc=fc: e.matmul(bank(0)[:, 0:CH], lhsT=ones_bf[:], rhs=cvb[:, fc, :],
                                                                   start=(fc == 0), stop=(fc == 3)), reads=[("cvb", fc)], writes=[bk(0)])
                        for fc in range(4):
                            S.add("pe", lambda e, fc=fc: e.matmul(bank(1)[:, 0:CH], lhsT=ones_bf[:], rhs=csq[:, fc, :],
                                                                   start=(fc == 0), stop=(fc == 3)), reads=[("csq", fc)], writes=[bk(1)])
                        S.add("dve", lambda e: e.tensor_scalar(out=mean[:], in0=bank(0)[:, 0:CH], scalar1=1.0 / 512, scalar2=None, op0=ALU.mult),
                              reads=[bk(0)], writes=["mean"])
                        S.add("dve", lambda e: e.tensor_tensor(out=msq[:], in0=mean[:], in1=mean[:], op=ALU.mult), reads=["mean"], writes=["msq"])
                        S.add("dve", lambda e: e.scalar_tensor_tensor(out=var[:], in0=bank(1)[:, 0:CH], scalar=1.0 / 512, in1=msq[:],
                                                                      op0=ALU.mult, op1=ALU.subtract), reads=[bk(1), "msq"], writes=["var"])
                        rsqrt_act(var[:], var[:], 1.0, ["var"], "var")
                        for fc in range(4):
                            tb = fc % 2
                            S.add("dve", lambda e, fc=fc, tb=tb: e.tensor_tensor(out=tt_[tb][:], in0=cvf[:, fc, :], in1=mean[:], op=ALU.subtract),
                                  reads=[("cvf", fc), "mean"], writes=[("tt", tb)])
                            S.add("dve", lambda e, tb=tb: e.tensor_tensor(out=tt_[tb][:], in0=tt_[tb][:], in1=var[:], op=ALU.mult),
                                  reads=[("tt", tb), "var"], writes=[("tt", tb)])
                            S.add("act", lambda e, fc=fc, tb=tb, c=c: e.activation(out=ycT[:, fc, c * CH:(c + 1) * CH], in_=tt_[tb][:], func=AF.Silu,
                                                                                 bias=cvec_t[:, 2, fc:fc + 1], scale=cvec_t[:, 1, fc:fc + 1]),
                                  reads=[("tt", tb)], writes=[("ycT", fc, c)])
                    if debug:
                        with ExitStack() as sd:
                            dtmp = sb(sd, "dtmp3", [128, 2080], F32)
                            for fc in range(4):
                                S.add("dve", lambda e, dtmp=dtmp, fc=fc: e.tensor_copy(out=dtmp[:], in_=hT[:, fc, PADH:PADH + E]), writes=["dtmp"])
                                S.add("sp", lambda e, dtmp=dtmp, fc=fc: e.dma_start(out=dbg["hT"][:, fc, :], in_=dtmp[:]), reads=["dtmp"], dsem=sem_out)
                            for fc in range(4):
                                S.add("dve", lambda e, dtmp=dtmp, fc=fc: e.tensor_copy(out=dtmp[:], in_=ycT[:, fc, :]),
                                      reads=[("ycT", fc, c) for c in range(NCH)], writes=["dtmp"])
                                S.add("sp", lambda e, dtmp=dtmp, fc=fc: e.dma_start(out=dbg["yc"][:, fc, :], in_=dtmp[:]), reads=["dtmp"], dsem=sem_out)
                    S.phase_end("C3")
                sC23.close()
                with ExitStack() as sC4:
                    wco_t = sb(sC4, "wco_t", [128, 4, D], BF16)
                    sem_wco = newsem()
                    S.add("pool", lambda e: e.dma_start(out=wco_t[:], in_=wco), writes=["w4c"], dsem=sem_wco)
                    xs_b = [sb(sC4, "dxs%d" % i, [128, KC, CH], F32) for i in range(2)]
                    xs_sem = [newsem() for _ in range(2)]
                    sq = sb(sC4, "dsq", [128, KC, CH], BF16)
                    tmp = sb(sC4, "dtmp_", [128, 4, CH], F32)
                    rstd = sb(sC4, "drstd", [128, CH], F32)
                    hx = sb(sC4, "dhx", [128, KC, CH], BF16)
                    mT = sb(sC4, "mT", [128, KC, CH], BF16)
                    sa = [sb(sC4, "sa%d" % i, [128, CH], F32) for i in range(2)]
                    sb_ = [sb(sC4, "sbb%d" % i, [128, CH], F32) for i in range(2)]
                    u1 = [sb(sC4, "u1%d" % i, [128, CH], F32) for i in range(2)]
                    u2 = [sb(sC4, "u2%d" % i, [128, CH], F32) for i in range(2)]
                    x1o_sem = [newsem() for _ in range(2)]
                    def d_load(c):
                        bi = c % 2
                        cs = slice(c * CH, (c + 1) * CH)
                        S.add("sp", lambda e, cs=cs, bi=bi: e.dma_start(out=xs_b[bi][:], in_=xT_own[:, :, cs]),
                              writes=[("xs", bi)], dsem=xs_sem[bi])

                    def d_front(c):
                        bi = c % 2
                        cs = slice(c * CH, (c + 1) * CH)
                        front("D", xs_b[bi], CH, sq, rstd, tmp, hx, a1, b1, 0, ("xs", bi), "hx")

                    def d_gates(c):
                        bi = c % 2
                        cs = slice(c * CH, (c + 1) * CH)
                        hxk = [("hx", kc) for kc in range(KC)]
                        for oc in range(KC):
                            ob = oc % 2
                            pga, pgb, pya, pyb = 4 * ob, 4 * ob + 1, 4 * ob + 2, 4 * ob + 3
                            for half, pp in ((0, pga), (1, pgb)):
                                col = half * D + oc * 128
                                for kc in range(KC):
                                    S.add("pe", lambda e, kc=kc, pp=pp, col=col: e.matmul(
                                        bank(pp)[:, 0:CH], lhsT=wg_t[:, kc, col:col + 128], rhs=hx[:, kc, :],
                                        start=(kc == 0), stop=(kc == KC - 1)), reads=["w4"] + hxk, writes=[bk(pp)])
                            for kc in range(KC):
                                S.add("pe", lambda e, kc=kc, pya=pya, oc=oc, cs=cs: e.matmul(
                                    bank(pya)[:, 0:CH], lhsT=wao_t[:, kc, oc * 128:(oc + 1) * 128], rhs=QA[:, kc, cs],
                                    start=(kc == 0), stop=(kc == KC - 1)), reads=["w4"], writes=[bk(pya)])
                            for fc in range(4):
                                S.add("pe", lambda e, fc=fc, pyb=pyb, oc=oc, cs=cs: e.matmul(
                                    bank(pyb)[:, 0:CH], lhsT=wco_t[:, fc, oc * 128:(oc + 1) * 128], rhs=ycT[:, fc, cs],
                                    start=(fc == 0), stop=(fc == 3)), reads=["w4", "w4c"], writes=[bk(pyb)])
                            S.add("act", lambda e, ob=ob, pga=pga: e.activation(out=sa[ob][:], in_=bank(pga)[:, 0:CH], func=AF.Sigmoid),
                                  reads=[bk(pga)], writes=[("sa", ob)])
                            S.add("act", lambda e, ob=ob, pgb=pgb: e.activation(out=sb_[ob][:], in_=bank(pgb)[:, 0:CH], func=AF.Sigmoid),
                                  reads=[bk(pgb)], writes=[("sb", ob)])
                            S.add("dve", lambda e, ob=ob, pya=pya: e.tensor_tensor(out=u1[ob][:], in0=bank(pya)[:, 0:CH], in1=sa[ob][:], op=ALU.mult),
                                  reads=[bk(pya), ("sa", ob)], writes=[("u1", ob)])
                            S.add("dve", lambda e, ob=ob, pyb=pyb: e.tensor_tensor(out=u2[ob][:], in0=bank(pyb)[:, 0:CH], in1=sb_[ob][:], op=ALU.mult),
                                  reads=[bk(pyb), ("sb", ob)], writes=[("u2", ob)])
                            S.add("dve", lambda e, ob=ob, oc=oc: e.tensor_tensor(out=mT[:, oc, :], in0=u1[ob][:], in1=u2[ob][:], op=ALU.add),
                                  reads=[("u1", ob), ("u2", ob)], writes=[("mT", oc)])

                    def d_out(c):
                        bi = c % 2
                        cs = slice(c * CH, (c + 1) * CH)
                        hxk = [("hx", kc) for kc in range(KC)]
                        mk = [("mT", oc) for oc in range(KC)]
                        for oc in range(KC):
                            pm = 2 + (oc % 2)
                            for kc in range(KC):
                                S.add("pe", lambda e, kc=kc, pm=pm, oc=oc: e.matmul(
                                    bank(pm)[:, 0:CH], lhsT=wout_t[:, kc, oc * 128:(oc + 1) * 128], rhs=mT[:, kc, :],
                                    start=(kc == 0), stop=(kc == KC - 1)), reads=["w4"] + mk, writes=[bk(pm)])
                            S.add("dve", lambda e, pm=pm, oc=oc, bi=bi: e.scalar_tensor_tensor(
                                out=xs_b[bi][:, oc, :], in0=bank(pm)[:, 0:CH], scalar=g1(oc), in1=xs_b[bi][:, oc, :], op0=ALU.mult, op1=ALU.add),
                                reads=[bk(pm), ("xs", bi)], writes=[("x1o", bi, oc)])
                        S.add("pool", lambda e, bi=bi, cs=cs: e.dma_start(out=x1scr[:, :, cs], in_=xs_b[bi][:]),
                              reads=[("x1o", bi, oc) for oc in range(KC)] + [("xs", bi)], writes=[("x1scr", c)], dsem=x1o_sem[bi])

                    d_load(0)
                    d_front(0)
                    for c in range(NCH):
                        if c + 1 < NCH:
                            d_load(c + 1)
                        d_gates(c)
                        if c + 1 < NCH:
                            d_front(c + 1)
                        d_out(c)
                    S.phase_end("C45")
        with ExitStack() as sF:
            wup_t = sb(sF, "wup_t", [128, KC, 2 * FF], BF16)
            wdn_t = sb(sF, "wdn_t", [128, NFC, D], BF16)
            fdw_t = sb(sF, "fdw_t", [128, NFC, 3], F32)
            sem_w = newsem()
            for kc in range(KC):
                S.add("pool", lambda e, kc=kc: e.dma_start(out=wup_t[:, kc], in_=wup[:, kc]), writes=["w6"], dsem=sem_w)
            for q in range(2):
                S.add("pool", lambda e, q=q: e.dma_start(out=wdn_t[:, q * 11:(q + 1) * 11], in_=wdn[:, q * 11:(q + 1) * 11]), writes=["w6"], dsem=sem_w)
            sem_w2 = newsem()
            S.add("sp", lambda e: e.dma_start(out=fdw_t[:], in_=fdw), writes=["w6b"], dsem=sem_w2)
            W6 = ["w6", "w6b"]
            CW = CH + 2
            x1c = [sb(sF, "x1c%d" % i, [128, KC, CW], F32) for i in range(2)]
            x1_sem = [newsem() for _ in range(2)]
            sq = sb(sF, "fsq", [128, KC, CW], BF16)
            tmp = sb(sF, "ftmp", [128, 4, CW], F32)
            rstd = sb(sF, "frstd", [128, CW], F32)
            hx2 = sb(sF, "hx2", [128, KC, CW], BF16)
            aT = [sb(sF, "aT%d" % i, [128, CW], BF16) for i in range(2)]
            fdg = [sb(sF, "fdg%d" % i, [128, 3, 128], BF16) for i in range(2)]
            gl = [sb(sF, "gl%d" % i, [128, CH], F32) for i in range(2)]
            hid = sb(sF, "hid", [128, NFC, CH], BF16)
            fin = sb(sF, "fin", [128, CH], F32)
            for i in range(2):
                S.add("pool", lambda e, i=i: e.memset(aT[i][:], 0.0), writes=[("aT", i)])

            class _V:
                def __init__(self, t, o):
                    self.t, self.o = t, o

                def __getitem__(self, key):
                    a, b, sl = key
                    return self.t[a, b, self.o + sl.start:self.o + sl.stop]

            class _V2:
                def __init__(self, t, o):
                    self.t, self.o = t, o

                def __getitem__(self, key):
                    a, sl = key
                    return self.t[a, self.o + sl.start:self.o + sl.stop]

            def win(c):
                lo = max(c * CH - 1, 0)
                hi = min((c + 1) * CH + 1, E)
                return lo, hi, hi - lo, lo - (c * CH - 1)

            def f_load(c):
                lo, hi, n, o0 = win(c)
                xb = x1c[c % 2]
                S.add("sp", lambda e, lo=lo, hi=hi, o0=o0, n=n, xb=xb: e.dma_start(out=xb[:, :, o0:o0 + n], in_=x1scr[:, :, lo:hi]),
                      writes=[("x1c", c % 2)], dsem=x1_sem[c % 2])

            def f_front(c):
                lo, hi, n, o0 = win(c)
                front("F", _V(x1c[c % 2], o0), n, _V(sq, o0), _V2(rstd, o0), _V(tmp, o0), _V(hx2, o0), a2, b2, 0, ("x1c", c % 2), "hx2")

            def f_up(c):
                lo, hi, n, o0 = win(c)
                hxk = [("hx2", kc) for kc in range(KC)]
                for fc in range(NFC):
                    fb = fc % 2
                    pa, pb_, pcv = 2 + 3 * fb, 3 + 3 * fb, 4 + 3 * fb
                    for j in range(3):
                        S.add("act", lambda e, fc=fc, j=j, fb=fb: e.activation(out=fdg[fb][:, j, :], in_=ident_bf[:], func=AF.Identity,
                                                                             scale=fdw_t[:, fc, j:j + 1]),
                              reads=W6, writes=[("fdg", fb)])
                    for kc in range(KC):
                        S.add("pe", lambda e, kc=kc, pa=pa, fc=fc, o0=o0, n=n: e.matmul(
                            bank(pa)[:, 0:n], lhsT=wup_t[:, kc, fc * 128:(fc + 1) * 128], rhs=hx2[:, kc, o0:o0 + n],
                            start=(kc == 0), stop=(kc == KC - 1)), reads=W6 + hxk, writes=[bk(pa)])
                    S.add("act", lambda e, fb=fb, pa=pa, o0=o0, n=n: e.activation(
                        out=aT[fb][:, o0:o0 + n], in_=bank(pa)[:, 0:n], func=AF.Identity),
                        reads=[bk(pa)], writes=[("aT", fb)])
                    if c == 0 or c == NCH - 1:
                        w0 = (o0 + 0) if c == 0 else (E - HALO - lo + o0)
                        m0 = 0 if c == 0 else HALO
                        S.add("dve", lambda e, fb=fb, w0=w0, m0=m0: e.tensor_tensor(
                            out=aT[fb][:, w0:w0 + HALO], in0=aT[fb][:, w0:w0 + HALO], in1=mask_t[:, m0:m0 + HALO], op=ALU.mult),
                            reads=[("aT", fb)], writes=[("aT", fb)])
                    for kc in range(KC):
                        S.add("pe", lambda e, kc=kc, pb_=pb_, fc=fc: e.matmul(
                            bank(pb_)[:, 0:CH], lhsT=wup_t[:, kc, FF + fc * 128:FF + (fc + 1) * 128], rhs=hx2[:, kc, 1:1 + CH],
                            start=(kc == 0), stop=(kc == KC - 1)), reads=W6 + hxk, writes=[bk(pb_)])
                    for j in range(3):
                        S.add("pe", lambda e, j=j, pcv=pcv, fb=fb: e.matmul(
                            bank(pcv)[:, 0:CH], lhsT=fdg[fb][:, j, :], rhs=aT[fb][:, j:j + CH],
                            start=(j == 0), stop=(j == 2)), reads=[("fdg", fb), ("aT", fb)], writes=[bk(pcv)])
                    S.add("act", lambda e, fb=fb, pcv=pcv, fc=fc: e.activation(out=gl[fb][:], in_=bank(pcv)[:, 0:CH], func=AF.Gelu_apprx_tanh,
                                                                              bias=fdb_t[:, fc:fc + 1], scale=1.0),
                          reads=[bk(pcv)], writes=[("gl", fb)])
                    S.add("dve", lambda e, fb=fb, pb_=pb_, fc=fc: e.tensor_tensor(out=hid[:, fc, :], in0=bank(pb_)[:, 0:CH], in1=gl[fb][:], op=ALU.mult),
                          reads=[bk(pb_), ("gl", fb)], writes=[("hid", fc)])

            def f_down(c):
                xb = x1c[c % 2]
                xk = ("x1c", c % 2)
                hk = [("hid", fc) for fc in range(NFC)]
                for oc in range(KC):
                    pd = oc % 2
                    for fc in range(NFC):
                        S.add("pe", lambda e, fc=fc, pd=pd, oc=oc: e.matmul(
                            bank(pd)[:, 0:CH], lhsT=wdn_t[:, fc, oc * 128:(oc + 1) * 128], rhs=hid[:, fc, :],
                            start=(fc == 0), stop=(fc == NFC - 1)), reads=W6 + hk, writes=[bk(pd)])
                    S.add("dve", lambda e, pd=pd, oc=oc, xb=xb: e.scalar_tensor_tensor(
                        out=xb[:, oc, 1:1 + CH], in0=bank(pd)[:, 0:CH], scalar=g2(oc), in1=xb[:, oc, 1:1 + CH], op0=ALU.mult, op1=ALU.add),
                        reads=[bk(pd), xk], writes=[("x2", c % 2, oc)])
                x2k = [("x2", c % 2, oc) for oc in range(KC)]
                S.add("act", lambda e, xb=xb: e.activation(out=sq[:, :, 1:1 + CH], in_=xb[:, :, 1:1 + CH], func=AF.Square),
                      reads=x2k, writes=["Fsq"])
                for kc in range(KC):
                    S.add("pe", lambda e, kc=kc: e.matmul(bank(2)[:, 0:CH], lhsT=ones_bf[:], rhs=sq[:, kc, 1:1 + CH],
                                                         start=(kc == 0), stop=(kc == KC - 1)), reads=["Fsq"], writes=[bk(2)])
                rsqrt_act(fin[:], bank(2)[:, 0:CH], 1.0 / D, [bk(2)], "fin")
                for oc in range(KC):
                    S.add("dve", lambda e, oc=oc, xb=xb: e.scalar_tensor_tensor(
                        out=xb[:, oc, 1:1 + CH], in0=xb[:, oc, 1:1 + CH], scalar=vec8_t[:, 2, oc:oc + 1], in1=fin[:], op0=ALU.mult, op1=ALU.mult),
                        reads=[("x2", c % 2, oc), "fin"], writes=[("x2", c % 2, oc)])
                elo = max(c * CH, HALO)
                ehi = min((c + 1) * CH, HALO + OWN)
                S.add("sp", lambda e, c=c, elo=elo, ehi=ehi, xb=xb: e.dma_start(
                    out=yT[:, :, elo - HALO:ehi - HALO], in_=xb[:, :, 1 + elo - c * CH:1 + ehi - c * CH]),
                    reads=x2k + [xk], dsem=sem_out)

            f_load(0)
            f_front(0)
            for c in range(NCH):
                if c + 1 < NCH:
                    f_load(c + 1)
                f_up(c)
                if c + 1 < NCH:
                    f_front(c + 1)
                f_down(c)

        stats = S.finalize(nc, engsem, final_sems=[sem_out])
    return nc, stats


def _fm(w):
    K, N = w.shape
    return np.ascontiguousarray(w.reshape(K // 128, 128, N).transpose(1, 0, 2))


def _pv(v):
    return np.ascontiguousarray(v.reshape(-1, 128).T)


def _rope_tables():
    half = 64
    inv_freq = (10000.0 ** (-np.arange(0, half, 2, dtype=np.float32) / half)).astype(np.float32)
    t = np.arange(L)
    r = (t // 64).astype(np.float32)
    col = (t % 64).astype(np.float32)
    ang = np.concatenate([r[:, None] * inv_freq, col[:, None] * inv_freq], axis=-1).astype(np.float32)
    cos = np.cos(ang).astype(np.float32)
    sin = np.sin(ang).astype(np.float32)
    d = np.arange(128)
    C = cos[:, d // 2].T
    sgn = np.where(d % 2 == 0, -1.0, 1.0).astype(np.float32)
    Sg = (sin[:, d // 2] * sgn[None, :]).T
    return np.ascontiguousarray(C), np.ascontiguousarray(Sg)


_CACHE = {}


def _prep(x, c, ctx, c_ctx, w_mod, b_mod, norm1_g, w_in, q_norm_g, k_norm_g, w_attn_out,
          conv_dw_w, conv_dw_b, conv_ln_g, conv_ln_b, w_conv_out, w_out, norm2_g,
          w_up, ffn_dw_w, ffn_dw_b, w_down, final_g):
    f = lambda a: np.asarray(a, dtype=np.float32)
    x, c, ctx, c_ctx = f(x), f(c), f(ctx), f(c_ctx)
    xT = np.ascontiguousarray(x[0].T)
    xT_all = np.ascontiguousarray(xT.reshape(KC, 128, L // 512, 512).transpose(2, 1, 0, 3).reshape(L // 512, 128, KC * 512))
    ctxT = np.ascontiguousarray(f(ctx)[0].T.reshape(KC, 128, 256).transpose(1, 0, 2))
    C, Sg = _rope_tables()
    ropeK = np.zeros((128, 2, L + 512), np.float32)
    ropeK[:, 0, :L] = C
    ropeK[:, 1, :L] = Sg
    ropeK[:, 0, L:] = 1.0
    ropeK = np.ascontiguousarray(ropeK.reshape(128, 2, L // 512 + 1, 512).transpose(2, 0, 1, 3).reshape(L // 512 + 1, 128, 1024))
    w_in0 = f(w_in)[0]
    sw = np.arange(D).reshape(-1, 2)[:, ::-1].reshape(-1)
    wq = w_in0[:, 0:D]
    wk = w_in0[:, D:D + 256]
    swk = np.arange(256).reshape(-1, 2)[:, ::-1].reshape(-1)
    wq2 = np.ascontiguousarray(np.stack([_fm(wq), _fm(wq[:, sw])], axis=2))
    wk2 = np.ascontiguousarray(np.stack([_fm(wk), _fm(wk[:, swk])], axis=2))
    sw128 = np.arange(128).reshape(-1, 2)[:, ::-1].reshape(-1)
    gq = f(q_norm_g)[0]
    gk = f(k_norm_g)[0]
    gqk = np.ascontiguousarray(np.stack([gq, gq[sw128], gk, gk[sw128]], axis=1))
    common = {
        "xT_all": xT_all, "ctxT": ctxT, "ropeK": ropeK,
        "ccin": np.ascontiguousarray(np.stack([_pv(c[0]), _pv(c_ctx)], axis=2)),
        "wmod": _fm(f(w_mod)[0]), "bmod": _pv(f(b_mod)[0]),
        "vec8": np.ascontiguousarray(np.stack([_pv(f(norm1_g)[0]), _pv(f(norm2_g)[0]), _pv(f(final_g))], axis=1)),
        "gqk": gqk, "gqk_row": np.ascontiguousarray(np.broadcast_to(np.concatenate([gq, gk])[None, :], (128, 256))),
        "wq2": wq2, "wk2": wk2, "wv": _fm(w_in0[:, D + 256:D + 512]),
        "wu": _fm(w_in0[:, D + 512:2 * D + 512]), "wg": _fm(w_in0[:, 2 * D + 512:]),
        "wao": _fm(f(w_attn_out)[0]), "wco": _fm(f(w_conv_out)[0]), "wout": _fm(f(w_out)[0]),
        "wup": _fm(f(w_up)[0]), "wdn": _fm(f(w_down)[0]),
        "cdw": np.ascontiguousarray(f(conv_dw_w)[0].T.reshape(4, 128, 31).transpose(1, 0, 2)),
        "cvec": np.ascontiguousarray(np.stack([_pv(f(conv_dw_b)[0]), _pv(f(conv_ln_g)[0]), _pv(f(conv_ln_b)[0])], axis=1)),
        "fdw": np.ascontiguousarray(f(ffn_dw_w)[0].T.reshape(NFC, 128, 3).transpose(1, 0, 2)),
        "fdb": _pv(f(ffn_dw_b)[0]),
        "ident": np.eye(128, dtype=np.float32),
    }
    in_maps = []
    for i in range(NCORES):
        t0 = i * OWN - HALO
        idx = np.arange(t0, t0 + E)
        valid = (idx >= 0) & (idx < L)
        idc = np.clip(idx, 0, L - 1)
        xo = xT[:, idc] * 0 if False else xT[:, idc].copy()
        xo[:, ~valid] = 0.0
        m = dict(common)
        m["xT_own"] = np.ascontiguousarray(xo.reshape(KC, 128, E).transpose(1, 0, 2))
        rq = np.zeros((128, 2, E), np.float32)
        rq[:, 0, :] = C[:, idc]
        rq[:, 1, :] = Sg[:, idc]
        m["ropeQ"] = rq
        vh = np.concatenate([valid[:HALO], valid[E - HALO:]]).astype(np.float32)
        m["maskE"] = np.ascontiguousarray(np.broadcast_to(vh[None, :], (128, 2 * HALO)))
        in_maps.append(m)
    return in_maps


def kernel(**inputs):
    in_maps = _prep(**inputs)
    if "main" not in _CACHE:
        _CACHE["main"] = build_program()[0]
    nc = _CACHE["main"]
    res = run_bass_kernel_spmd(nc, in_maps, core_ids=list(range(NCORES)))
    out = np.empty((1, L, D), np.float32)
    for i in range(NCORES):
        y = res.results[i]["yT"]
        out[0, i * OWN:(i + 1) * OWN, :] = y.transpose(2, 1, 0).reshape(OWN, D)
    return out
```
